# Optimizing a Trainium2 kernel written in Bass

```python
import math
import jax, jax.numpy as jnp
from jax import lax
import numpy as np

D_MODEL = 1024
BATCH = 4
SEQ = 8192
DEPTH = 1

GRID_W = 64
CTX_LEN = 256

D_INNER = 2 * D_MODEL
HEAD_DIM = 64
N_HEADS = D_INNER // HEAD_DIM
N_GROUPS = 8
HPG = N_HEADS // N_GROUPS
D_STATE = 128
SSM_CONV = 4
SSM_PAD = (2, 1)
CHUNK = 128

D_CONF = D_MODEL
CONF_KERNEL = 31
CONF_PAD = (CONF_KERNEL // 2, CONF_KERNEL // 2)

EPS = 1e-6

GN = N_GROUPS * D_STATE
X_END = D_INNER
B_END = X_END + GN
C_END = B_END + GN
DT_END = C_END + 2 * N_HEADS
Z_END = DT_END + D_INNER
GLU_END = Z_END + 2 * D_CONF
CG_END = GLU_END + D_CONF
IN_COLS = CG_END + 2 * D_MODEL

kernel_name = 'hybrid_ssd_conformer_prefix_block'


def rms_norm(x, w):
    xf = x.astype(jnp.float32)
    y = xf * lax.rsqrt(jnp.mean(xf * xf, axis=-1, keepdims=True) + EPS)
    return y.astype(x.dtype) * w


def group_rms_norm(x, w):
    shp = x.shape
    xg = x.reshape(*shp[:-1], N_GROUPS, shp[-1] // N_GROUPS).astype(jnp.float32)
    y = xg * lax.rsqrt(jnp.mean(xg * xg, axis=-1, keepdims=True) + EPS)
    return y.reshape(shp).astype(x.dtype) * w


def layer_norm(x, w, b):
    xf = x.astype(jnp.float32)
    mu = jnp.mean(xf, axis=-1, keepdims=True)
    var = jnp.mean(jnp.square(xf - mu), axis=-1, keepdims=True)
    return ((xf - mu) * lax.rsqrt(var + EPS)).astype(x.dtype) * w + b


def modulate(h, shift, scale):
    return h * (1 + scale) + shift


def depthwise_conv(u, w, b, pad):
    out = lax.conv_general_dilated(u, w[:, None, :], (1,), [pad],
                                   dimension_numbers=('NWC', 'WIO', 'NWC'),
                                   feature_group_count=u.shape[-1])
    return out + b


def rev(t):
    return jnp.flip(t, axis=1)


def to_chunks(t):
    return t.reshape(t.shape[0], t.shape[1] // CHUNK, CHUNK, *t.shape[2:])


def ssm_dt(dt_raw, dt_bias):
    b, L = dt_raw.shape[:2]
    return jax.nn.softplus(dt_raw.astype(jnp.float32).reshape(b, L, 2, N_GROUPS, HPG)
                           + dt_bias.astype(jnp.float32).reshape(2, N_GROUPS, HPG))


def ssm_decay(a_log):
    return -jnp.exp(a_log.astype(jnp.float32)).reshape(2, N_GROUPS, HPG)


def chunk_states(xs, dt, a, bm, h0):
    la = jnp.cumsum(dt * a, axis=2)
    w_end = jnp.exp(la[:, :, -1:] - la) * dt
    contrib = jnp.einsum('bcsgn,bcsgrp->bcgrpn', bm, xs * w_end[..., None])
    chunk_decay = jnp.exp(la[:, :, -1])

    def step(h, inp):
        s, d = inp
        return h * d[..., None, None] + s, h

    h_last, h_prev = lax.scan(step, h0, (jnp.moveaxis(contrib, 1, 0), jnp.moveaxis(chunk_decay, 1, 0)))
    return la, jnp.moveaxis(h_prev, 0, 1), h_last


def ssd_scan(xs, dt, a, bm, cm, h0):
    xs_c, dt_c, bm_c, cm_c = [to_chunks(t.astype(jnp.float32)) for t in (xs, dt, bm, cm)]
    la, h_prev, h_last = chunk_states(xs_c, dt_c, a, bm_c, h0)
    idx = jnp.arange(CHUNK)
    order = (idx[:, None] >= idx[None, :])[None, None, :, :, None, None]
    seg = la[:, :, :, None] - la[:, :, None, :]
    decay = jnp.exp(jnp.where(order, seg, -jnp.inf))
    scores = jnp.einsum('bclgn,bcsgn->bclsg', cm_c, bm_c)
    mix = scores[..., None] * decay * dt_c[:, :, None]
    y_diag = jnp.einsum('bclsgr,bcsgrp->bclgrp', mix, xs_c)
    y_off = jnp.einsum('bclgn,bcgrpn->bclgrp', cm_c, h_prev) * jnp.exp(la)[..., None]
    return (y_diag + y_off).reshape(xs.shape), h_last


def ssd_final_state(xs, dt, a, bm, h0):
    xs_c, dt_c, bm_c = [to_chunks(t.astype(jnp.float32)) for t in (xs, dt, bm)]
    _, _, h_last = chunk_states(xs_c, dt_c, a, bm_c, h0)
    return h_last


def context_states(h_ctx, w_in, ssm_conv_w, ssm_conv_b, dt_bias, a_log, h0):
    b, L, _ = h_ctx.shape
    xb = jax.nn.silu(depthwise_conv(h_ctx @ w_in[:, :B_END], ssm_conv_w[:, :B_END], ssm_conv_b[:B_END], SSM_PAD))
    xs = xb[..., :X_END].reshape(b, L, N_GROUPS, HPG, HEAD_DIM)
    bm = xb[..., X_END:B_END].reshape(b, L, N_GROUPS, D_STATE)
    dt = ssm_dt(h_ctx @ w_in[:, C_END:DT_END], dt_bias)
    a = ssm_decay(a_log)
    h_f = ssd_final_state(xs, dt[:, :, 0], a[0], bm, h0)
    h_b = ssd_final_state(rev(xs), rev(dt[:, :, 1]), a[1], rev(bm), h0)
    return h_f, h_b


def mixer(h, p, h0_f, h0_b, rows):
    b, L, _ = h.shape
    proj = h @ p['w_in']
    xbc = jax.nn.silu(depthwise_conv(proj[..., :C_END], p['ssm_conv_w'], p['ssm_conv_b'], SSM_PAD))
    xs = xbc[..., :X_END].reshape(b, L, N_GROUPS, HPG, HEAD_DIM)
    bm = xbc[..., X_END:B_END].reshape(b, L, N_GROUPS, D_STATE)
    cm = xbc[..., B_END:C_END].reshape(b, L, N_GROUPS, D_STATE)
    dt = ssm_dt(proj[..., C_END:DT_END], p['dt_bias'])
    a = ssm_decay(p['a_log'])
    y_f, h_f = ssd_scan(xs, dt[:, :, 0], a[0], bm, cm, h0_f)
    y_b, h_b = ssd_scan(rev(xs), rev(dt[:, :, 1]), a[1], rev(bm), rev(cm), h0_b)
    y = (y_f + rev(y_b)).astype(h.dtype) + p['d_skip'].reshape(N_GROUPS, HPG, 1) * xs
    y = y.reshape(b, L, D_INNER) * jax.nn.silu(proj[..., DT_END:Z_END])
    branch_ssm = group_rms_norm(y, p['ssm_norm_w']) @ p['w_out_ssm']
    glu = proj[..., Z_END:GLU_END]
    u = glu[..., :D_CONF] * jax.nn.sigmoid(glu[..., D_CONF:])
    if rows is not None:
        u = u.reshape(b * rows, GRID_W, D_CONF)
    u = depthwise_conv(u, p['conf_conv_w'], p['conf_conv_b'], CONF_PAD).reshape(b, L, D_CONF)
    u = jax.nn.silu(layer_norm(u, p['conf_ln_w'], p['conf_ln_b'])) * jax.nn.silu(proj[..., GLU_END:CG_END])
    branch_conf = u @ p['w_out_conf']
    g = jax.nn.sigmoid(proj[..., CG_END:])
    merged = g[..., :D_MODEL] * branch_ssm + g[..., D_MODEL:] * branch_conf
    return merged @ p['w_out'], h_f, h_b


def setup_inputs(seed: int = 0) -> dict:
    key = jax.random.key(seed)
    ks = jax.random.split(key, 24)
    f32 = jnp.float32

    def nrm(k, shape, s):
        return jax.random.normal(k, shape, f32) * s

    dt0 = jnp.exp(jax.random.uniform(ks[10], (DEPTH, 2, N_HEADS), f32, math.log(1e-3), math.log(1e-1)))
    return {
        'x': nrm(ks[0], (BATCH, SEQ, D_MODEL), 1.0),
        'c': nrm(ks[1], (BATCH, D_MODEL), 1.0),
        'ctx': nrm(ks[2], (BATCH, CTX_LEN, D_MODEL), 1.0),
        'c_ctx': nrm(ks[3], (D_MODEL,), 1.0),
        'w_mod': nrm(ks[4], (DEPTH, D_MODEL, 3 * D_MODEL), D_MODEL ** -0.5),
        'b_mod': nrm(ks[5], (DEPTH, 3 * D_MODEL), 0.01),
        'norm_w': 1.0 + nrm(ks[6], (DEPTH, D_MODEL), 0.01),
        'w_in': nrm(ks[7], (DEPTH, D_MODEL, IN_COLS), D_MODEL ** -0.5),
        'ssm_conv_w': nrm(ks[8], (DEPTH, SSM_CONV, C_END), SSM_CONV ** -0.5),
        'ssm_conv_b': nrm(ks[9], (DEPTH, C_END), 0.01),
        'dt_bias': dt0 + jnp.log(-jnp.expm1(-dt0)),
        'a_log': jnp.log(jax.random.uniform(ks[11], (DEPTH, 2, N_HEADS), f32, 1.0, 16.0)),
        'd_skip': 1.0 + nrm(ks[12], (DEPTH, N_HEADS), 0.01),
        'ssm_norm_w': 1.0 + nrm(ks[13], (DEPTH, D_INNER), 0.01),
        'w_out_ssm': nrm(ks[14], (DEPTH, D_INNER, D_MODEL), D_INNER ** -0.5),
        'conf_conv_w': nrm(ks[15], (DEPTH, CONF_KERNEL, D_CONF), CONF_KERNEL ** -0.5),
        'conf_conv_b': nrm(ks[16], (DEPTH, D_CONF), 0.01),
        'conf_ln_w': 1.0 + nrm(ks[17], (DEPTH, D_CONF), 0.01),
        'conf_ln_b': nrm(ks[18], (DEPTH, D_CONF), 0.01),
        'w_out_conf': nrm(ks[19], (DEPTH, D_CONF, D_MODEL), D_CONF ** -0.5),
        'w_out': nrm(ks[20], (DEPTH, D_MODEL, D_MODEL), D_MODEL ** -0.5),
        'final_norm_w': 1.0 + nrm(ks[21], (D_MODEL,), 0.01),
    }


def reference(x, c, ctx, c_ctx, w_mod, b_mod, norm_w, w_in, ssm_conv_w, ssm_conv_b, dt_bias, a_log,
              d_skip, ssm_norm_w, w_out_ssm, conf_conv_w, conf_conv_b, conf_ln_w, conf_ln_b,
              w_out_conf, w_out, final_norm_w):
    rows = x.shape[1] // GRID_W
    h0 = jnp.zeros((ctx.shape[0], N_GROUPS, HPG, HEAD_DIM, D_STATE), jnp.float32)
    for i in range(DEPTH):
        p = {'w_in': w_in[i], 'ssm_conv_w': ssm_conv_w[i], 'ssm_conv_b': ssm_conv_b[i],
             'dt_bias': dt_bias[i], 'a_log': a_log[i], 'd_skip': d_skip[i], 'ssm_norm_w': ssm_norm_w[i],
             'w_out_ssm': w_out_ssm[i], 'conf_conv_w': conf_conv_w[i], 'conf_conv_b': conf_conv_b[i],
             'conf_ln_w': conf_ln_w[i], 'conf_ln_b': conf_ln_b[i], 'w_out_conf': w_out_conf[i],
             'w_out': w_out[i]}
        mod_x = jax.nn.silu(c) @ w_mod[i] + b_mod[i]
        mod_c = jax.nn.silu(c_ctx) @ w_mod[i] + b_mod[i]
        shift_x, scale_x, gate_x = jnp.split(mod_x[:, None, :], 3, axis=-1)
        shift_c, scale_c, gate_c = jnp.split(mod_c, 3)
        h_ctx = modulate(rms_norm(ctx, norm_w[i]), shift_c, scale_c)
        if i < DEPTH - 1:
            ctx_out, h_f, h_b = mixer(h_ctx, p, h0, h0, None)
        else:
            h_f, h_b = context_states(h_ctx, p['w_in'], p['ssm_conv_w'], p['ssm_conv_b'],
                                      p['dt_bias'], p['a_log'], h0)
        h = modulate(rms_norm(x, norm_w[i]), shift_x, scale_x)
        out, _, _ = mixer(h, p, h_f, h_b, rows)
        x = x + gate_x * out
        if i < DEPTH - 1:
            ctx = ctx + gate_c * ctx_out
    return rms_norm(x, final_norm_w)
```

```python
import numpy as np
from contextlib import ExitStack
import concourse.bass as bass
import concourse.mybir as mybir
from concourse.bass_utils import run_bass_kernel_spmd

F32 = mybir.dt.float32
BF16 = mybir.dt.bfloat16
AF = mybir.ActivationFunctionType
ALU = mybir.AluOpType

ENGINES = ("pe", "act", "dve", "pool", "sp")
N_DMA_SEMS = 8
SEM_EPOCH = 8192

D = 1024
SEQ = 8192
OWN = 4096
CTX = 256
DIN = 2048
NH = 32
HD = 64
NG = 8
DS = 128
C_END = 4096
DT_END = C_END + 64
Z_END = DT_END + DIN
GLU_END = Z_END + 2 * D
CG_END = GLU_END + D
IN_COLS = CG_END + 2 * D
EPS = 1e-6


class Buf:
    __slots__ = ("last_w", "readers", "excl")

    def __init__(self, excl=False):
        self.last_w = None
        self.readers = {}
        self.excl = excl


class Op:
    __slots__ = ("eng", "fn", "deps", "signal", "semkey", "value", "is_dma", "implied", "waits")

    def __init__(self, eng, fn, is_dma=False):
        self.eng = eng
        self.fn = fn
        self.deps = []
        self.signal = is_dma
        self.semkey = None
        self.value = None
        self.is_dma = is_dma
        self.implied = None
        self.waits = None


class T:
    __slots__ = ("t", "b")

    def __init__(self, t):
        self.t = t
        self.b = Buf()


class Prog:
    def __init__(self):
        self.nc = bass.Bass("TRN2", target_bir_lowering=False)
        self.stack = ExitStack()
        self.order = []
        self.ndma = 0
        self.dmas = []

    def dram(self, name, shape, dtype, kind="Internal"):
        return self.nc.dram_tensor(name, list(shape), dtype, kind=kind).ap()

    def sb(self, name, shape, dtype=F32):
        return T(self.stack.enter_context(self.nc.sbuf_tensor(name, list(shape), dtype)))

    def ps(self, name, shape, dtype=F32):
        t = T(self.stack.enter_context(self.nc.psum_tensor(name, list(shape), dtype)))
        t.b.excl = True
        return t

    def _add(self, eng, fn, reads, writes, is_dma=False, pe_acc=False):
        op = Op(eng, fn, is_dma)
        deps = []
        if any(b.excl for b in reads):
            writes = list(writes) + [b for b in reads if b.excl and b not in writes]
            reads = [b for b in reads if not b.excl]
        for b in reads:
            if b.last_w is not None:
                deps.append(b.last_w)
        for b in writes:
            if b.last_w is not None:
                if not (pe_acc and b.last_w.eng == "pe" and not b.last_w.is_dma and not b.readers):
                    deps.append(b.last_w)
            deps.extend(b.readers.values())
        seen = set()
        for d in deps:
            if id(d) not in seen and d is not op:
                seen.add(id(d))
                d.signal = True
                op.deps.append(d)
        for b in reads:
            if is_dma:
                self.ndma += 1
                b.readers[("dma", self.ndma)] = op
            else:
                b.readers[eng] = op
        for b in writes:
            b.last_w = op
            b.readers = {}
        self.order.append(op)
        return op

    def pe(self, fn, reads, writes, acc=False):
        return self._add("pe", fn, reads, writes, pe_acc=acc)

    def act(self, fn, reads, writes):
        return self._add("act", fn, reads, writes)

    def dve(self, fn, reads, writes):
        return self._add("dve", fn, reads, writes)

    def pool(self, fn, reads, writes):
        return self._add("pool", fn, reads, writes)

    def dma(self, q, fn, reads, writes):
        op = self._add(q, fn, reads, writes, is_dma=True)
        self.dmas.append(op)
        return op

    def barrier(self, tiny, pe_bufs=()):
        xs = []
        for e in ("pe", "act", "dve", "pool"):
            op = self._add(e, tiny[e], [], list(pe_bufs) if e == "pe" else [])
            op.signal = True
            xs.append(op)
        dm = list(self.dmas)
        self.dmas = []
        for e in ("pe", "act", "dve", "pool", "sp"):
            op = self._add(e, tiny[e] if e != "sp" else None, [], list(pe_bufs) if e == "pe" else [])
            for d in xs + dm:
                d.signal = True
                op.deps.append(d)

    def finalize(self, final_bufs):
        nc = self.nc
        st = self.stack
        self._add("sp", None, final_bufs, [])
        cnt = {e: 0 for e in ENGINES}
        nd = {e: 0 for e in ENGINES}
        dcount = {}
        prev_on = {}
        for op in self.order:
            e = op.eng
            if op.is_dma:
                s = nd[e] % N_DMA_SEMS
                nd[e] += 1
                key = ("d", e, s)
                dcount[key] = dcount.get(key, 0) + 16
                op.semkey = key
                op.value = dcount[key]
                if key in prev_on:
                    op.deps.append(prev_on[key])
                prev_on[key] = op
            elif op.signal:
                op.semkey = ("c", e, cnt[e] // SEM_EPOCH)
                op.value = cnt[e] % SEM_EPOCH + 1
                cnt[e] += 1
        known = {e: {} for e in ENGINES}
        nw = 0
        for op in self.order:
            kn = known[op.eng]
            need = {}
            for d in op.deps:
                if kn.get(d.semkey, 0) >= d.value:
                    continue
                if d.semkey not in need or need[d.semkey].value < d.value:
                    need[d.semkey] = d
            waits = []
            for k, d in need.items():
                if kn.get(k, 0) >= d.value:
                    continue
                waits.append((k, d.value))
                kn[k] = d.value
                if d.implied:
                    for kk, vv in d.implied.items():
                        if kn.get(kk, 0) < vv:
                            kn[kk] = vv
            op.waits = waits
            nw += len(waits)
            if op.signal:
                op.implied = dict(kn)
        self.n_waits = nw
        self.counts = cnt
        sems = {}
        for e in ENGINES:
            for ep in range(cnt[e] // SEM_EPOCH + 1):
                sems[("c", e, ep)] = st.enter_context(nc.semaphore("c_%s%d" % (e, ep)))
        for e in ("sp", "act", "pool"):
            for i in range(N_DMA_SEMS):
                sems[("d", e, i)] = st.enter_context(nc.semaphore("d_%s%d" % (e, i)))
        per = {e: [op for op in self.order if op.eng == e] for e in ENGINES}
        block = st.enter_context(nc.Block())

        def run(e, eng):
            for op in per[e]:
                for (k, v) in op.waits:
                    eng.wait_ge(sems[k], v)
                if op.fn is None:
                    continue
                ins = op.fn(eng)
                if op.signal:
                    ins.then_inc(sems[op.semkey], 16 if op.is_dma else 1)

        @block.tensor
        def _(eng):
            run("pe", eng)

        @block.vector
        def _(eng):
            run("dve", eng)

        @block.scalar
        def _(eng):
            run("act", eng)

        @block.gpsimd
        def _(eng):
            run("pool", eng)

        @block.sync
        def _(eng):
            run("sp", eng)

        return nc


class V:
    __slots__ = ("t", "b")

    def __init__(self, ap, b=None):
        self.t = ap
        self.b = b if b is not None else Buf()


class Arena:
    def __init__(self, tile_f32, n):
        self.tile = tile_f32
        self.n = n
        self.off = 0

    def f32(self, p, cols):
        o = self.off
        self.off += cols
        assert self.off <= self.n, ("arena overflow", self.off, self.n)
        return V(self.tile[0:p, o:o + cols])

    def bf16(self, p, cols):
        w = (cols + 1) // 2
        o = self.off
        self.off += w
        assert self.off <= self.n, ("arena overflow", self.off, self.n)
        return V(self.tile[0:p, o:o + w].bitcast(BF16)[:, 0:cols])


def run_interleaved(gen_fns, max_active, start_every=1):
    active = []
    nxt = 0
    rounds = 0
    while active or nxt < len(gen_fns):
        if nxt < len(gen_fns) and len(active) < max_active and rounds % start_every == 0:
            active.append(gen_fns[nxt]())
            nxt += 1
        for gen in list(active):
            try:
                next(gen)
            except StopIteration:
                active.remove(gen)
        rounds += 1


def build_program(n_own=32, n_oth=32, n_ctx=2, sweeps="SP12", use_yS=True, dbg=False):
    P = Prog()
    nc = P.nc
    EI = "ExternalInput"
    xl = P.dram("xl", [SEQ, D], F32, EI)
    xr = P.dram("xr", [SEQ, D], F32, EI)
    cl = P.dram("cl", [CTX, D], F32, EI)
    cr = P.dram("cr", [CTX, D], F32, EI)
    cvec = P.dram("cvec", [2, D], F32, EI)
    w_mod = P.dram("w_mod", [D, 3 * D], F32, EI)
    b_mod2 = P.dram("b_mod2", [2, 3 * D], F32, EI)
    norm_w2 = P.dram("norm_w2", [2, D], F32, EI)
    w_in = P.dram("w_in", [D, IN_COLS], F32, EI)
    w_dt = P.dram("w_dt", [2, D, NH], F32, EI)
    taps = P.dram("taps", [2, 5, C_END], F32, EI)
    conv_b = P.dram("conv_b", [C_END], F32, EI)
    dtb = P.dram("dtb", [2, NH], F32, EI)
    alog = P.dram("alog", [2, NH], F32, EI)
    dskip = P.dram("dskip", [NH], F32, EI)
    ssm_nw = P.dram("ssm_nw", [DIN], F32, EI)
    w_oss = P.dram("w_oss", [DIN, D], F32, EI)
    cconv = P.dram("cconv", [31, D], F32, EI)
    cconv_b = P.dram("cconv_b", [D], F32, EI)
    ln_w = P.dram("ln_w", [D], F32, EI)
    ln_b = P.dram("ln_b", [D], F32, EI)
    w_oc = P.dram("w_oc", [D, D], F32, EI)
    w_o = P.dram("w_o", [D, D], F32, EI)
    fnw = P.dram("fnw", [D], F32, EI)
    consts = P.dram("consts", [7, 128, 128], F32, EI)
    out = P.dram("out", [OWN, D], F32, "ExternalOutput")
    dk = "ExternalOutput" if dbg else "Internal"
    yS_d = P.dram("yS_d", [32, 128, DIN], BF16, dk)
    yt_d = P.dram("yt_d", [32, 128, DIN], F32, dk)
    bs_d = P.dram("bs_d", [32, 128, D], F32, dk)
    if dbg:
        dbg_xbc = P.dram("dbg_xbc", [128, 32, 128], BF16, "ExternalOutput")
        dbg_sm = P.dram("dbg_sm", [11, 128, NH], F32, "ExternalOutput")
        dbg_hT = P.dram("dbg_hT", [128, 8, 132], BF16, "ExternalOutput")
    dbg_done = [False]
    used = []

    def dma(out_ap, in_ap, reads, writes, queue="sp", slow=False):
        if slow:
            return P.dma(queue, lambda e: e.dma_start(out=out_ap, in_=in_ap, allow_slow_non_contiguous=True), reads, writes)
        return P.dma(queue, lambda e: e.dma_start(out=out_ap, in_=in_ap), reads, writes)

    ident_f = P.sb("ident_f", [128, 128])
    UP = P.sb("UP", [128, 128])
    negP = P.sb("negP", [128, 128])
    ones_f = P.sb("ones_f", [128, 128])
    J_f = P.sb("J_f", [128, 128])
    sel0 = P.sb("sel0", [2, 128])
    ident_b = P.sb("ident_b", [128, 128], BF16)
    dummy = P.sb("dummy_t", [128, 8])
    for i, tl in enumerate([ident_f, UP, negP, ones_f, J_f]):
        dma(tl.t[:], consts[i, :, :], [], [tl.b])
    dma(sel0.t[:], consts[5, 0:2, :], [], [sel0.b])
    P.dve(lambda e: e.tensor_copy(out=ident_b.t[:], in_=ident_f.t[:]), [ident_f.b], [ident_b.b])
    J_b = P.sb("J_b", [128, 128], BF16)
    P.dve(lambda e: e.tensor_copy(out=J_b.t[:], in_=J_f.t[:]), [J_f.b], [J_b.b])
    negP4b = P.sb("negP4b", [128, 512], BF16)
    P.dve(lambda e: e.tensor_copy(out=negP4b.t[:].rearrange("p (h l) -> p h l", l=128), in_=negP.t[:].unsqueeze(1).broadcast_to([128, 4, 128])), [negP.b], [negP4b.b])

    PS = [P.ps("ps%d" % i, [128, 512]) for i in range(7)]
    psT = P.ps("psT", [128, 1024], BF16)

    tiny = {
        "pe": lambda e: e.matmul(PS[2].t[0:2, 0:2], lhsT=ident_f.t[0:2, 0:2], rhs=ident_f.t[0:2, 0:2], start=True, stop=True),
        "act": lambda e: e.copy(out=dummy.t[0:1, 0:1], in_=ident_f.t[0:1, 0:1]),
        "dve": lambda e: e.memset(dummy.t[0:1, 2:3], 0.0),
        "pool": lambda e: e.memset(dummy.t[0:1, 4:5], 0.0),
    }

    WA = P.sb("WA", [128, 65536], BF16)
    bWlo = Buf()
    bWhi = Buf()
    WKN = 17408
    WK = P.sb("WK", [128, WKN])
    ar = Arena(WK.t, WKN)

    def wload(col0, src_ap, kb, n, buf, step=4):
        view = WA.t[:, col0:col0 + kb * n].rearrange("p (k n) -> p k n", n=n)
        for k0 in range(0, kb, step):
            dma(view[:, k0:k0 + step, :], src_ap[:, k0:k0 + step, :], [], [buf], queue="pool")
        return view

    scT = P.sb("scT", [128, 8, 2])
    for j in range(2):
        cTj = P.sb("cT%d" % j, [128, 8])
        dma(cTj.t[:], cvec[j].rearrange("(k p) -> p k", p=128), [], [cTj.b], slow=True)
        P.act(lambda e, j=j, cTj=cTj: e.activation(out=scT.t[:, :, j], in_=cTj.t[:], func=AF.Silu), [cTj.b], [scT.b])
    wm = ar.f32(128, 8192)
    wm3 = wm.t.rearrange("p (k n) -> p k n", n=1024)
    modrow = ar.f32(2, 3 * D)
    bmod = ar.f32(2, 3 * D)
    nw2 = ar.f32(2, D)
    Arow = ar.f32(2, D)
    dma(bmod.t, b_mod2[:, :], [], [bmod.b])
    dma(nw2.t, norm_w2[:, :], [], [nw2.b])
    for piece in range(3):
        dma(wm3, w_mod.rearrange("(k p) n -> p k n", p=128)[:, :, piece * 1024:(piece + 1) * 1024], [], [wm.b])
        for hf in range(2):
            pst = PS[hf]
            for k in range(8):
                P.pe(lambda e, k=k, hf=hf, pst=pst: e.matmul(pst.t[0:2, :], lhsT=scT.t[:, k, :], rhs=wm3[:, k, hf * 512:(hf + 1) * 512],
                                                             start=(k == 0), stop=(k == 7)),
                     [scT.b, wm.b], [pst.b], acc=True)
            c0 = piece * 1024 + hf * 512
            P.dve(lambda e, pst=pst, c0=c0: e.tensor_tensor(out=modrow.t[:, c0:c0 + 512], in0=pst.t[0:2, :], in1=bmod.t[:, c0:c0 + 512], op=ALU.add),
                  [pst.b, bmod.b], [modrow.b])
    P.dve(lambda e: e.scalar_tensor_tensor(out=Arow.t, in0=modrow.t[:, D:2 * D], scalar=1.0, in1=nw2.t, op0=ALU.add, op1=ALU.mult),
          [modrow.b, nw2.b], [Arow.b])
    Afm = [P.sb("Afm%d" % j, [128, 8]) for j in range(2)]
    Sfm = [P.sb("Sfm%d" % j, [128, 8]) for j in range(2)]
    for (src, dst) in ((Arow, Afm), (modrow, Sfm)):
        for k in range(8):
            P.pe(lambda e, src=src, k=k: e.transpose(out=PS[2].t[:, 2 * k:2 * k + 2], in_=src.t[0:2, k * 128:(k + 1) * 128], identity=ident_f.t[0:2, 0:2]),
                 [src.b, ident_f.b], [PS[2].b], acc=True)
        for j in range(2):
            P.dve(lambda e, j=j, dst=dst: e.tensor_copy(out=dst[j].t[:], in_=PS[2].t[:, 0:16].rearrange("p (k j) -> p k j", j=2)[:, :, j]),
                  [PS[2].b], [dst[j].b])
    gate_d = P.dram("gate_d", [1, D], F32)
    bgate = Buf()
    dma(gate_d[:, :], modrow.t[0:1, 2 * D:3 * D], [modrow.b], [bgate])

    def bc_load(name, vec_ap, n):
        tl = P.sb(name, [128, n])
        dma(tl.t[:], vec_ap.partition_broadcast(128), [], [tl.b])
        return tl

    a_bc = []
    dtb_bc = []
    for j in range(2):
        al = bc_load("alog%d" % j, alog[j, :], NH)
        a = P.sb("a_bc%d" % j, [128, NH])
        P.act(lambda e, al=al, a=a: e.activation(out=a.t[:], in_=al.t[:], func=AF.Exp), [al.b], [a.b])
        P.dve(lambda e, a=a: e.tensor_scalar(out=a.t[:], in0=a.t[:], scalar1=-1.0, scalar2=None, op0=ALU.mult), [a.b], [a.b])
        a_bc.append(a)
        dtb_bc.append(bc_load("dtb%d" % j, dtb[j, :], NH))
    dsk_bc = bc_load("dsk", dskip, NH)
    tapsT = []
    for j in range(2):
        tl = P.sb("taps%d" % j, [128, 5, 32])
        for o in range(5):
            dma(tl.t[:, o, :], taps[j, o].rearrange("(cb p) -> p cb", p=128), [], [tl.b], slow=True)
        tapsT.append(tl)
    convb = P.sb("convb", [128, 32])
    dma(convb.t[:], conv_b.rearrange("(cb p) -> p cb", p=128), [], [convb.b], slow=True)
    wdt = []
    for j in range(2):
        tl = P.sb("wdt%d" % j, [128, 8, NH], BF16)
        dma(tl.t[:], w_dt[j].rearrange("(k p) n -> p k n", p=128), [], [tl.b], queue="pool")
        wdt.append(tl)

    w_in_v = w_in.rearrange("(k p) n -> p k n", p=128)
    wxbc = wload(0, w_in_v[:, :, 0:C_END], 8, C_END, bWlo)

    P.barrier(tiny, [PS[2].b])
    ar.off = 0
    xt0 = ar.f32(128, D)
    xt = [xt0, xt0]
    xn = ar.f32(128, D)
    xh = ar.f32(4, D)
    xhn = xh
    ss = ar.f32(128, 2)
    hTe = [ar.bf16(128, 8 * 132) for i in range(2)]
    for h_ in hTe:
        h_.t = h_.t.rearrange("p (k n) -> p k n", n=132)
    AR_BASE = ar.off

    def make_hT_gen(dst, src, t0, lo_ok, hi_ok, j, slot, halo=True):
        x_t = xt[slot]
        dma(x_t.t, src[t0:t0 + 128, :], [], [x_t.b])
        P.pool(lambda e: e.memset(ss.t, 0.0), [], [ss.b])
        P.act(lambda e: e.activation(out=xn.t, in_=x_t.t, func=AF.Square, accum_out=ss.t[:, 0:1]), [x_t.b], [xn.b, ss.b])
        if halo:
            P.pool(lambda e: e.memset(xh.t, 0.0), [], [xh.b])
            if lo_ok:
                dma(xh.t[0:2, :], src[t0 - 2:t0, :], [], [xh.b])
            if hi_ok:
                dma(xh.t[2:4, :], src[t0 + 128:t0 + 130, :], [], [xh.b])
            P.act(lambda e: e.activation(out=xn.t[0:4, :], in_=xh.t, func=AF.Square, accum_out=ss.t[0:4, 1:2]), [xh.b], [xn.b, ss.b])
        yield
        P.dve(lambda e: e.tensor_scalar(out=ss.t, in0=ss.t, scalar1=1.0 / D, scalar2=EPS, op0=ALU.mult, op1=ALU.add), [ss.b], [ss.b])
        yield
        P.act(lambda e: e.activation(out=ss.t, in_=ss.t, func=AF.Ln), [ss.b], [ss.b])
        yield
        P.act(lambda e: e.activation(out=ss.t, in_=ss.t, func=AF.Exp, scale=-0.5), [ss.b], [ss.b])
        yield
        P.dve(lambda e: e.tensor_scalar(out=xn.t, in0=x_t.t, scalar1=ss.t[:, 0:1], scalar2=None, op0=ALU.mult), [x_t.b, ss.b], [xn.b])
        yield
        for hf in range(2):
            pst = PS[hf]
            yield
            for kk in range(4):
                k = hf * 4 + kk
                P.pe(lambda e, k=k, kk=kk, pst=pst: e.transpose(out=pst.t[:, kk * 128:(kk + 1) * 128], in_=xn.t[:, k * 128:(k + 1) * 128], identity=ident_f.t[:]),
                     [xn.b, ident_f.b], [pst.b], acc=True)
            for kk in range(4):
                k = hf * 4 + kk
                P.act(lambda e, k=k, kk=kk, pst=pst: e.activation(out=dst.t[:, k, 2:130], in_=pst.t[:, kk * 128:(kk + 1) * 128], func=AF.Identity,
                                                                  scale=Afm[j].t[:, k:k + 1], bias=Sfm[j].t[:, k:k + 1]),
                      [pst.b, Afm[j].b, Sfm[j].b], [dst.b])
        yield
        if halo:
            P.dve(lambda e: e.tensor_scalar(out=xhn.t, in0=xh.t, scalar1=ss.t[0:4, 1:2], scalar2=None, op0=ALU.mult), [xh.b, ss.b], [xhn.b])
            pst = PS[2]
            for k in range(8):
                P.pe(lambda e, k=k: e.transpose(out=pst.t[:, 128 + k * 4:128 + (k + 1) * 4], in_=xhn.t[0:4, k * 128:(k + 1) * 128], identity=ident_f.t[0:4, 0:4]),
                     [xhn.b, ident_f.b], [pst.b], acc=True)
            yield
            for k in range(8):
                yield
                P.act(lambda e, k=k: e.activation(out=dst.t[:, k, 0:2], in_=pst.t[:, 128 + k * 4:128 + k * 4 + 2], func=AF.Identity,
                                                  scale=Afm[j].t[:, k:k + 1], bias=Sfm[j].t[:, k:k + 1]),
                      [pst.b, Afm[j].b, Sfm[j].b], [dst.b])
                P.act(lambda e, k=k: e.activation(out=dst.t[:, k, 130:132], in_=pst.t[:, 128 + k * 4 + 2:128 + k * 4 + 4], func=AF.Identity,
                                                  scale=Afm[j].t[:, k:k + 1], bias=Sfm[j].t[:, k:k + 1]),
                      [pst.b, Afm[j].b, Sfm[j].b], [dst.b])
            if not lo_ok:
                P.dve(lambda e: e.memset(dst.t[:, :, 0:2], 0.0), [], [dst.b])
            if not hi_ok:
                P.dve(lambda e: e.memset(dst.t[:, :, 130:132], 0.0), [], [dst.b])

    def make_hT(*a_, **k_):
        for _ in make_hT_gen(*a_, **k_):
            pass

    ar2 = Arena(WA.t[:, 32768:65536].bitcast(F32), 16384)
    acc = [ar.f32(128, 128) for i in range(8)]
    xbcT = ar2.bf16(128, 32 * 128)
    xbcT.t = xbcT.t.rearrange("p (c n) -> p c n", n=128)
    bxb = [Buf() for _ in range(32)]
    xs_tok = ar2.bf16(128, DIN)
    B_tok = ar2.bf16(128, 1024)
    xsD = ar2.bf16(128, DIN)
    sm = {n: ar.f32(128, NH) for n in ("x1", "ab", "e", "l1", "dt", "dta", "la", "cd", "d1", "e1", "we", "ela")}
    xsdt = ar2.bf16(128, DIN)
    Zs = [ar2.bf16(128, 256) for i in range(2)]
    state = ar2.f32(128, DIN)
    state_b = ar2.bf16(128, DIN)
    bst = [Buf() for _ in range(NG)]
    bstb = [Buf() for _ in range(NG)]

    def v3(v, n):
        v.t = v.t.rearrange("p (h l) -> p h l", l=n)
        return v
    dtaU = [v3(ar.f32(128, 512), 128) for i in range(2)]
    diff = [v3(ar.f32(128, 512), 128) for i in range(2)]
    ela = [v3(ar.f32(128, 512), 128) for i in range(2)]
    mixT = [v3(ar.bf16(128, 512), 128) for i in range(2)]
    CS = [v3(ar.bf16(128, 512), 128) for i in range(2)]
    xsw = [ar.bf16(128, 256) for i in range(2)]
    y_g = [ar.f32(128, 256) for i in range(2)]
    yS_g = [ar.bf16(128, 256) for i in range(2)]
    y_gb = [ar.bf16(128, 256) for i in range(2)]
    psPRE = [PS[3], PS[4], PS[5], PS[6]]
    psLs = [PS[5], PS[3]]
    psYs = [PS[6], PS[4]]
    pscs = [PS[0], PS[2]]

    def ssd_chunk(hT, j, full, p_sweep=False, ys_src=None, y_dst=None, ys_buf=None, yd_buf=None, extra_gens=()):
        nblk = 32 if full else 24
        tp = tapsT[j]
        s = sm
        def dt_gen():
            pm = PS[2]
            for k in range(8):
                P.pe(lambda e, k=k: e.matmul(pm.t[:, 0:32], lhsT=hT.t[:, k, 2:130], rhs=wdt[j].t[:, k, :], start=(k == 0), stop=(k == 7)),
                     [hT.b, wdt[j].b], [pm.b], acc=True)
            s = sm
            P.dve(lambda e: e.tensor_tensor(out=s["x1"].t, in0=pm.t[:, 0:32], in1=dtb_bc[j].t[:], op=ALU.add), [pm.b, dtb_bc[j].b], [s["x1"].b])
            yield
            P.dve(lambda e: e.scalar_tensor_tensor(out=s["ab"].t, in0=s["x1"].t, scalar=-1.0, in1=s["x1"].t, op0=ALU.mult, op1=ALU.max), [s["x1"].b], [s["ab"].b])
            yield
            P.act(lambda e: e.activation(out=s["e"].t, in_=s["ab"].t, func=AF.Exp, scale=-1.0), [s["ab"].b], [s["e"].b])
            yield
            P.dve(lambda e: e.tensor_scalar(out=s["d1"].t, in0=s["e"].t, scalar1=2.0, scalar2=None, op0=ALU.add), [s["e"].b], [s["d1"].b])
            yield
            P.dve(lambda e: e.reciprocal(out=s["d1"].t, in_=s["d1"].t), [s["d1"].b], [s["d1"].b])
            yield
            P.dve(lambda e: e.tensor_tensor(out=s["e1"].t, in0=s["e"].t, in1=s["d1"].t, op=ALU.mult), [s["e"].b, s["d1"].b], [s["e1"].b])
            P.dve(lambda e: e.tensor_tensor(out=s["d1"].t, in0=s["e1"].t, in1=s["e1"].t, op=ALU.mult), [s["e1"].b], [s["d1"].b])
            P.dve(lambda e: e.tensor_scalar(out=s["l1"].t, in0=s["d1"].t, scalar1=1.0 / 11, scalar2=1.0 / 9, op0=ALU.mult, op1=ALU.add), [s["d1"].b], [s["l1"].b])
            yield
            for cst in (1.0 / 7, 1.0 / 5, 1.0 / 3, 1.0):
                P.dve(lambda e: e.tensor_tensor(out=s["l1"].t, in0=s["l1"].t, in1=s["d1"].t, op=ALU.mult), [s["l1"].b, s["d1"].b], [s["l1"].b])
                P.dve(lambda e, cst=cst: e.tensor_scalar(out=s["l1"].t, in0=s["l1"].t, scalar1=cst, scalar2=None, op0=ALU.add), [s["l1"].b], [s["l1"].b])
            P.dve(lambda e: e.scalar_tensor_tensor(out=s["l1"].t, in0=s["l1"].t, scalar=2.0, in1=s["e1"].t, op0=ALU.mult, op1=ALU.mult), [s["l1"].b, s["e1"].b], [s["l1"].b])
            yield
            P.dve(lambda e: e.scalar_tensor_tensor(out=s["dt"].t, in0=s["x1"].t, scalar=0.0, in1=s["l1"].t, op0=ALU.max, op1=ALU.add),
                  [s["x1"].b, s["l1"].b], [s["dt"].b])
            P.dve(lambda e: e.tensor_tensor(out=s["dta"].t, in0=s["dt"].t, in1=a_bc[j].t[:], op=ALU.mult), [s["dt"].b, a_bc[j].b], [s["dta"].b])
            yield
            P.pe(lambda e: e.matmul(pm.t[:, 32:64], lhsT=UP.t[:], rhs=s["dta"].t, start=True, stop=True), [UP.b, s["dta"].b], [pm.b])
            yield
            P.pe(lambda e: e.matmul(pm.t[:, 64:96], lhsT=ones_f.t[:], rhs=s["dta"].t, start=True, stop=True), [ones_f.b, s["dta"].b], [pm.b], acc=True)
            P.act(lambda e: e.copy(out=s["la"].t, in_=pm.t[:, 32:64]), [pm.b], [s["la"].b])
            yield
            P.act(lambda e: e.activation(out=s["ela"].t, in_=s["la"].t, func=AF.Exp), [s["la"].b], [s["ela"].b])
            yield
            P.act(lambda e: e.activation(out=s["cd"].t, in_=pm.t[:, 64:96], func=AF.Exp), [pm.b], [s["cd"].b])
            yield
            P.dve(lambda e: e.tensor_tensor(out=s["d1"].t, in0=pm.t[:, 64:96], in1=s["la"].t, op=ALU.subtract), [pm.b, s["la"].b], [s["d1"].b])
            yield
            P.act(lambda e: e.activation(out=s["e1"].t, in_=s["d1"].t, func=AF.Exp), [s["d1"].b], [s["e1"].b])
            yield
            P.dve(lambda e: e.tensor_tensor(out=s["we"].t, in0=s["e1"].t, in1=s["dt"].t, op=ALU.mult), [s["e1"].b, s["dt"].b], [s["we"].b])
            yield
            yield

        def blk_gen(cb):
            pst = psPRE[cb % 4]
            o0 = 0
            for k in range(8):
                P.pe(lambda e, k=k: e.matmul(pst.t[:, o0:o0 + 132], lhsT=wxbc[:, k, cb * 128:(cb + 1) * 128], rhs=hT.t[:, k, :],
                                             start=(k == 0), stop=(k == 7)),
                     [bWlo, hT.b], [pst.b], acc=True)
            a_t = acc[cb % len(acc)]
            P.act(lambda e: e.activation(out=a_t.t, in_=pst.t[:, o0:o0 + 128], func=AF.Identity,
                                         scale=tp.t[:, 0, cb:cb + 1], bias=convb.t[:, cb:cb + 1]),
                  [pst.b, tp.b, convb.b], [a_t.b])
            yield
            for o in range(1, 5):
                P.dve(lambda e, o=o: e.scalar_tensor_tensor(out=a_t.t, in0=pst.t[:, o0 + o:o0 + o + 128], scalar=tp.t[:, o, cb:cb + 1],
                                                            in1=a_t.t, op0=ALU.mult, op1=ALU.add),
                      [pst.b, tp.b, a_t.b], [a_t.b])
                yield
            P.act(lambda e: e.activation(out=xbcT.t[:, cb, :], in_=a_t.t, func=AF.Silu), [a_t.b], [bxb[cb]])
            yield

        def tr_gen(c0, dst, db):
            for i in range(8):
                P.pe(lambda e, i=i: e.transpose(out=psT.t[:, i * 128:(i + 1) * 128], in_=xbcT.t[:, c0 + i, :], identity=ident_b.t[:]),
                     [bxb[c0 + i], ident_b.b], [psT.b], acc=True)
            yield
            P.dve(lambda e: e.tensor_copy(out=dst, in_=psT.t[:, :]), [psT.b], [db])
            yield

        tr_list = [(lambda: tr_gen(0, xs_tok.t[:, 0:1024], xs_tok.b)), (lambda: tr_gen(8, xs_tok.t[:, 1024:2048], xs_tok.b)),
                   (lambda: tr_gen(16, B_tok.t[:, :], B_tok.b))]
        run_interleaved([dt_gen] + list(extra_gens) + [(lambda cb=cb: blk_gen(cb)) for cb in range(nblk)] + (tr_list if full else tr_list[:2]),
                        max_active=5 + len(extra_gens))
        if not full:
            run_interleaved(tr_list[2:], max_active=1)
        if full and dbg and not dbg_done[0]:
            dbg_done[0] = True
            ob = Buf()
            dma(dbg_xbc[:, :, :], xbcT.t, bxb, [ob])
            used.append(ob)
            for i_, n_ in enumerate(("x1", "ab", "e", "l1", "dt", "dta", "la", "cd", "d1", "e1", "we")):
                ob = Buf()
                dma(dbg_sm[i_], s[n_].t, [s[n_].b], [ob])
                used.append(ob)
            ob = Buf()
            dma(dbg_hT[:, :, :], hT.t, [hT.b], [ob])
            used.append(ob)
        if full:
            P.pool(lambda e: e.tensor_tensor(out=xsdt.t.rearrange("p (h d) -> p h d", d=HD), in0=xs_tok.t.rearrange("p (h d) -> p h d", d=HD),
                                             in1=s["dt"].t.unsqueeze(2).broadcast_to([128, NH, HD]), op=ALU.mult),
                   [xs_tok.b, s["dt"].b], [xsdt.b])
        if full and p_sweep:
            P.dve(lambda e: e.tensor_tensor(out=xsD.t.rearrange("p (h d) -> p h d", d=HD), in0=xs_tok.t.rearrange("p (h d) -> p h d", d=HD),
                                            in1=dsk_bc.t[:].unsqueeze(2).broadcast_to([128, NH, HD]), op=ALU.mult),
                  [xs_tok.b, dsk_bc.b], [xsD.b])
        def group_gen(g):
            i2 = g % 2
            pL = psLs[i2]
            pY = psYs[i2]
            psc = pscs[i2]
            if full:
                if p_sweep and use_yS:
                    dma(yS_g[i2].t, ys_src[:, g * 256:(g + 1) * 256], [ys_buf], [yS_g[i2].b])
                P.pool(lambda e: e.tensor_tensor(out=dtaU[i2].t, in0=UP.t[:].unsqueeze(1).broadcast_to([128, 4, 128]),
                                                 in1=s["dta"].t[:, 4 * g:4 * g + 4].unsqueeze(2).broadcast_to([128, 4, 128]), op=ALU.mult),
                       [UP.b, s["dta"].b], [dtaU[i2].b])
                P.pe(lambda e: e.matmul(psc.t[:, 0:128], lhsT=xbcT.t[:, 16 + g, :], rhs=xbcT.t[:, 24 + g, :], start=True, stop=True),
                     [bxb[16 + g], bxb[24 + g]], [psc.b])
                P.pe(lambda e: e.matmul(psc.t[:, 128:384], lhsT=xbcT.t[:, 24 + g, :], rhs=state_b.t[:, g * 256:(g + 1) * 256], start=True, stop=True),
                     [bxb[24 + g], bstb[g]], [psc.b], acc=True)
                yield
                P.pe(lambda e: e.matmul(pL.t[:, :], lhsT=ones_f.t[:], rhs=dtaU[i2].t.rearrange("p h l -> p (h l)"), start=True, stop=False, skip_group_check=True),
                     [ones_f.b, dtaU[i2].b], [pL.b])
                P.pe(lambda e: e.matmul(pL.t[:, :], lhsT=ident_b.t[:], rhs=negP4b.t[:], start=False, stop=True, skip_group_check=True),
                     [ident_b.b, negP4b.b], [pL.b], acc=True)
                yield
                P.dve(lambda e: e.tensor_tensor(out=diff[i2].t, in0=pL.t[:, :].rearrange("p (h l) -> p h l", l=128),
                                                in1=s["la"].t[:, 4 * g:4 * g + 4].unsqueeze(2).broadcast_to([128, 4, 128]), op=ALU.subtract),
                      [pL.b, s["la"].b], [diff[i2].b])
                yield
                P.act(lambda e: e.activation(out=diff[i2].t, in_=diff[i2].t, func=AF.Exp), [diff[i2].b], [diff[i2].b])
                P.dve(lambda e: e.tensor_tensor(out=Zs[i2].t.rearrange("p (h d) -> p h d", d=HD), in0=psc.t[:, 128:384].rearrange("p (h d) -> p h d", d=HD),
                                                in1=s["ela"].t[:, 4 * g:4 * g + 4].unsqueeze(2).broadcast_to([128, 4, HD]), op=ALU.mult),
                      [psc.b, s["ela"].b], [Zs[i2].b])
                yield
                P.dve(lambda e: e.tensor_tensor(out=mixT[i2].t, in0=diff[i2].t, in1=psc.t[:, 0:128].unsqueeze(1).broadcast_to([128, 4, 128]), op=ALU.mult),
                      [diff[i2].b, psc.b], [mixT[i2].b])
                yield
                P.pe(lambda e: e.matmul(pY.t[:, 0:256], lhsT=ident_b.t[:], rhs=Zs[i2].t, start=True, stop=False, skip_group_check=True),
                     [ident_b.b, Zs[i2].b], [pY.b])
                if p_sweep:
                    P.pe(lambda e: e.matmul(pY.t[:, 0:256], lhsT=ident_b.t[:], rhs=xsD.t[:, g * 256:(g + 1) * 256], start=False, stop=False, skip_group_check=True),
                         [ident_b.b, xsD.b], [pY.b], acc=True)
                    if use_yS:
                        P.pe(lambda e: e.matmul(pY.t[:, 0:256], lhsT=J_b.t[:], rhs=yS_g[i2].t, start=False, stop=False, skip_group_check=True),
                             [J_b.b, yS_g[i2].b], [pY.b], acc=True)
                for h in range(4):
                    hh = 4 * g + h
                    P.pe(lambda e, h=h, hh=hh: e.matmul(pY.t[:, h * 64:(h + 1) * 64], lhsT=mixT[i2].t[:, h, :], rhs=xsdt.t[:, hh * 64:(hh + 1) * 64],
                                                       start=False, stop=True, skip_group_check=True),
                         [mixT[i2].b, xsdt.b], [pY.b], acc=True)
                yield
                yo = y_g[i2] if p_sweep else y_gb[i2]
                P.act(lambda e: e.copy(out=yo.t, in_=pY.t[:, 0:256]), [pY.b], [yo.b])
                dma(y_dst[:, g * 256:(g + 1) * 256], yo.t, [yo.b], [yd_buf])
                yield
            P.pool(lambda e: e.tensor_tensor(out=xsw[i2].t.rearrange("p (h d) -> p h d", d=HD), in0=xs_tok.t[:, g * 256:(g + 1) * 256].rearrange("p (h d) -> p h d", d=HD),
                                             in1=s["we"].t[:, 4 * g:4 * g + 4].unsqueeze(2).broadcast_to([128, 4, HD]), op=ALU.mult),
                   [xs_tok.b, s["we"].b], [xsw[i2].b])
            yield
            pct = PS[1]
            P.pe(lambda e: e.matmul(pct.t[:, i2 * 256:(i2 + 1) * 256], lhsT=B_tok.t[:, g * 128:(g + 1) * 128], rhs=xsw[i2].t, start=True, stop=True),
                 [B_tok.b, xsw[i2].b], [pct.b])
            yield
            sv = state.t[:, g * 256:(g + 1) * 256]
            P.dve(lambda e: e.tensor_tensor(out=sv.rearrange("p (h d) -> p h d", d=HD), in0=sv.rearrange("p (h d) -> p h d", d=HD),
                                            in1=s["cd"].t[:, 4 * g:4 * g + 4].unsqueeze(2).broadcast_to([128, 4, HD]), op=ALU.mult),
                  [bst[g], s["cd"].b], [bst[g]])
            yield
            P.dve(lambda e: e.tensor_tensor(out=sv, in0=pct.t[:, i2 * 256:(i2 + 1) * 256], in1=sv, op=ALU.add), [pct.b, bst[g]], [bst[g]])
            yield
            P.act(lambda e: e.copy(out=state_b.t[:, g * 256:(g + 1) * 256], in_=sv), [bst[g]], [bstb[g]])
            yield

        skew = 4 if full else 2
        active = []
        nxt = 0
        steps_of_last = 0
        while active or nxt < NG:
            if nxt < NG and len(active) < 2 and (not active or steps_of_last >= skew):
                active.append(group_gen(nxt))
                nxt += 1
                steps_of_last = 0
            for gen in list(active):
                try:
                    next(gen)
                except StopIteration:
                    active.remove(gen)
            steps_of_last += 1

    def reset_state():
        P.pool(lambda e: e.memset(state.t, 0.0), bst, bst)
        P.pool(lambda e: e.memset(state_b.t, 0.0), bstb, bstb)

    bYS = [Buf() for _ in range(32)]
    bYT = [Buf() for _ in range(32)]
    bBS = [Buf() for _ in range(32)]

    def run_sweep(chunks, j, p_sweep):
        make_hT(hTe[0], chunks[0][0], chunks[0][1], chunks[0][2], chunks[0][3], chunks[0][4], 0)
        for i, (src, t0, lo, hi, mj, full, oi) in enumerate(chunks):
            eg = []
            if i + 1 < len(chunks):
                n = chunks[i + 1]
                eg = [lambda n=n, i=i: make_hT_gen(hTe[(i + 1) % 2], n[0], n[1], n[2], n[3], n[4], (i + 1) % 2)]
            if full and p_sweep:
                cS = 31 - oi
                ssd_chunk(hTe[i % 2], j, True, True, ys_src=yS_d[cS], y_dst=yt_d[oi], ys_buf=bYS[cS], yd_buf=bYT[oi], extra_gens=eg)
                used.append(bYT[oi])
            elif full:
                ssd_chunk(hTe[i % 2], j, True, False, y_dst=yS_d[oi], yd_buf=bYS[oi], extra_gens=eg)
                used.append(bYS[oi])
            else:
                ssd_chunk(hTe[i % 2], j, False, extra_gens=eg)

    if "S" in sweeps:
        reset_state()
        ch = []
        for c in range(n_ctx):
            ch.append((cr, c * 128, c > 0, c < CTX // 128 - 1, 1, False, None))
        for c in range(32 - n_oth, 32):
            ch.append((xr, c * 128, c > 0, True, 0, False, None))
        for c in range(n_own):
            ch.append((xr, OWN + c * 128, True, (OWN + c * 128 + 128) < SEQ, 0, True, c))
        run_sweep(ch, 1, False)
    if "P" in sweeps:
        reset_state()
        ch = []
        for c in range(n_ctx):
            ch.append((cl, c * 128, c > 0, c < CTX // 128 - 1, 1, False, None))
        for c in range(n_own):
            ch.append((xl, c * 128, c > 0, True, 0, True, c))
        run_sweep(ch, 0, True)

    if "1" in sweeps:
        P.barrier(tiny, [PS[2].b])
        ar.off = AR_BASE
        wz = wload(32768, w_in_v[:, :, DT_END:Z_END], 8, DIN, bWhi)
        woss = wload(49152, w_oss.rearrange("(k p) n -> p k n", p=128), 16, D, bWhi, step=8)
        wglu = wload(0, w_in_v[:, :, Z_END:GLU_END], 8, 2 * D, bWlo)
        wgt = wload(16384, w_in_v[:, :, CG_END:IN_COLS], 8, 2 * D, bWlo)
        zs = ar.f32(128, DIN)
        y_in = ar.f32(128, DIN)
        ssg = ar.f32(128, 8)
        yn = ar.bf16(128, DIN)
        ynT = ar.bf16(128, 16 * 128)
        ynT.t = ynT.t.rearrange("p (k n) -> p k n", n=128)
        bs_sb = ar.f32(128, D)
        normw_bc = ar.f32(128, DIN)
        dma(normw_bc.t, ssm_nw.partition_broadcast(128), [], [normw_bc.b])

        def t1_gen(hT, c):
            for qd in range(4):
                pz = PS[3 + qd]
                for k in range(8):
                    P.pe(lambda e, k=k, qd=qd, pz=pz: e.matmul(pz.t[:, :], lhsT=hT.t[:, k, 2:130], rhs=wz[:, k, qd * 512:(qd + 1) * 512], start=(k == 0), stop=(k == 7)),
                         [hT.b, bWhi], [pz.b], acc=True)
                P.act(lambda e, qd=qd, pz=pz: e.activation(out=zs.t[:, qd * 512:(qd + 1) * 512], in_=pz.t[:, :], func=AF.Silu), [pz.b], [zs.b])
                yield
            dma(y_in.t, yt_d[c], [bYT[c]], [y_in.b])
            P.dve(lambda e: e.tensor_tensor(out=y_in.t, in0=y_in.t, in1=zs.t, op=ALU.mult), [y_in.b, zs.b], [y_in.b])
            yield
            P.pool(lambda e: e.memset(ssg.t, 0.0), [], [ssg.b])
            for g in range(NG):
                P.act(lambda e, g=g: e.activation(out=zs.t[:, g * 256:(g + 1) * 256], in_=y_in.t[:, g * 256:(g + 1) * 256], func=AF.Square, accum_out=ssg.t[:, g:g + 1]),
                      [y_in.b], [zs.b, ssg.b])
                yield
            P.dve(lambda e: e.tensor_scalar(out=ssg.t, in0=ssg.t, scalar1=1.0 / 256, scalar2=EPS, op0=ALU.mult, op1=ALU.add), [ssg.b], [ssg.b])
            P.act(lambda e: e.activation(out=ssg.t, in_=ssg.t, func=AF.Ln), [ssg.b], [ssg.b])
            P.act(lambda e: e.activation(out=ssg.t, in_=ssg.t, func=AF.Exp, scale=-0.5), [ssg.b], [ssg.b])
            yield
            for g in range(NG):
                yield
                P.dve(lambda e, g=g: e.scalar_tensor_tensor(out=yn.t[:, g * 256:(g + 1) * 256], in0=y_in.t[:, g * 256:(g + 1) * 256], scalar=ssg.t[:, g:g + 1],
                                                            in1=normw_bc.t[:, g * 256:(g + 1) * 256], op0=ALU.mult, op1=ALU.mult),
                      [y_in.b, ssg.b, normw_bc.b], [yn.b])
            for ps_ in range(2):
                for i in range(8):
                    kb = ps_ * 8 + i
                    P.pe(lambda e, kb=kb, i=i: e.transpose(out=psT.t[:, i * 128:(i + 1) * 128], in_=yn.t[:, kb * 128:(kb + 1) * 128], identity=ident_b.t[:]),
                         [yn.b, ident_b.b], [psT.b], acc=True)
                yield
                P.dve(lambda e, ps_=ps_: e.tensor_copy(out=ynT.t[:, ps_ * 8:(ps_ + 1) * 8, :].rearrange("p k n -> p (k n)"), in_=psT.t[:, :]), [psT.b], [ynT.b])
                yield
            for hf in range(2):
                po = PS[2] if hf == 0 else PS[3]
                for kb in range(16):
                    P.pe(lambda e, kb=kb, hf=hf, po=po: e.matmul(po.t[:, :], lhsT=ynT.t[:, kb, :], rhs=woss[:, kb, hf * 512:(hf + 1) * 512], start=(kb == 0), stop=(kb == 15)),
                         [ynT.b, bWhi], [po.b], acc=True)
                yield
                P.act(lambda e, hf=hf, po=po: e.copy(out=bs_sb.t[:, hf * 512:(hf + 1) * 512], in_=po.t[:, :]), [po.b], [bs_sb.b])
                yield
            dma(bs_d[c], bs_sb.t, [bs_sb.b], [bBS[c]])
            yield

        make_hT(hTe[0], xl, 0, False, False, 0, 0, halo=False)
        for c in range(n_own):
            gl = [lambda c=c: t1_gen(hTe[c % 2], c)]
            if c + 1 < n_own:
                gl.append(lambda c=c: make_hT_gen(hTe[(c + 1) % 2], xl, (c + 1) * 128, False, False, 0, 0, halo=False))
            run_interleaved(gl, max_active=2)
            if "2" not in sweeps:
                used.append(bBS[c])

    if "2" in sweeps:
        P.barrier(tiny, [PS[2].b])
        ar.off = AR_BASE
        wcg = wload(32768, w_in_v[:, :, GLU_END:CG_END], 8, D, bWhi)
        woc = wload(40960, w_oc.rearrange("(k p) n -> p k n", p=128), 8, D, bWhi)
        wo = wload(49152, w_o.rearrange("(k p) n -> p k n", p=128), 8, D, bWhi)
        sg = [ar.f32(128, 128) for _ in range(4)]
        bup = [Buf() for _ in range(8)]
        bca = [Buf() for _ in range(8)]
        bcg = [Buf() for _ in range(8)]
        u_pad = ar.f32(128, 8 * 2 * 94)
        u_pad.t = u_pad.t.rearrange("p (c r n) -> p c r n", r=2, n=94)
        cacc = ar.f32(128, 8 * 128)
        cacc.t = cacc.t.rearrange("p (c n) -> p c n", n=128)
        ut = ar.f32(128, D)
        st = ar.f32(128, 8)
        suT = ar.f32(128, 8 * 128)
        cgT = ar.f32(128, 8 * 128)
        vT = ar.bf16(128, 8 * 128)
        vT.t = vT.t.rearrange("p (c n) -> p c n", n=128)
        gs = ar.f32(128, D)
        bs_in = ar.f32(128, D)
        mrg = ar.bf16(128, D)
        mT = ar.bf16(128, 8 * 128)
        mT.t = mT.t.rearrange("p (c n) -> p c n", n=128)
        gate_bc = ar.f32(128, D)
        fnw_bc = ar.f32(128, D)
        cw = ar.f32(128, 31 * 8)
        cw.t = cw.t.rearrange("p (k c) -> p k c", c=8)
        cwb = ar.f32(128, 8)
        lnw_fm = ar.f32(128, 8)
        lnb_fm = ar.f32(128, 8)
        dma(gate_bc.t, gate_d[0].partition_broadcast(128), [bgate], [gate_bc.b])
        dma(fnw_bc.t, fnw.partition_broadcast(128), [], [fnw_bc.b])
        for k in range(31):
            dma(cw.t[:, k, :], cconv[k].rearrange("(cb p) -> p cb", p=128), [], [cw.b], slow=True)
        dma(cwb.t, cconv_b.rearrange("(cb p) -> p cb", p=128), [], [cwb.b], slow=True)
        dma(lnw_fm.t, ln_w.rearrange("(cb p) -> p cb", p=128), [], [lnw_fm.b], slow=True)
        dma(lnb_fm.t, ln_b.rearrange("(cb p) -> p cb", p=128), [], [lnb_fm.b], slow=True)
        P.pool(lambda e: e.memset(u_pad.t, 0.0), [], bup)
        NPE = 8
        u_pb = ar.bf16(128, 8 * 2 * 94)
        u_pb.t = u_pb.t.rearrange("p (c r n) -> p c r n", r=2, n=94)
        bupb = [Buf() for _ in range(8)]
        P.pool(lambda e: e.memset(u_pb.t, 0.0), [], bupb)
        dg = WA.t[:, 57344:57344 + NPE * 8 * 128].rearrange("p (k c n) -> p k c n", c=8, n=128)
        bdg = Buf()
        for k in range(NPE):
            for cb in range(8):
                if (k * 8 + cb) % 2 == 0:
                    P.act(lambda e, k=k, cb=cb: e.activation(out=dg[:, k, cb, :], in_=ident_b.t[:], func=AF.Identity, scale=cw.t[:, k, cb:cb + 1]),
                          [ident_b.b, cw.b], [bdg])
                else:
                    P.pool(lambda e, k=k, cb=cb: e.tensor_scalar(out=dg[:, k, cb, :], in0=ident_b.t[:], scalar1=cw.t[:, k, cb:cb + 1], scalar2=None, op0=ALU.mult),
                           [ident_b.b, cw.b], [bdg])

        def t2_chunk(hT, c, eg=()):
            def cblk_gen(cb):
                pg = PS[3 + cb % 4]
                for (o0, wv, c0) in ((0, wglu, cb * 128), (128, wglu, D + cb * 128), (256, wcg, cb * 128)):
                    wb = bWlo if wv is wglu else bWhi
                    for k in range(8):
                        P.pe(lambda e, k=k, o0=o0, wv=wv, c0=c0: e.matmul(pg.t[:, o0:o0 + 128], lhsT=wv[:, k, c0:c0 + 128], rhs=hT.t[:, k, 2:130], start=(k == 0), stop=(k == 7)),
                             [hT.b, wb], [pg.b], acc=True)
                sgt = sg[cb % 4]
                P.act(lambda e: e.activation(out=sgt.t, in_=pg.t[:, 128:256], func=AF.Sigmoid), [pg.b], [sgt.b])
                P.act(lambda e: e.activation(out=cgT.t[:, cb * 128:(cb + 1) * 128], in_=pg.t[:, 256:384], func=AF.Silu), [pg.b], [bcg[cb]])
                yield
                P.dve(lambda e: e.tensor_tensor(out=u_pad.t[:, cb, :, 15:79], in0=pg.t[:, 0:128].rearrange("p (r n) -> p r n", n=64),
                                                in1=sgt.t.rearrange("p (r n) -> p r n", n=64), op=ALU.mult),
                      [pg.b, sgt.b], [bup[cb]])
                yield
                P.act(lambda e: e.copy(out=u_pb.t[:, cb, :, 15:79], in_=u_pad.t[:, cb, :, 15:79]), [bup[cb]], [bupb[cb]])
                yield
                cv = cacc.t[:, cb, :].rearrange("p (r n) -> p r n", n=64)
                pcv = pg.t[:, 384:512].rearrange("p (r n) -> p r n", n=64)
                for k in range(NPE):
                    P.pe(lambda e, k=k: e.matmul(pcv, lhsT=dg[:, k, cb, :], rhs=u_pb.t[:, cb, :, k:k + 64], start=(k == 0), stop=(k == NPE - 1)),
                         [bdg, bupb[cb]], [pg.b], acc=True)
                yield
                P.act(lambda e: e.activation(out=cv, in_=pcv, func=AF.Identity, bias=cwb.t[:, cb:cb + 1]),
                      [pg.b, cwb.b], [bca[cb]])
                yield
                for k in range(NPE, 31):
                    P.dve(lambda e, k=k: e.scalar_tensor_tensor(out=cv, in0=u_pad.t[:, cb, :, k:k + 64], scalar=cw.t[:, k, cb:cb + 1], in1=cv, op0=ALU.mult, op1=ALU.add),
                          [bup[cb], cw.b, bca[cb]], [bca[cb]])
                    yield

            run_interleaved(list(eg) + [(lambda cb=cb: cblk_gen(cb)) for cb in range(8)], max_active=4 + len(eg))
            P.pool(lambda e: e.memset(st.t, 0.0), [], [st.b])
            for hf in range(2):
                pu = PS[hf]
                for i in range(4):
                    cb = hf * 4 + i
                    P.pe(lambda e, cb=cb, i=i, pu=pu: e.transpose(out=pu.t[:, i * 128:(i + 1) * 128], in_=cacc.t[:, cb, :], identity=ident_f.t[:]),
                         [bca[cb], ident_f.b], [pu.b], acc=True)
                P.act(lambda e, hf=hf, pu=pu: e.activation(out=ut.t[:, hf * 512:(hf + 1) * 512], in_=pu.t[:, :], func=AF.Identity, accum_out=st.t[:, hf:hf + 1]),
                      [pu.b], [ut.b, st.b])
            P.act(lambda e: e.activation(out=gs.t, in_=ut.t, func=AF.Square, accum_out=st.t[:, 2:3]), [ut.b], [gs.b, st.b])
            P.dve(lambda e: e.tensor_tensor(out=st.t[:, 3:4], in0=st.t[:, 0:1], in1=st.t[:, 1:2], op=ALU.add), [st.b], [st.b])
            P.dve(lambda e: e.tensor_scalar(out=st.t[:, 3:4], in0=st.t[:, 3:4], scalar1=1.0 / D, scalar2=None, op0=ALU.mult), [st.b], [st.b])
            P.dve(lambda e: e.tensor_tensor(out=st.t[:, 4:5], in0=st.t[:, 3:4], in1=st.t[:, 3:4], op=ALU.mult), [st.b], [st.b])
            P.dve(lambda e: e.scalar_tensor_tensor(out=st.t[:, 5:6], in0=st.t[:, 2:3], scalar=1.0 / D, in1=st.t[:, 4:5], op0=ALU.mult, op1=ALU.subtract), [st.b], [st.b])
            P.dve(lambda e: e.tensor_scalar(out=st.t[:, 5:6], in0=st.t[:, 5:6], scalar1=EPS, scalar2=None, op0=ALU.add), [st.b], [st.b])
            P.act(lambda e: e.activation(out=st.t[:, 5:6], in_=st.t[:, 5:6], func=AF.Ln), [st.b], [st.b])
            P.act(lambda e: e.activation(out=st.t[:, 5:6], in_=st.t[:, 5:6], func=AF.Exp, scale=-0.5), [st.b], [st.b])
            P.dve(lambda e: e.scalar_tensor_tensor(out=st.t[:, 6:7], in0=st.t[:, 3:4], scalar=-1.0, in1=st.t[:, 5:6], op0=ALU.mult, op1=ALU.mult), [st.b], [st.b])
            P.dve(lambda e: e.tensor_scalar(out=ut.t, in0=ut.t, scalar1=st.t[:, 5:6], scalar2=st.t[:, 6:7], op0=ALU.mult, op1=ALU.add), [ut.b, st.b], [ut.b])
            for hf in range(2):
                pb = PS[5 + hf]
                for i in range(4):
                    cb = hf * 4 + i
                    P.pe(lambda e, cb=cb, i=i, pb=pb: e.transpose(out=pb.t[:, i * 128:(i + 1) * 128], in_=ut.t[:, cb * 128:(cb + 1) * 128], identity=ident_f.t[:]),
                         [ut.b, ident_f.b], [pb.b], acc=True)
                for i in range(4):
                    cb = hf * 4 + i
                    P.act(lambda e, cb=cb, i=i, pb=pb: e.activation(out=suT.t[:, cb * 128:(cb + 1) * 128], in_=pb.t[:, i * 128:(i + 1) * 128], func=AF.Silu,
                                                                    scale=lnw_fm.t[:, cb:cb + 1], bias=lnb_fm.t[:, cb:cb + 1]),
                          [pb.b, lnw_fm.b, lnb_fm.b], [suT.b])
            P.dve(lambda e: e.tensor_tensor(out=vT.t.rearrange("p c n -> p (c n)"), in0=suT.t, in1=cgT.t, op=ALU.mult), [suT.b] + bcg, [vT.b])
            for hf in range(2):
                pc = PS[hf]
                for kb in range(8):
                    P.pe(lambda e, kb=kb, hf=hf, pc=pc: e.matmul(pc.t[:, :], lhsT=vT.t[:, kb, :], rhs=woc[:, kb, hf * 512:(hf + 1) * 512], start=(kb == 0), stop=(kb == 7)),
                         [vT.b, bWhi], [pc.b], acc=True)
            dma(bs_in.t, bs_d[c], [bBS[c]], [bs_in.b])
            for gi in range(2):
                for hf in range(2):
                    pq = PS[5 + hf]
                    for k in range(8):
                        P.pe(lambda e, k=k, gi=gi, hf=hf, pq=pq: e.matmul(pq.t[:, :], lhsT=hT.t[:, k, 2:130], rhs=wgt[:, k, gi * D + hf * 512:gi * D + (hf + 1) * 512],
                                                                         start=(k == 0), stop=(k == 7)),
                             [hT.b, bWlo], [pq.b], acc=True)
                    P.act(lambda e, hf=hf, pq=pq: e.activation(out=gs.t[:, hf * 512:(hf + 1) * 512], in_=pq.t[:, :], func=AF.Sigmoid), [pq.b], [gs.b])
                if gi == 0:
                    P.dve(lambda e: e.tensor_tensor(out=bs_in.t, in0=bs_in.t, in1=gs.t, op=ALU.mult), [bs_in.b, gs.b], [bs_in.b])
                else:
                    for hf in range(2):
                        P.dve(lambda e, hf=hf: e.tensor_tensor(out=gs.t[:, hf * 512:(hf + 1) * 512], in0=PS[hf].t[:, :], in1=gs.t[:, hf * 512:(hf + 1) * 512], op=ALU.mult),
                              [PS[hf].b, gs.b], [gs.b])
                    P.dve(lambda e: e.tensor_tensor(out=mrg.t, in0=bs_in.t, in1=gs.t, op=ALU.add), [bs_in.b, gs.b], [mrg.b])
            for i in range(8):
                P.pe(lambda e, i=i: e.transpose(out=psT.t[:, i * 128:(i + 1) * 128], in_=mrg.t[:, i * 128:(i + 1) * 128], identity=ident_b.t[:]),
                     [mrg.b, ident_b.b], [psT.b], acc=True)
            P.dve(lambda e: e.tensor_copy(out=mT.t.rearrange("p c n -> p (c n)"), in_=psT.t[:, :]), [psT.b], [mT.b])
            dma(suT.t, xl[c * 128:(c + 1) * 128, :], [], [suT.b])
            for hf in range(2):
                po = PS[3 + hf]
                for kb in range(8):
                    P.pe(lambda e, kb=kb, hf=hf, po=po: e.matmul(po.t[:, :], lhsT=mT.t[:, kb, :], rhs=wo[:, kb, hf * 512:(hf + 1) * 512], start=(kb == 0), stop=(kb == 7)),
                         [mT.b, bWhi], [po.b], acc=True)
                P.dve(lambda e, hf=hf, po=po: e.tensor_tensor(out=ut.t[:, hf * 512:(hf + 1) * 512], in0=po.t[:, :], in1=gate_bc.t[:, hf * 512:(hf + 1) * 512], op=ALU.mult),
                      [po.b, gate_bc.b], [ut.b])
            P.dve(lambda e: e.tensor_tensor(out=ut.t, in0=ut.t, in1=suT.t, op=ALU.add), [ut.b, suT.b], [ut.b])
            P.pool(lambda e: e.memset(st.t[:, 7:8], 0.0), [], [st.b])
            P.act(lambda e: e.activation(out=gs.t, in_=ut.t, func=AF.Square, accum_out=st.t[:, 7:8]), [ut.b], [gs.b, st.b])
            P.dve(lambda e: e.tensor_scalar(out=st.t[:, 7:8], in0=st.t[:, 7:8], scalar1=1.0 / D, scalar2=EPS, op0=ALU.mult, op1=ALU.add), [st.b], [st.b])
            P.act(lambda e: e.activation(out=st.t[:, 7:8], in_=st.t[:, 7:8], func=AF.Ln), [st.b], [st.b])
            P.act(lambda e: e.activation(out=st.t[:, 7:8], in_=st.t[:, 7:8], func=AF.Exp, scale=-0.5), [st.b], [st.b])
            P.dve(lambda e: e.scalar_tensor_tensor(out=ut.t, in0=ut.t, scalar=st.t[:, 7:8], in1=fnw_bc.t, op0=ALU.mult, op1=ALU.mult), [ut.b, st.b, fnw_bc.b], [ut.b])
            ob = Buf()
            dma(out[c * 128:(c + 1) * 128, :], ut.t, [ut.b], [ob])
            used.append(ob)

        make_hT(hTe[0], xl, 0, False, False, 0, 0, halo=False)
        for c in range(n_own):
            eg = []
            if c + 1 < n_own:
                eg = [lambda c=c: make_hT_gen(hTe[(c + 1) % 2], xl, (c + 1) * 128, False, False, 0, 0, halo=False)]
            t2_chunk(hTe[c % 2], c, eg)

    nc = P.finalize(used)
    return P, nc


def _consts():
    c = np.zeros((7, 128, 128), np.float32)
    i = np.arange(128)
    c[0] = np.eye(128)
    c[1] = (i[:, None] <= i[None, :]).astype(np.float32)
    c[2] = np.where(i[None, :] >= i[:, None], 0.0, -30000.0)
    c[3] = 1.0
    c[4] = np.eye(128)[::-1]
    c[5, 0, :] = 1.0
    c[6, 1, :] = 1.0
    return c


def make_in_maps(inp):
    f = lambda a: np.ascontiguousarray(np.asarray(a, dtype=np.float32))
    x = np.asarray(inp["x"], np.float32)
    ctx = np.asarray(inp["ctx"], np.float32)
    c = np.asarray(inp["c"], np.float32)
    w_in = f(inp["w_in"][0])
    cw = np.asarray(inp["ssm_conv_w"][0], np.float32)
    z = np.zeros((1, C_END), np.float32)
    taps_nat = np.concatenate([cw, z], 0)
    taps_rev = np.concatenate([z, cw[::-1]], 0)
    consts = _consts()
    maps = []
    for core in range(8):
        b, half = core // 2, core % 2
        xb = x[b]
        cb = ctx[b]
        if half == 0:
            xl, xr, cl, cr = xb, xb[::-1], cb, cb[::-1]
            tP, tS = taps_nat, taps_rev
            dP, dS = 0, 1
            cc = np.asarray(inp["conf_conv_w"][0], np.float32)
        else:
            xl, xr, cl, cr = xb[::-1], xb, cb[::-1], cb
            tP, tS = taps_rev, taps_nat
            dP, dS = 1, 0
            cc = np.asarray(inp["conf_conv_w"][0], np.float32)[::-1]
        wdt = np.stack([w_in[:, C_END + 32 * dP:C_END + 32 * dP + 32], w_in[:, C_END + 32 * dS:C_END + 32 * dS + 32]], 0)
        m = {
            "xl": f(xl), "xr": f(xr), "cl": f(cl), "cr": f(cr),
            "cvec": f(np.stack([c[b], np.asarray(inp["c_ctx"], np.float32)], 0)),
            "w_mod": f(inp["w_mod"][0]),
            "b_mod2": f(np.stack([inp["b_mod"][0]] * 2, 0)),
            "norm_w2": f(np.stack([inp["norm_w"][0]] * 2, 0)),
            "w_in": w_in,
            "w_dt": f(wdt),
            "taps": f(np.stack([tP, tS], 0)),
            "conv_b": f(inp["ssm_conv_b"][0]),
            "dtb": f(np.stack([inp["dt_bias"][0][dP], inp["dt_bias"][0][dS]], 0)),
            "alog": f(np.stack([inp["a_log"][0][dP], inp["a_log"][0][dS]], 0)),
            "dskip": f(inp["d_skip"][0]),
            "ssm_nw": f(inp["ssm_norm_w"][0]),
            "w_oss": f(inp["w_out_ssm"][0]),
            "cconv": f(cc),
            "cconv_b": f(inp["conf_conv_b"][0]),
            "ln_w": f(inp["conf_ln_w"][0]),
            "ln_b": f(inp["conf_ln_b"][0]),
            "w_oc": f(inp["w_out_conf"][0]),
            "w_o": f(inp["w_out"][0]),
            "fnw": f(inp["final_norm_w"]),
            "consts": consts,
        }
        maps.append(m)
    return maps


def kernel(**inp):
    P, nc = build_program()
    maps = make_in_maps(inp)
    res = run_bass_kernel_spmd(nc, maps, core_ids=list(range(8)))
    outp = np.empty((4, SEQ, D), np.float32)
    for core in range(8):
        b, half = core // 2, core % 2
        o = res.results[core]["out"]
        if half == 0:
            outp[b, :OWN] = o
        else:
            outp[b, OWN:] = o[::-1]
    return outp
```

```python
import numpy as np
from contextlib import ExitStack
import concourse.bass as bass
import concourse.mybir as mybir
from concourse.bass_utils import run_bass_kernel_spmd

F32 = mybir.dt.float32
BF16 = mybir.dt.bfloat16
AF = mybir.ActivationFunctionType
ALU = mybir.AluOpType

ENGINES = ("pe", "act", "dve", "pool", "sp")
N_DMA_SEMS = 8
SEM_EPOCH = 8192

D = 1024
SEQ = 8192
OWN = 4096
CTX = 256
DIN = 2048
NH = 32
HD = 64
NG = 8
DS = 128
C_END = 4096
DT_END = C_END + 64
Z_END = DT_END + DIN
GLU_END = Z_END + 2 * D
CG_END = GLU_END + D
IN_COLS = CG_END + 2 * D
EPS = 1e-6


class Buf:
    __slots__ = ("last_w", "readers", "excl")

    def __init__(self, excl=False):
        self.last_w = None
        self.readers = {}
        self.excl = excl


class Op:
    __slots__ = ("eng", "fn", "deps", "signal", "semkey", "value", "is_dma", "implied", "waits")

    def __init__(self, eng, fn, is_dma=False):
        self.eng = eng
        self.fn = fn
        self.deps = []
        self.signal = is_dma
        self.semkey = None
        self.value = None
        self.is_dma = is_dma
        self.implied = None
        self.waits = None


class T:
    __slots__ = ("t", "b")

    def __init__(self, t):
        self.t = t
        self.b = Buf()


class Prog:
    def __init__(self):
        self.nc = bass.Bass("TRN2", target_bir_lowering=False)
        self.stack = ExitStack()
        self.order = []
        self.ndma = 0
        self.dmas = []

    def dram(self, name, shape, dtype, kind="Internal"):
        return self.nc.dram_tensor(name, list(shape), dtype, kind=kind).ap()

    def sb(self, name, shape, dtype=F32):
        return T(self.stack.enter_context(self.nc.sbuf_tensor(name, list(shape), dtype)))

    def ps(self, name, shape, dtype=F32):
        t = T(self.stack.enter_context(self.nc.psum_tensor(name, list(shape), dtype)))
        t.b.excl = True
        return t

    def _add(self, eng, fn, reads, writes, is_dma=False, pe_acc=False):
        op = Op(eng, fn, is_dma)
        deps = []
        if any(b.excl for b in reads):
            writes = list(writes) + [b for b in reads if b.excl and b not in writes]
            reads = [b for b in reads if not b.excl]
        for b in reads:
            if b.last_w is not None:
                deps.append(b.last_w)
        for b in writes:
            if b.last_w is not None:
                if not (pe_acc and b.last_w.eng == "pe" and not b.last_w.is_dma and not b.readers):
                    deps.append(b.last_w)
            deps.extend(b.readers.values())
        seen = set()
        for d in deps:
            if id(d) not in seen and d is not op:
                seen.add(id(d))
                d.signal = True
                op.deps.append(d)
        for b in reads:
            if is_dma:
                self.ndma += 1
                b.readers[("dma", self.ndma)] = op
            else:
                b.readers[eng] = op
        for b in writes:
            b.last_w = op
            b.readers = {}
        self.order.append(op)
        return op

    def pe(self, fn, reads, writes, acc=False):
        return self._add("pe", fn, reads, writes, pe_acc=acc)

    def act(self, fn, reads, writes):
        return self._add("act", fn, reads, writes)

    def dve(self, fn, reads, writes):
        return self._add("dve", fn, reads, writes)

    def pool(self, fn, reads, writes):
        return self._add("pool", fn, reads, writes)

    def dma(self, q, fn, reads, writes):
        op = self._add(q, fn, reads, writes, is_dma=True)
        self.dmas.append(op)
        return op

    def barrier(self, tiny, pe_bufs=()):
        xs = []
        for e in ("pe", "act", "dve", "pool"):
            op = self._add(e, tiny[e], [], list(pe_bufs) if e == "pe" else [])
            op.signal = True
            xs.append(op)
        dm = list(self.dmas)
        self.dmas = []
        for e in ("pe", "act", "dve", "pool", "sp"):
            op = self._add(e, tiny[e] if e != "sp" else None, [], list(pe_bufs) if e == "pe" else [])
            for d in xs + dm:
                d.signal = True
                op.deps.append(d)

    def finalize(self, final_bufs):
        nc = self.nc
        st = self.stack
        self._add("sp", None, final_bufs, [])
        cnt = {e: 0 for e in ENGINES}
        nd = {e: 0 for e in ENGINES}
        dcount = {}
        prev_on = {}
        for op in self.order:
            e = op.eng
            if op.is_dma:
                s = nd[e] % N_DMA_SEMS
                nd[e] += 1
                key = ("d", e, s)
                dcount[key] = dcount.get(key, 0) + 16
                op.semkey = key
                op.value = dcount[key]
                if key in prev_on:
                    op.deps.append(prev_on[key])
                prev_on[key] = op
            elif op.signal:
                op.semkey = ("c", e, cnt[e] // SEM_EPOCH)
                op.value = cnt[e] % SEM_EPOCH + 1
                cnt[e] += 1
        known = {e: {} for e in ENGINES}
        nw = 0
        for op in self.order:
            kn = known[op.eng]
            need = {}
            for d in op.deps:
                if kn.get(d.semkey, 0) >= d.value:
                    continue
                if d.semkey not in need or need[d.semkey].value < d.value:
                    need[d.semkey] = d
            waits = []
            for k, d in need.items():
                if kn.get(k, 0) >= d.value:
                    continue
                waits.append((k, d.value))
                kn[k] = d.value
                if d.implied:
                    for kk, vv in d.implied.items():
                        if kn.get(kk, 0) < vv:
                            kn[kk] = vv
            op.waits = waits
            nw += len(waits)
            if op.signal:
                op.implied = dict(kn)
        self.n_waits = nw
        self.counts = cnt
        sems = {}
        for e in ENGINES:
            for ep in range(cnt[e] // SEM_EPOCH + 1):
                sems[("c", e, ep)] = st.enter_context(nc.semaphore("c_%s%d" % (e, ep)))
        for e in ("sp", "act", "pool"):
            for i in range(N_DMA_SEMS):
                sems[("d", e, i)] = st.enter_context(nc.semaphore("d_%s%d" % (e, i)))
        per = {e: [op for op in self.order if op.eng == e] for e in ENGINES}
        block = st.enter_context(nc.Block())

        def run(e, eng):
            for op in per[e]:
                for (k, v) in op.waits:
                    eng.wait_ge(sems[k], v)
                if op.fn is None:
                    continue
                ins = op.fn(eng)
                if op.signal:
                    ins.then_inc(sems[op.semkey], 16 if op.is_dma else 1)

        @block.tensor
        def _(eng):
            run("pe", eng)

        @block.vector
        def _(eng):
            run("dve", eng)

        @block.scalar
        def _(eng):
            run("act", eng)

        @block.gpsimd
        def _(eng):
            run("pool", eng)

        @block.sync
        def _(eng):
            run("sp", eng)

        return nc


class V:
    __slots__ = ("t", "b")

    def __init__(self, ap, b=None):
        self.t = ap
        self.b = b if b is not None else Buf()


class Arena:
    def __init__(self, tile_f32, n):
        self.tile = tile_f32
        self.n = n
        self.off = 0

    def f32(self, p, cols):
        o = self.off
        self.off += cols
        assert self.off <= self.n, ("arena overflow", self.off, self.n)
        return V(self.tile[0:p, o:o + cols])

    def bf16(self, p, cols):
        w = (cols + 1) // 2
        o = self.off
        self.off += w
        assert self.off <= self.n, ("arena overflow", self.off, self.n)
        return V(self.tile[0:p, o:o + w].bitcast(BF16)[:, 0:cols])


def run_interleaved(gen_fns, max_active, start_every=1):
    active = []
    nxt = 0
    rounds = 0
    while active or nxt < len(gen_fns):
        if nxt < len(gen_fns) and len(active) < max_active and rounds % start_every == 0:
            active.append(gen_fns[nxt]())
            nxt += 1
        for gen in list(active):
            try:
                next(gen)
            except StopIteration:
                active.remove(gen)
        rounds += 1


def build_program(n_own=32, n_oth=32, n_ctx=2, sweeps="SP12", use_yS=True, dbg=False):
    P = Prog()
    nc = P.nc
    EI = "ExternalInput"
    xl = P.dram("xl", [SEQ, D], F32, EI)
    xr = P.dram("xr", [SEQ, D], F32, EI)
    cl = P.dram("cl", [CTX, D], F32, EI)
    cr = P.dram("cr", [CTX, D], F32, EI)
    cvec = P.dram("cvec", [2, D], F32, EI)
    w_mod = P.dram("w_mod", [D, 3 * D], F32, EI)
    b_mod2 = P.dram("b_mod2", [2, 3 * D], F32, EI)
    norm_w2 = P.dram("norm_w2", [2, D], F32, EI)
    w_in = P.dram("w_in", [D, IN_COLS], F32, EI)
    w_dt = P.dram("w_dt", [2, D, NH], F32, EI)
    taps = P.dram("taps", [2, 5, C_END], F32, EI)
    conv_b = P.dram("conv_b", [C_END], F32, EI)
    dtb = P.dram("dtb", [2, NH], F32, EI)
    alog = P.dram("alog", [2, NH], F32, EI)
    dskip = P.dram("dskip", [NH], F32, EI)
    ssm_nw = P.dram("ssm_nw", [DIN], F32, EI)
    w_oss = P.dram("w_oss", [DIN, D], F32, EI)
    cconv = P.dram("cconv", [31, D], F32, EI)
    cconv_b = P.dram("cconv_b", [D], F32, EI)
    ln_w = P.dram("ln_w", [D], F32, EI)
    ln_b = P.dram("ln_b", [D], F32, EI)
    w_oc = P.dram("w_oc", [D, D], F32, EI)
    w_o = P.dram("w_o", [D, D], F32, EI)
    fnw = P.dram("fnw", [D], F32, EI)
    consts = P.dram("consts", [7, 128, 128], F32, EI)
    out = P.dram("out", [OWN, D], F32, "ExternalOutput")
    dk = "ExternalOutput" if dbg else "Internal"
    yS_d = P.dram("yS_d", [32, 128, DIN], BF16, dk)
    yt_d = P.dram("yt_d", [32, 128, DIN], F32, dk)
    bs_d = P.dram("bs_d", [32, 128, D], F32, dk)
    if dbg:
        dbg_xbc = P.dram("dbg_xbc", [128, 32, 128], BF16, "ExternalOutput")
        dbg_sm = P.dram("dbg_sm", [11, 128, NH], F32, "ExternalOutput")
        dbg_hT = P.dram("dbg_hT", [128, 8, 132], BF16, "ExternalOutput")
    dbg_done = [False]
    used = []

    def dma(out_ap, in_ap, reads, writes, queue="sp", slow=False):
        if slow:
            return P.dma(queue, lambda e: e.dma_start(out=out_ap, in_=in_ap, allow_slow_non_contiguous=True), reads, writes)
        return P.dma(queue, lambda e: e.dma_start(out=out_ap, in_=in_ap), reads, writes)

    ident_f = P.sb("ident_f", [128, 128])
    UP = P.sb("UP", [128, 128])
    negP = P.sb("negP", [128, 128])
    ones_f = P.sb("ones_f", [128, 128])
    J_f = P.sb("J_f", [128, 128])
    sel0 = P.sb("sel0", [2, 128])
    ident_b = P.sb("ident_b", [128, 128], BF16)
    dummy = P.sb("dummy_t", [128, 8])
    for i, tl in enumerate([ident_f, UP, negP, ones_f, J_f]):
        dma(tl.t[:], consts[i, :, :], [], [tl.b])
    dma(sel0.t[:], consts[5, 0:2, :], [], [sel0.b])
    P.dve(lambda e: e.tensor_copy(out=ident_b.t[:], in_=ident_f.t[:]), [ident_f.b], [ident_b.b])
    J_b = P.sb("J_b", [128, 128], BF16)
    P.dve(lambda e: e.tensor_copy(out=J_b.t[:], in_=J_f.t[:]), [J_f.b], [J_b.b])
    negP4b = P.sb("negP4b", [128, 512], BF16)
    P.dve(lambda e: e.tensor_copy(out=negP4b.t[:].rearrange("p (h l) -> p h l", l=128), in_=negP.t[:].unsqueeze(1).broadcast_to([128, 4, 128])), [negP.b], [negP4b.b])

    PS = [P.ps("ps%d" % i, [128, 512]) for i in range(7)]
    psT = P.ps("psT", [128, 1024], BF16)

    tiny = {
        "pe": lambda e: e.matmul(PS[2].t[0:2, 0:2], lhsT=ident_f.t[0:2, 0:2], rhs=ident_f.t[0:2, 0:2], start=True, stop=True),
        "act": lambda e: e.copy(out=dummy.t[0:1, 0:1], in_=ident_f.t[0:1, 0:1]),
        "dve": lambda e: e.memset(dummy.t[0:1, 2:3], 0.0),
        "pool": lambda e: e.memset(dummy.t[0:1, 4:5], 0.0),
    }

    WA = P.sb("WA", [128, 65536], BF16)
    bWlo = Buf()
    bWhi = Buf()
    WKN = 17408
    WK = P.sb("WK", [128, WKN])
    ar = Arena(WK.t, WKN)

    def wload(col0, src_ap, kb, n, buf, step=4):
        view = WA.t[:, col0:col0 + kb * n].rearrange("p (k n) -> p k n", n=n)
        for k0 in range(0, kb, step):
            dma(view[:, k0:k0 + step, :], src_ap[:, k0:k0 + step, :], [], [buf], queue="pool")
        return view

    scT = P.sb("scT", [128, 8, 2])
    for j in range(2):
        cTj = P.sb("cT%d" % j, [128, 8])
        dma(cTj.t[:], cvec[j].rearrange("(k p) -> p k", p=128), [], [cTj.b], slow=True)
        P.act(lambda e, j=j, cTj=cTj: e.activation(out=scT.t[:, :, j], in_=cTj.t[:], func=AF.Silu), [cTj.b], [scT.b])
    wm = ar.f32(128, 8192)
    wm3 = wm.t.rearrange("p (k n) -> p k n", n=1024)
    modrow = ar.f32(2, 3 * D)
    bmod = ar.f32(2, 3 * D)
    nw2 = ar.f32(2, D)
    Arow = ar.f32(2, D)
    dma(bmod.t, b_mod2[:, :], [], [bmod.b])
    dma(nw2.t, norm_w2[:, :], [], [nw2.b])
    for piece in range(3):
        dma(wm3, w_mod.rearrange("(k p) n -> p k n", p=128)[:, :, piece * 1024:(piece + 1) * 1024], [], [wm.b])
        for hf in range(2):
            pst = PS[hf]
            for k in range(8):
                P.pe(lambda e, k=k, hf=hf, pst=pst: e.matmul(pst.t[0:2, :], lhsT=scT.t[:, k, :], rhs=wm3[:, k, hf * 512:(hf + 1) * 512],
                                                             start=(k == 0), stop=(k == 7)),
                     [scT.b, wm.b], [pst.b], acc=True)
            c0 = piece * 1024 + hf * 512
            P.dve(lambda e, pst=pst, c0=c0: e.tensor_tensor(out=modrow.t[:, c0:c0 + 512], in0=pst.t[0:2, :], in1=bmod.t[:, c0:c0 + 512], op=ALU.add),
                  [pst.b, bmod.b], [modrow.b])
    P.dve(lambda e: e.scalar_tensor_tensor(out=Arow.t, in0=modrow.t[:, D:2 * D], scalar=1.0, in1=nw2.t, op0=ALU.add, op1=ALU.mult),
          [modrow.b, nw2.b], [Arow.b])
    Afm = [P.sb("Afm%d" % j, [128, 8]) for j in range(2)]
    Sfm = [P.sb("Sfm%d" % j, [128, 8]) for j in range(2)]
    for (src, dst) in ((Arow, Afm), (modrow, Sfm)):
        for k in range(8):
            P.pe(lambda e, src=src, k=k: e.transpose(out=PS[2].t[:, 2 * k:2 * k + 2], in_=src.t[0:2, k * 128:(k + 1) * 128], identity=ident_f.t[0:2, 0:2]),
                 [src.b, ident_f.b], [PS[2].b], acc=True)
        for j in range(2):
            P.dve(lambda e, j=j, dst=dst: e.tensor_copy(out=dst[j].t[:], in_=PS[2].t[:, 0:16].rearrange("p (k j) -> p k j", j=2)[:, :, j]),
                  [PS[2].b], [dst[j].b])
    gate_d = P.dram("gate_d", [1, D], F32)
    bgate = Buf()
    dma(gate_d[:, :], modrow.t[0:1, 2 * D:3 * D], [modrow.b], [bgate])

    def bc_load(name, vec_ap, n):
        tl = P.sb(name, [128, n])
        dma(tl.t[:], vec_ap.partition_broadcast(128), [], [tl.b])
        return tl

    a_bc = []
    dtb_bc = []
    for j in range(2):
        al = bc_load("alog%d" % j, alog[j, :], NH)
        a = P.sb("a_bc%d" % j, [128, NH])
        P.act(lambda e, al=al, a=a: e.activation(out=a.t[:], in_=al.t[:], func=AF.Exp), [al.b], [a.b])
        P.dve(lambda e, a=a: e.tensor_scalar(out=a.t[:], in0=a.t[:], scalar1=-1.0, scalar2=None, op0=ALU.mult), [a.b], [a.b])
        a_bc.append(a)
        dtb_bc.append(bc_load("dtb%d" % j, dtb[j, :], NH))
    dsk_bc = bc_load("dsk", dskip, NH)
    tapsT = []
    for j in range(2):
        tl = P.sb("taps%d" % j, [128, 5, 32])
        for o in range(5):
            dma(tl.t[:, o, :], taps[j, o].rearrange("(cb p) -> p cb", p=128), [], [tl.b], slow=True)
        tapsT.append(tl)
    convb = P.sb("convb", [128, 32])
    dma(convb.t[:], conv_b.rearrange("(cb p) -> p cb", p=128), [], [convb.b], slow=True)
    wdt = []
    for j in range(2):
        tl = P.sb("wdt%d" % j, [128, 8, NH], BF16)
        dma(tl.t[:], w_dt[j].rearrange("(k p) n -> p k n", p=128), [], [tl.b], queue="pool")
        wdt.append(tl)

    w_in_v = w_in.rearrange("(k p) n -> p k n", p=128)
    wxbc = wload(0, w_in_v[:, :, 0:C_END], 8, C_END, bWlo)

    P.barrier(tiny, [PS[2].b])
    ar.off = 0
    xt0 = ar.f32(128, D)
    xt = [xt0, xt0]
    xn = ar.f32(128, D)
    xh = ar.f32(4, D)
    xhn = xh
    ss = ar.f32(128, 2)
    hTe = [ar.bf16(128, 8 * 132) for i in range(2)]
    for h_ in hTe:
        h_.t = h_.t.rearrange("p (k n) -> p k n", n=132)
    AR_BASE = ar.off

    def make_hT_gen(dst, src, t0, lo_ok, hi_ok, j, slot, halo=True):
        x_t = xt[slot]
        dma(x_t.t, src[t0:t0 + 128, :], [], [x_t.b])
        P.pool(lambda e: e.memset(ss.t, 0.0), [], [ss.b])
        P.act(lambda e: e.activation(out=xn.t, in_=x_t.t, func=AF.Square, accum_out=ss.t[:, 0:1]), [x_t.b], [xn.b, ss.b])
        if halo:
            P.pool(lambda e: e.memset(xh.t, 0.0), [], [xh.b])
            if lo_ok:
                dma(xh.t[0:2, :], src[t0 - 2:t0, :], [], [xh.b])
            if hi_ok:
                dma(xh.t[2:4, :], src[t0 + 128:t0 + 130, :], [], [xh.b])
            P.act(lambda e: e.activation(out=xn.t[0:4, :], in_=xh.t, func=AF.Square, accum_out=ss.t[0:4, 1:2]), [xh.b], [xn.b, ss.b])
        yield
        P.dve(lambda e: e.tensor_scalar(out=ss.t, in0=ss.t, scalar1=1.0 / D, scalar2=EPS, op0=ALU.mult, op1=ALU.add), [ss.b], [ss.b])
        yield
        P.act(lambda e: e.activation(out=ss.t, in_=ss.t, func=AF.Ln), [ss.b], [ss.b])
        yield
        P.act(lambda e: e.activation(out=ss.t, in_=ss.t, func=AF.Exp, scale=-0.5), [ss.b], [ss.b])
        yield
        P.dve(lambda e: e.tensor_scalar(out=xn.t, in0=x_t.t, scalar1=ss.t[:, 0:1], scalar2=None, op0=ALU.mult), [x_t.b, ss.b], [xn.b])
        yield
        for hf in range(2):
            pst = PS[hf]
            yield
            for kk in range(4):
                k = hf * 4 + kk
                P.pe(lambda e, k=k, kk=kk, pst=pst: e.transpose(out=pst.t[:, kk * 128:(kk + 1) * 128], in_=xn.t[:, k * 128:(k + 1) * 128], identity=ident_f.t[:]),
                     [xn.b, ident_f.b], [pst.b], acc=True)
            for kk in range(4):
                k = hf * 4 + kk
                P.act(lambda e, k=k, kk=kk, pst=pst: e.activation(out=dst.t[:, k, 2:130], in_=pst.t[:, kk * 128:(kk + 1) * 128], func=AF.Identity,
                                                                  scale=Afm[j].t[:, k:k + 1], bias=Sfm[j].t[:, k:k + 1]),
                      [pst.b, Afm[j].b, Sfm[j].b], [dst.b])
        yield
        if halo:
            P.dve(lambda e: e.tensor_scalar(out=xhn.t, in0=xh.t, scalar1=ss.t[0:4, 1:2], scalar2=None, op0=ALU.mult), [xh.b, ss.b], [xhn.b])
            pst = PS[2]
            for k in range(8):
                P.pe(lambda e, k=k: e.transpose(out=pst.t[:, 128 + k * 4:128 + (k + 1) * 4], in_=xhn.t[0:4, k * 128:(k + 1) * 128], identity=ident_f.t[0:4, 0:4]),
                     [xhn.b, ident_f.b], [pst.b], acc=True)
            yield
            for k in range(8):
                yield
                P.act(lambda e, k=k: e.activation(out=dst.t[:, k, 0:2], in_=pst.t[:, 128 + k * 4:128 + k * 4 + 2], func=AF.Identity,
                                                  scale=Afm[j].t[:, k:k + 1], bias=Sfm[j].t[:, k:k + 1]),
                      [pst.b, Afm[j].b, Sfm[j].b], [dst.b])
                P.act(lambda e, k=k: e.activation(out=dst.t[:, k, 130:132], in_=pst.t[:, 128 + k * 4 + 2:128 + k * 4 + 4], func=AF.Identity,
                                                  scale=Afm[j].t[:, k:k + 1], bias=Sfm[j].t[:, k:k + 1]),
                      [pst.b, Afm[j].b, Sfm[j].b], [dst.b])
            if not lo_ok:
                P.dve(lambda e: e.memset(dst.t[:, :, 0:2], 0.0), [], [dst.b])
            if not hi_ok:
                P.dve(lambda e: e.memset(dst.t[:, :, 130:132], 0.0), [], [dst.b])

    def make_hT(*a_, **k_):
        for _ in make_hT_gen(*a_, **k_):
            pass

    ar2 = Arena(WA.t[:, 32768:65536].bitcast(F32), 16384)
    acc = [ar.f32(128, 128) for i in range(8)]
    xbcT = ar2.bf16(128, 32 * 128)
    xbcT.t = xbcT.t.rearrange("p (c n) -> p c n", n=128)
    bxb = [Buf() for _ in range(32)]
    xs_tok = ar2.bf16(128, DIN)
    B_tok = ar2.bf16(128, 1024)
    xsD = ar2.bf16(128, DIN)
    sm = {n: ar.f32(128, NH) for n in ("x1", "ab", "e", "l1", "dt", "dta", "la", "cd", "d1", "e1", "we", "ela")}
    xsdt = ar2.bf16(128, DIN)
    Zs = [ar2.bf16(128, 256) for i in range(2)]
    state = ar2.f32(128, DIN)
    state_b = ar2.bf16(128, DIN)
    bst = [Buf() for _ in range(NG)]
    bstb = [Buf() for _ in range(NG)]

    def v3(v, n):
        v.t = v.t.rearrange("p (h l) -> p h l", l=n)
        return v
    dtaU = [v3(ar.f32(128, 512), 128) for i in range(2)]
    diff = [v3(ar.f32(128, 512), 128) for i in range(2)]
    ela = [v3(ar.f32(128, 512), 128) for i in range(2)]
    mixT = [v3(ar.bf16(128, 512), 128) for i in range(2)]
    CS = [v3(ar.bf16(128, 512), 128) for i in range(2)]
    xsw = [ar.bf16(128, 256) for i in range(2)]
    y_g = [ar.f32(128, 256) for i in range(2)]
    yS_g = [ar.bf16(128, 256) for i in range(2)]
    y_gb = [ar.bf16(128, 256) for i in range(2)]
    psPRE = [PS[3], PS[4], PS[5], PS[6]]
    psLs = [PS[5], PS[3]]
    psYs = [PS[6], PS[4]]
    pscs = [PS[0], PS[2]]

    def ssd_chunk(hT, j, full, p_sweep=False, ys_src=None, y_dst=None, ys_buf=None, yd_buf=None, extra_gens=()):
        nblk = 32 if full else 24
        tp = tapsT[j]
        s = sm
        def dt_gen():
            pm = PS[2]
            for k in range(8):
                P.pe(lambda e, k=k: e.matmul(pm.t[:, 0:32], lhsT=hT.t[:, k, 2:130], rhs=wdt[j].t[:, k, :], start=(k == 0), stop=(k == 7)),
                     [hT.b, wdt[j].b], [pm.b], acc=True)
            s = sm
            P.dve(lambda e: e.tensor_tensor(out=s["x1"].t, in0=pm.t[:, 0:32], in1=dtb_bc[j].t[:], op=ALU.add), [pm.b, dtb_bc[j].b], [s["x1"].b])
            yield
            P.dve(lambda e: e.scalar_tensor_tensor(out=s["ab"].t, in0=s["x1"].t, scalar=-1.0, in1=s["x1"].t, op0=ALU.mult, op1=ALU.max), [s["x1"].b], [s["ab"].b])
            yield
            P.act(lambda e: e.activation(out=s["e"].t, in_=s["ab"].t, func=AF.Exp, scale=-1.0), [s["ab"].b], [s["e"].b])
            yield
            P.dve(lambda e: e.tensor_scalar(out=s["d1"].t, in0=s["e"].t, scalar1=2.0, scalar2=None, op0=ALU.add), [s["e"].b], [s["d1"].b])
            yield
            P.dve(lambda e: e.reciprocal(out=s["d1"].t, in_=s["d1"].t), [s["d1"].b], [s["d1"].b])
            yield
            P.dve(lambda e: e.tensor_tensor(out=s["e1"].t, in0=s["e"].t, in1=s["d1"].t, op=ALU.mult), [s["e"].b, s["d1"].b], [s["e1"].b])
            P.dve(lambda e: e.tensor_tensor(out=s["d1"].t, in0=s["e1"].t, in1=s["e1"].t, op=ALU.mult), [s["e1"].b], [s["d1"].b])
            P.dve(lambda e: e.tensor_scalar(out=s["l1"].t, in0=s["d1"].t, scalar1=1.0 / 11, scalar2=1.0 / 9, op0=ALU.mult, op1=ALU.add), [s["d1"].b], [s["l1"].b])
            yield
            for cst in (1.0 / 7, 1.0 / 5, 1.0 / 3, 1.0):
                P.dve(lambda e: e.tensor_tensor(out=s["l1"].t, in0=s["l1"].t, in1=s["d1"].t, op=ALU.mult), [s["l1"].b, s["d1"].b], [s["l1"].b])
                P.dve(lambda e, cst=cst: e.tensor_scalar(out=s["l1"].t, in0=s["l1"].t, scalar1=cst, scalar2=None, op0=ALU.add), [s["l1"].b], [s["l1"].b])
            P.dve(lambda e: e.scalar_tensor_tensor(out=s["l1"].t, in0=s["l1"].t, scalar=2.0, in1=s["e1"].t, op0=ALU.mult, op1=ALU.mult), [s["l1"].b, s["e1"].b], [s["l1"].b])
            yield
            P.dve(lambda e: e.scalar_tensor_tensor(out=s["dt"].t, in0=s["x1"].t, scalar=0.0, in1=s["l1"].t, op0=ALU.max, op1=ALU.add),
                  [s["x1"].b, s["l1"].b], [s["dt"].b])
            P.dve(lambda e: e.tensor_tensor(out=s["dta"].t, in0=s["dt"].t, in1=a_bc[j].t[:], op=ALU.mult), [s["dt"].b, a_bc[j].b], [s["dta"].b])
            yield
            P.pe(lambda e: e.matmul(pm.t[:, 32:64], lhsT=UP.t[:], rhs=s["dta"].t, start=True, stop=True), [UP.b, s["dta"].b], [pm.b])
            yield
            P.pe(lambda e: e.matmul(pm.t[:, 64:96], lhsT=ones_f.t[:], rhs=s["dta"].t, start=True, stop=True), [ones_f.b, s["dta"].b], [pm.b], acc=True)
            P.act(lambda e: e.copy(out=s["la"].t, in_=pm.t[:, 32:64]), [pm.b], [s["la"].b])
            yield
            P.act(lambda e: e.activation(out=s["ela"].t, in_=s["la"].t, func=AF.Exp), [s["la"].b], [s["ela"].b])
            yield
            P.act(lambda e: e.activation(out=s["cd"].t, in_=pm.t[:, 64:96], func=AF.Exp), [pm.b], [s["cd"].b])
            yield
            P.dve(lambda e: e.tensor_tensor(out=s["d1"].t, in0=pm.t[:, 64:96], in1=s["la"].t, op=ALU.subtract), [pm.b, s["la"].b], [s["d1"].b])
            yield
            P.act(lambda e: e.activation(out=s["e1"].t, in_=s["d1"].t, func=AF.Exp), [s["d1"].b], [s["e1"].b])
            yield
            P.dve(lambda e: e.tensor_tensor(out=s["we"].t, in0=s["e1"].t, in1=s["dt"].t, op=ALU.mult), [s["e1"].b, s["dt"].b], [s["we"].b])
            yield
            yield

        def blk_gen(cb):
            pst = psPRE[cb % 4]
            o0 = 0
            for k in range(8):
                P.pe(lambda e, k=k: e.matmul(pst.t[:, o0:o0 + 132], lhsT=wxbc[:, k, cb * 128:(cb + 1) * 128], rhs=hT.t[:, k, :],
                                             start=(k == 0), stop=(k == 7)),
                     [bWlo, hT.b], [pst.b], acc=True)
            a_t = acc[cb % len(acc)]
            P.act(lambda e: e.activation(out=a_t.t, in_=pst.t[:, o0:o0 + 128], func=AF.Identity,
                                         scale=tp.t[:, 0, cb:cb + 1], bias=convb.t[:, cb:cb + 1]),
                  [pst.b, tp.b, convb.b], [a_t.b])
            yield
            for o in range(1, 5):
                P.dve(lambda e, o=o: e.scalar_tensor_tensor(out=a_t.t, in0=pst.t[:, o0 + o:o0 + o + 128], scalar=tp.t[:, o, cb:cb + 1],
                                                            in1=a_t.t, op0=ALU.mult, op1=ALU.add),
                      [pst.b, tp.b, a_t.b], [a_t.b])
                yield
            P.act(lambda e: e.activation(out=xbcT.t[:, cb, :], in_=a_t.t, func=AF.Silu), [a_t.b], [bxb[cb]])
            yield

        def tr_gen(c0, dst, db):
            for i in range(8):
                P.pe(lambda e, i=i: e.transpose(out=psT.t[:, i * 128:(i + 1) * 128], in_=xbcT.t[:, c0 + i, :], identity=ident_b.t[:]),
                     [bxb[c0 + i], ident_b.b], [psT.b], acc=True)
            yield
            P.dve(lambda e: e.tensor_copy(out=dst, in_=psT.t[:, :]), [psT.b], [db])
            yield

        tr_list = [(lambda: tr_gen(0, xs_tok.t[:, 0:1024], xs_tok.b)), (lambda: tr_gen(8, xs_tok.t[:, 1024:2048], xs_tok.b)),
                   (lambda: tr_gen(16, B_tok.t[:, :], B_tok.b))]
        run_interleaved([dt_gen] + list(extra_gens) + [(lambda cb=cb: blk_gen(cb)) for cb in range(nblk)] + (tr_list if full else tr_list[:2]),
                        max_active=5 + len(extra_gens))
        if not full:
            run_interleaved(tr_list[2:], max_active=1)
        if full and dbg and not dbg_done[0]:
            dbg_done[0] = True
            ob = Buf()
            dma(dbg_xbc[:, :, :], xbcT.t, bxb, [ob])
            used.append(ob)
            for i_, n_ in enumerate(("x1", "ab", "e", "l1", "dt", "dta", "la", "cd", "d1", "e1", "we")):
                ob = Buf()
                dma(dbg_sm[i_], s[n_].t, [s[n_].b], [ob])
                used.append(ob)
            ob = Buf()
            dma(dbg_hT[:, :, :], hT.t, [hT.b], [ob])
            used.append(ob)
        if full:
            P.pool(lambda e: e.tensor_tensor(out=xsdt.t.rearrange("p (h d) -> p h d", d=HD), in0=xs_tok.t.rearrange("p (h d) -> p h d", d=HD),
                                             in1=s["dt"].t.unsqueeze(2).broadcast_to([128, NH, HD]), op=ALU.mult),
                   [xs_tok.b, s["dt"].b], [xsdt.b])
        if full and p_sweep:
            P.dve(lambda e: e.tensor_tensor(out=xsD.t.rearrange("p (h d) -> p h d", d=HD), in0=xs_tok.t.rearrange("p (h d) -> p h d", d=HD),
                                            in1=dsk_bc.t[:].unsqueeze(2).broadcast_to([128, NH, HD]), op=ALU.mult),
                  [xs_tok.b, dsk_bc.b], [xsD.b])
        def group_gen(g):
            i2 = g % 2
            pL = psLs[i2]
            pY = psYs[i2]
            psc = pscs[i2]
            if full:
                if p_sweep and use_yS:
                    dma(yS_g[i2].t, ys_src[:, g * 256:(g + 1) * 256], [ys_buf], [yS_g[i2].b])
                P.pool(lambda e: e.tensor_tensor(out=dtaU[i2].t, in0=UP.t[:].unsqueeze(1).broadcast_to([128, 4, 128]),
                                                 in1=s["dta"].t[:, 4 * g:4 * g + 4].unsqueeze(2).broadcast_to([128, 4, 128]), op=ALU.mult),
                       [UP.b, s["dta"].b], [dtaU[i2].b])
                P.pe(lambda e: e.matmul(psc.t[:, 0:128], lhsT=xbcT.t[:, 16 + g, :], rhs=xbcT.t[:, 24 + g, :], start=True, stop=True),
                     [bxb[16 + g], bxb[24 + g]], [psc.b])
                P.pe(lambda e: e.matmul(psc.t[:, 128:384], lhsT=xbcT.t[:, 24 + g, :], rhs=state_b.t[:, g * 256:(g + 1) * 256], start=True, stop=True),
                     [bxb[24 + g], bstb[g]], [psc.b], acc=True)
                yield
                P.pe(lambda e: e.matmul(pL.t[:, :], lhsT=ones_f.t[:], rhs=dtaU[i2].t.rearrange("p h l -> p (h l)"), start=True, stop=False, skip_group_check=True),
                     [ones_f.b, dtaU[i2].b], [pL.b])
                P.pe(lambda e: e.matmul(pL.t[:, :], lhsT=ident_b.t[:], rhs=negP4b.t[:], start=False, stop=True, skip_group_check=True),
                     [ident_b.b, negP4b.b], [pL.b], acc=True)
                yield
                P.dve(lambda e: e.tensor_tensor(out=diff[i2].t, in0=pL.t[:, :].rearrange("p (h l) -> p h l", l=128),
                                                in1=s["la"].t[:, 4 * g:4 * g + 4].unsqueeze(2).broadcast_to([128, 4, 128]), op=ALU.subtract),
                      [pL.b, s["la"].b], [diff[i2].b])
                yield
                P.act(lambda e: e.activation(out=diff[i2].t, in_=diff[i2].t, func=AF.Exp), [diff[i2].b], [diff[i2].b])
                P.dve(lambda e: e.tensor_tensor(out=Zs[i2].t.rearrange("p (h d) -> p h d", d=HD), in0=psc.t[:, 128:384].rearrange("p (h d) -> p h d", d=HD),
                                                in1=s["ela"].t[:, 4 * g:4 * g + 4].unsqueeze(2).broadcast_to([128, 4, HD]), op=ALU.mult),
                      [psc.b, s["ela"].b], [Zs[i2].b])
                yield
                P.dve(lambda e: e.tensor_tensor(out=mixT[i2].t, in0=diff[i2].t, in1=psc.t[:, 0:128].unsqueeze(1).broadcast_to([128, 4, 128]), op=ALU.mult),
                      [diff[i2].b, psc.b], [mixT[i2].b])
                yield
                P.pe(lambda e: e.matmul(pY.t[:, 0:256], lhsT=ident_b.t[:], rhs=Zs[i2].t, start=True, stop=False, skip_group_check=True),
                     [ident_b.b, Zs[i2].b], [pY.b])
                if p_sweep:
                    P.pe(lambda e: e.matmul(pY.t[:, 0:256], lhsT=ident_b.t[:], rhs=xsD.t[:, g * 256:(g + 1) * 256], start=False, stop=False, skip_group_check=True),
                         [ident_b.b, xsD.b], [pY.b], acc=True)
                    if use_yS:
                        P.pe(lambda e: e.matmul(pY.t[:, 0:256], lhsT=J_b.t[:], rhs=yS_g[i2].t, start=False, stop=False, skip_group_check=True),
                             [J_b.b, yS_g[i2].b], [pY.b], acc=True)
                for h in range(4):
                    hh = 4 * g + h
                    P.pe(lambda e, h=h, hh=hh: e.matmul(pY.t[:, h * 64:(h + 1) * 64], lhsT=mixT[i2].t[:, h, :], rhs=xsdt.t[:, hh * 64:(hh + 1) * 64],
                                                       start=False, stop=True, skip_group_check=True),
                         [mixT[i2].b, xsdt.b], [pY.b], acc=True)
                yield
                yo = y_g[i2] if p_sweep else y_gb[i2]
                P.act(lambda e: e.copy(out=yo.t, in_=pY.t[:, 0:256]), [pY.b], [yo.b])
                dma(y_dst[:, g * 256:(g + 1) * 256], yo.t, [yo.b], [yd_buf])
                yield
            P.pool(lambda e: e.tensor_tensor(out=xsw[i2].t.rearrange("p (h d) -> p h d", d=HD), in0=xs_tok.t[:, g * 256:(g + 1) * 256].rearrange("p (h d) -> p h d", d=HD),
                                             in1=s["we"].t[:, 4 * g:4 * g + 4].unsqueeze(2).broadcast_to([128, 4, HD]), op=ALU.mult),
                   [xs_tok.b, s["we"].b], [xsw[i2].b])
            yield
            pct = PS[1]
            P.pe(lambda e: e.matmul(pct.t[:, i2 * 256:(i2 + 1) * 256], lhsT=B_tok.t[:, g * 128:(g + 1) * 128], rhs=xsw[i2].t, start=True, stop=True),
                 [B_tok.b, xsw[i2].b], [pct.b])
            yield
            sv = state.t[:, g * 256:(g + 1) * 256]
            P.dve(lambda e: e.tensor_tensor(out=sv.rearrange("p (h d) -> p h d", d=HD), in0=sv.rearrange("p (h d) -> p h d", d=HD),
                                            in1=s["cd"].t[:, 4 * g:4 * g + 4].unsqueeze(2).broadcast_to([128, 4, HD]), op=ALU.mult),
                  [bst[g], s["cd"].b], [bst[g]])
            yield
            P.dve(lambda e: e.tensor_tensor(out=sv, in0=pct.t[:, i2 * 256:(i2 + 1) * 256], in1=sv, op=ALU.add), [pct.b, bst[g]], [bst[g]])
            yield
            P.act(lambda e: e.copy(out=state_b.t[:, g * 256:(g + 1) * 256], in_=sv), [bst[g]], [bstb[g]])
            yield

        skew = 4 if full else 2
        active = []
        nxt = 0
        steps_of_last = 0
        while active or nxt < NG:
            if nxt < NG and len(active) < 2 and (not active or steps_of_last >= skew):
                active.append(group_gen(nxt))
                nxt += 1
                steps_of_last = 0
            for gen in list(active):
                try:
                    next(gen)
                except StopIteration:
                    active.remove(gen)
            steps_of_last += 1

    def reset_state():
        P.pool(lambda e: e.memset(state.t, 0.0), bst, bst)
        P.pool(lambda e: e.memset(state_b.t, 0.0), bstb, bstb)

    bYS = [Buf() for _ in range(32)]
    bYT = [Buf() for _ in range(32)]
    bBS = [Buf() for _ in range(32)]

    def run_sweep(chunks, j, p_sweep):
        make_hT(hTe[0], chunks[0][0], chunks[0][1], chunks[0][2], chunks[0][3], chunks[0][4], 0)
        for i, (src, t0, lo, hi, mj, full, oi) in enumerate(chunks):
            eg = []
            if i + 1 < len(chunks):
                n = chunks[i + 1]
                eg = [lambda n=n, i=i: make_hT_gen(hTe[(i + 1) % 2], n[0], n[1], n[2], n[3], n[4], (i + 1) % 2)]
            if full and p_sweep:
                cS = 31 - oi
                ssd_chunk(hTe[i % 2], j, True, True, ys_src=yS_d[cS], y_dst=yt_d[oi], ys_buf=bYS[cS], yd_buf=bYT[oi], extra_gens=eg)
                used.append(bYT[oi])
            elif full:
                ssd_chunk(hTe[i % 2], j, True, False, y_dst=yS_d[oi], yd_buf=bYS[oi], extra_gens=eg)
                used.append(bYS[oi])
            else:
                ssd_chunk(hTe[i % 2], j, False, extra_gens=eg)

    if "S" in sweeps:
        reset_state()
        ch = []
        for c in range(n_ctx):
            ch.append((cr, c * 128, c > 0, c < CTX // 128 - 1, 1, False, None))
        for c in range(32 - n_oth, 32):
            ch.append((xr, c * 128, c > 0, True, 0, False, None))
        for c in range(n_own):
            ch.append((xr, OWN + c * 128, True, (OWN + c * 128 + 128) < SEQ, 0, True, c))
        run_sweep(ch, 1, False)
    if "P" in sweeps:
        reset_state()
        ch = []
        for c in range(n_ctx):
            ch.append((cl, c * 128, c > 0, c < CTX // 128 - 1, 1, False, None))
        for c in range(n_own):
            ch.append((xl, c * 128, c > 0, True, 0, True, c))
        run_sweep(ch, 0, True)

    if "1" in sweeps:
        P.barrier(tiny, [PS[2].b])
        ar.off = AR_BASE
        wz = wload(32768, w_in_v[:, :, DT_END:Z_END], 8, DIN, bWhi)
        woss = wload(49152, w_oss.rearrange("(k p) n -> p k n", p=128), 16, D, bWhi, step=8)
        wglu = wload(0, w_in_v[:, :, Z_END:GLU_END], 8, 2 * D, bWlo)
        wgt = wload(16384, w_in_v[:, :, CG_END:IN_COLS], 8, 2 * D, bWlo)
        zs = ar.f32(128, DIN)
        y_in = ar.f32(128, DIN)
        ssg = ar.f32(128, 8)
        yn = ar.bf16(128, DIN)
        ynT = ar.bf16(128, 16 * 128)
        ynT.t = ynT.t.rearrange("p (k n) -> p k n", n=128)
        bs_sb = ar.f32(128, D)
        normw_bc = ar.f32(128, DIN)
        dma(normw_bc.t, ssm_nw.partition_broadcast(128), [], [normw_bc.b])

        def t1_gen(hT, c):
            for qd in range(4):
                pz = PS[3 + qd]
                for k in range(8):
                    P.pe(lambda e, k=k, qd=qd, pz=pz: e.matmul(pz.t[:, :], lhsT=hT.t[:, k, 2:130], rhs=wz[:, k, qd * 512:(qd + 1) * 512], start=(k == 0), stop=(k == 7)),
                         [hT.b, bWhi], [pz.b], acc=True)
                P.act(lambda e, qd=qd, pz=pz: e.activation(out=zs.t[:, qd * 512:(qd + 1) * 512], in_=pz.t[:, :], func=AF.Silu), [pz.b], [zs.b])
                yield
            dma(y_in.t, yt_d[c], [bYT[c]], [y_in.b])
            P.dve(lambda e: e.tensor_tensor(out=y_in.t, in0=y_in.t, in1=zs.t, op=ALU.mult), [y_in.b, zs.b], [y_in.b])
            yield
            P.pool(lambda e: e.memset(ssg.t, 0.0), [], [ssg.b])
            for g in range(NG):
                P.act(lambda e, g=g: e.activation(out=zs.t[:, g * 256:(g + 1) * 256], in_=y_in.t[:, g * 256:(g + 1) * 256], func=AF.Square, accum_out=ssg.t[:, g:g + 1]),
                      [y_in.b], [zs.b, ssg.b])
                yield
            P.dve(lambda e: e.tensor_scalar(out=ssg.t, in0=ssg.t, scalar1=1.0 / 256, scalar2=EPS, op0=ALU.mult, op1=ALU.add), [ssg.b], [ssg.b])
            P.act(lambda e: e.activation(out=ssg.t, in_=ssg.t, func=AF.Ln), [ssg.b], [ssg.b])
            P.act(lambda e: e.activation(out=ssg.t, in_=ssg.t, func=AF.Exp, scale=-0.5), [ssg.b], [ssg.b])
            yield
            for g in range(NG):
                yield
                P.dve(lambda e, g=g: e.scalar_tensor_tensor(out=yn.t[:, g * 256:(g + 1) * 256], in0=y_in.t[:, g * 256:(g + 1) * 256], scalar=ssg.t[:, g:g + 1],
                                                            in1=normw_bc.t[:, g * 256:(g + 1) * 256], op0=ALU.mult, op1=ALU.mult),
                      [y_in.b, ssg.b, normw_bc.b], [yn.b])
            for ps_ in range(2):
                for i in range(8):
                    kb = ps_ * 8 + i
                    P.pe(lambda e, kb=kb, i=i: e.transpose(out=psT.t[:, i * 128:(i + 1) * 128], in_=yn.t[:, kb * 128:(kb + 1) * 128], identity=ident_b.t[:]),
                         [yn.b, ident_b.b], [psT.b], acc=True)
                yield
                P.dve(lambda e, ps_=ps_: e.tensor_copy(out=ynT.t[:, ps_ * 8:(ps_ + 1) * 8, :].rearrange("p k n -> p (k n)"), in_=psT.t[:, :]), [psT.b], [ynT.b])
                yield
            for hf in range(2):
                po = PS[2] if hf == 0 else PS[3]
                for kb in range(16):
                    P.pe(lambda e, kb=kb, hf=hf, po=po: e.matmul(po.t[:, :], lhsT=ynT.t[:, kb, :], rhs=woss[:, kb, hf * 512:(hf + 1) * 512], start=(kb == 0), stop=(kb == 15)),
                         [ynT.b, bWhi], [po.b], acc=True)
                yield
                P.act(lambda e, hf=hf, po=po: e.copy(out=bs_sb.t[:, hf * 512:(hf + 1) * 512], in_=po.t[:, :]), [po.b], [bs_sb.b])
                yield
            dma(bs_d[c], bs_sb.t, [bs_sb.b], [bBS[c]])
            yield

        make_hT(hTe[0], xl, 0, False, False, 0, 0, halo=False)
        for c in range(n_own):
            gl = [lambda c=c: t1_gen(hTe[c % 2], c)]
            if c + 1 < n_own:
                gl.append(lambda c=c: make_hT_gen(hTe[(c + 1) % 2], xl, (c + 1) * 128, False, False, 0, 0, halo=False))
            run_interleaved(gl, max_active=2)
            if "2" not in sweeps:
                used.append(bBS[c])

    if "2" in sweeps:
        P.barrier(tiny, [PS[2].b])
        ar.off = AR_BASE
        wcg = wload(32768, w_in_v[:, :, GLU_END:CG_END], 8, D, bWhi)
        woc = wload(40960, w_oc.rearrange("(k p) n -> p k n", p=128), 8, D, bWhi)
        wo = wload(49152, w_o.rearrange("(k p) n -> p k n", p=128), 8, D, bWhi)
        sg = [ar.f32(128, 128) for _ in range(4)]
        bup = [Buf() for _ in range(8)]
        bca = [Buf() for _ in range(8)]
        bcg = [Buf() for _ in range(8)]
        u_pad = ar.f32(128, 8 * 2 * 94)
        u_pad.t = u_pad.t.rearrange("p (c r n) -> p c r n", r=2, n=94)
        cacc = ar.f32(128, 8 * 128)
        cacc.t = cacc.t.rearrange("p (c n) -> p c n", n=128)
        ut = ar.f32(128, D)
        st = ar.f32(128, 8)
        suT = ar.f32(128, 8 * 128)
        cgT = ar.f32(128, 8 * 128)
        vT = ar.bf16(128, 8 * 128)
        vT.t = vT.t.rearrange("p (c n) -> p c n", n=128)
        gs = ar.f32(128, D)
        bs_in = ar.f32(128, D)
        mrg = ar.bf16(128, D)
        mT = ar.bf16(128, 8 * 128)
        mT.t = mT.t.rearrange("p (c n) -> p c n", n=128)
        gate_bc = ar.f32(128, D)
        fnw_bc = ar.f32(128, D)
        cw = ar.f32(128, 31 * 8)
        cw.t = cw.t.rearrange("p (k c) -> p k c", c=8)
        cwb = ar.f32(128, 8)
        lnw_fm = ar.f32(128, 8)
        lnb_fm = ar.f32(128, 8)
        dma(gate_bc.t, gate_d[0].partition_broadcast(128), [bgate], [gate_bc.b])
        dma(fnw_bc.t, fnw.partition_broadcast(128), [], [fnw_bc.b])
        for k in range(31):
            dma(cw.t[:, k, :], cconv[k].rearrange("(cb p) -> p cb", p=128), [], [cw.b], slow=True)
        dma(cwb.t, cconv_b.rearrange("(cb p) -> p cb", p=128), [], [cwb.b], slow=True)
        dma(lnw_fm.t, ln_w.rearrange("(cb p) -> p cb", p=128), [], [lnw_fm.b], slow=True)
        dma(lnb_fm.t, ln_b.rearrange("(cb p) -> p cb", p=128), [], [lnb_fm.b], slow=True)
        P.pool(lambda e: e.memset(u_pad.t, 0.0), [], bup)
        NPE = 8
        u_pb = ar.bf16(128, 8 * 2 * 94)
        u_pb.t = u_pb.t.rearrange("p (c r n) -> p c r n", r=2, n=94)
        bupb = [Buf() for _ in range(8)]
        P.pool(lambda e: e.memset(u_pb.t, 0.0), [], bupb)
        dg = WA.t[:, 57344:57344 + NPE * 8 * 128].rearrange("p (k c n) -> p k c n", c=8, n=128)
        bdg = Buf()
        for k in range(NPE):
            for cb in range(8):
                if (k * 8 + cb) % 2 == 0:
                    P.act(lambda e, k=k, cb=cb: e.activation(out=dg[:, k, cb, :], in_=ident_b.t[:], func=AF.Identity, scale=cw.t[:, k, cb:cb + 1]),
                          [ident_b.b, cw.b], [bdg])
                else:
                    P.pool(lambda e, k=k, cb=cb: e.tensor_scalar(out=dg[:, k, cb, :], in0=ident_b.t[:], scalar1=cw.t[:, k, cb:cb + 1], scalar2=None, op0=ALU.mult),
                           [ident_b.b, cw.b], [bdg])

        def t2_chunk(hT, c, eg=()):
            def cblk_gen(cb):
                pg = PS[3 + cb % 4]
                for (o0, wv, c0) in ((0, wglu, cb * 128), (128, wglu, D + cb * 128), (256, wcg, cb * 128)):
                    wb = bWlo if wv is wglu else bWhi
                    for k in range(8):
                        P.pe(lambda e, k=k, o0=o0, wv=wv, c0=c0: e.matmul(pg.t[:, o0:o0 + 128], lhsT=wv[:, k, c0:c0 + 128], rhs=hT.t[:, k, 2:130], start=(k == 0), stop=(k == 7)),
                             [hT.b, wb], [pg.b], acc=True)
                sgt = sg[cb % 4]
                P.act(lambda e: e.activation(out=sgt.t, in_=pg.t[:, 128:256], func=AF.Sigmoid), [pg.b], [sgt.b])
                P.act(lambda e: e.activation(out=cgT.t[:, cb * 128:(cb + 1) * 128], in_=pg.t[:, 256:384], func=AF.Silu), [pg.b], [bcg[cb]])
                yield
                P.dve(lambda e: e.tensor_tensor(out=u_pad.t[:, cb, :, 15:79], in0=pg.t[:, 0:128].rearrange("p (r n) -> p r n", n=64),
                                                in1=sgt.t.rearrange("p (r n) -> p r n", n=64), op=ALU.mult),
                      [pg.b, sgt.b], [bup[cb]])
                yield
                P.act(lambda e: e.copy(out=u_pb.t[:, cb, :, 15:79], in_=u_pad.t[:, cb, :, 15:79]), [bup[cb]], [bupb[cb]])
                yield
                cv = cacc.t[:, cb, :].rearrange("p (r n) -> p r n", n=64)
                pcv = pg.t[:, 384:512].rearrange("p (r n) -> p r n", n=64)
                for k in range(NPE):
                    P.pe(lambda e, k=k: e.matmul(pcv, lhsT=dg[:, k, cb, :], rhs=u_pb.t[:, cb, :, k:k + 64], start=(k == 0), stop=(k == NPE - 1)),
                         [bdg, bupb[cb]], [pg.b], acc=True)
                yield
                P.act(lambda e: e.activation(out=cv, in_=pcv, func=AF.Identity, bias=cwb.t[:, cb:cb + 1]),
                      [pg.b, cwb.b], [bca[cb]])
                yield
                for k in range(NPE, 31):
                    P.dve(lambda e, k=k: e.scalar_tensor_tensor(out=cv, in0=u_pad.t[:, cb, :, k:k + 64], scalar=cw.t[:, k, cb:cb + 1], in1=cv, op0=ALU.mult, op1=ALU.add),
                          [bup[cb], cw.b, bca[cb]], [bca[cb]])
                    yield

            def gate1_gen():
                dma(bs_in.t, bs_d[c], [bBS[c]], [bs_in.b])
                for hf in range(2):
                    pq = PS[2]
                    for k in range(8):
                        P.pe(lambda e, k=k, hf=hf, pq=pq: e.matmul(pq.t[:, :], lhsT=hT.t[:, k, 2:130], rhs=wgt[:, k, hf * 512:(hf + 1) * 512],
                                                                   start=(k == 0), stop=(k == 7)),
                             [hT.b, bWlo], [pq.b], acc=True)
                    yield
                    P.act(lambda e, hf=hf, pq=pq: e.activation(out=gs.t[:, hf * 512:(hf + 1) * 512], in_=pq.t[:, :], func=AF.Sigmoid), [pq.b], [gs.b])
                    yield
                P.dve(lambda e: e.tensor_tensor(out=bs_in.t, in0=bs_in.t, in1=gs.t, op=ALU.mult), [bs_in.b, gs.b], [bs_in.b])
                yield

            run_interleaved(list(eg) + [gate1_gen] + [(lambda cb=cb: cblk_gen(cb)) for cb in range(8)], max_active=5 + len(eg))
            P.pool(lambda e: e.memset(st.t, 0.0), [], [st.b])
            for hf in range(2):
                pu = PS[hf]
                for i in range(4):
                    cb = hf * 4 + i
                    P.pe(lambda e, cb=cb, i=i, pu=pu: e.transpose(out=pu.t[:, i * 128:(i + 1) * 128], in_=cacc.t[:, cb, :], identity=ident_f.t[:]),
                         [bca[cb], ident_f.b], [pu.b], acc=True)
                P.act(lambda e, hf=hf, pu=pu: e.activation(out=ut.t[:, hf * 512:(hf + 1) * 512], in_=pu.t[:, :], func=AF.Identity, accum_out=st.t[:, hf:hf + 1]),
                      [pu.b], [ut.b, st.b])
            P.act(lambda e: e.activation(out=gs.t, in_=ut.t, func=AF.Square, accum_out=st.t[:, 2:3]), [ut.b], [gs.b, st.b])
            P.dve(lambda e: e.tensor_tensor(out=st.t[:, 3:4], in0=st.t[:, 0:1], in1=st.t[:, 1:2], op=ALU.add), [st.b], [st.b])
            P.dve(lambda e: e.tensor_scalar(out=st.t[:, 3:4], in0=st.t[:, 3:4], scalar1=1.0 / D, scalar2=None, op0=ALU.mult), [st.b], [st.b])
            P.dve(lambda e: e.tensor_tensor(out=st.t[:, 4:5], in0=st.t[:, 3:4], in1=st.t[:, 3:4], op=ALU.mult), [st.b], [st.b])
            P.dve(lambda e: e.scalar_tensor_tensor(out=st.t[:, 5:6], in0=st.t[:, 2:3], scalar=1.0 / D, in1=st.t[:, 4:5], op0=ALU.mult, op1=ALU.subtract), [st.b], [st.b])
            P.dve(lambda e: e.tensor_scalar(out=st.t[:, 5:6], in0=st.t[:, 5:6], scalar1=EPS, scalar2=None, op0=ALU.add), [st.b], [st.b])
            P.act(lambda e: e.activation(out=st.t[:, 5:6], in_=st.t[:, 5:6], func=AF.Ln), [st.b], [st.b])
            P.act(lambda e: e.activation(out=st.t[:, 5:6], in_=st.t[:, 5:6], func=AF.Exp, scale=-0.5), [st.b], [st.b])
            P.dve(lambda e: e.scalar_tensor_tensor(out=st.t[:, 6:7], in0=st.t[:, 3:4], scalar=-1.0, in1=st.t[:, 5:6], op0=ALU.mult, op1=ALU.mult), [st.b], [st.b])
            P.dve(lambda e: e.tensor_scalar(out=ut.t, in0=ut.t, scalar1=st.t[:, 5:6], scalar2=st.t[:, 6:7], op0=ALU.mult, op1=ALU.add), [ut.b, st.b], [ut.b])
            for hf in range(2):
                pb = PS[5 + hf]
                for i in range(4):
                    cb = hf * 4 + i
                    P.pe(lambda e, cb=cb, i=i, pb=pb: e.transpose(out=pb.t[:, i * 128:(i + 1) * 128], in_=ut.t[:, cb * 128:(cb + 1) * 128], identity=ident_f.t[:]),
                         [ut.b, ident_f.b], [pb.b], acc=True)
                for i in range(4):
                    cb = hf * 4 + i
                    P.act(lambda e, cb=cb, i=i, pb=pb: e.activation(out=suT.t[:, cb * 128:(cb + 1) * 128], in_=pb.t[:, i * 128:(i + 1) * 128], func=AF.Silu,
                                                                    scale=lnw_fm.t[:, cb:cb + 1], bias=lnb_fm.t[:, cb:cb + 1]),
                          [pb.b, lnw_fm.b, lnb_fm.b], [suT.b])
            P.dve(lambda e: e.tensor_tensor(out=vT.t.rearrange("p c n -> p (c n)"), in0=suT.t, in1=cgT.t, op=ALU.mult), [suT.b] + bcg, [vT.b])
            for hf in range(2):
                pc = PS[hf]
                for kb in range(8):
                    P.pe(lambda e, kb=kb, hf=hf, pc=pc: e.matmul(pc.t[:, :], lhsT=vT.t[:, kb, :], rhs=woc[:, kb, hf * 512:(hf + 1) * 512], start=(kb == 0), stop=(kb == 7)),
                         [vT.b, bWhi], [pc.b], acc=True)
            for gi in range(1, 2):
                for hf in range(2):
                    pq = PS[5 + hf]
                    for k in range(8):
                        P.pe(lambda e, k=k, gi=gi, hf=hf, pq=pq: e.matmul(pq.t[:, :], lhsT=hT.t[:, k, 2:130], rhs=wgt[:, k, gi * D + hf * 512:gi * D + (hf + 1) * 512],
                                                                         start=(k == 0), stop=(k == 7)),
                             [hT.b, bWlo], [pq.b], acc=True)
                    P.act(lambda e, hf=hf, pq=pq: e.activation(out=gs.t[:, hf * 512:(hf + 1) * 512], in_=pq.t[:, :], func=AF.Sigmoid), [pq.b], [gs.b])
                if gi == 0:
                    P.dve(lambda e: e.tensor_tensor(out=bs_in.t, in0=bs_in.t, in1=gs.t, op=ALU.mult), [bs_in.b, gs.b], [bs_in.b])
                else:
                    for hf in range(2):
                        P.dve(lambda e, hf=hf: e.tensor_tensor(out=gs.t[:, hf * 512:(hf + 1) * 512], in0=PS[hf].t[:, :], in1=gs.t[:, hf * 512:(hf + 1) * 512], op=ALU.mult),
                              [PS[hf].b, gs.b], [gs.b])
                    P.dve(lambda e: e.tensor_tensor(out=mrg.t, in0=bs_in.t, in1=gs.t, op=ALU.add), [bs_in.b, gs.b], [mrg.b])
            for i in range(8):
                P.pe(lambda e, i=i: e.transpose(out=psT.t[:, i * 128:(i + 1) * 128], in_=mrg.t[:, i * 128:(i + 1) * 128], identity=ident_b.t[:]),
                     [mrg.b, ident_b.b], [psT.b], acc=True)
            P.dve(lambda e: e.tensor_copy(out=mT.t.rearrange("p c n -> p (c n)"), in_=psT.t[:, :]), [psT.b], [mT.b])
            dma(suT.t, xl[c * 128:(c + 1) * 128, :], [], [suT.b])
            for hf in range(2):
                po = PS[3 + hf]
                for kb in range(8):
                    P.pe(lambda e, kb=kb, hf=hf, po=po: e.matmul(po.t[:, :], lhsT=mT.t[:, kb, :], rhs=wo[:, kb, hf * 512:(hf + 1) * 512], start=(kb == 0), stop=(kb == 7)),
                         [mT.b, bWhi], [po.b], acc=True)
                P.dve(lambda e, hf=hf, po=po: e.tensor_tensor(out=ut.t[:, hf * 512:(hf + 1) * 512], in0=po.t[:, :], in1=gate_bc.t[:, hf * 512:(hf + 1) * 512], op=ALU.mult),
                      [po.b, gate_bc.b], [ut.b])
            P.dve(lambda e: e.tensor_tensor(out=ut.t, in0=ut.t, in1=suT.t, op=ALU.add), [ut.b, suT.b], [ut.b])
            P.pool(lambda e: e.memset(st.t[:, 7:8], 0.0), [], [st.b])
            P.act(lambda e: e.activation(out=gs.t, in_=ut.t, func=AF.Square, accum_out=st.t[:, 7:8]), [ut.b], [gs.b, st.b])
            P.dve(lambda e: e.tensor_scalar(out=st.t[:, 7:8], in0=st.t[:, 7:8], scalar1=1.0 / D, scalar2=EPS, op0=ALU.mult, op1=ALU.add), [st.b], [st.b])
            P.act(lambda e: e.activation(out=st.t[:, 7:8], in_=st.t[:, 7:8], func=AF.Ln), [st.b], [st.b])
            P.act(lambda e: e.activation(out=st.t[:, 7:8], in_=st.t[:, 7:8], func=AF.Exp, scale=-0.5), [st.b], [st.b])
            P.dve(lambda e: e.scalar_tensor_tensor(out=ut.t, in0=ut.t, scalar=st.t[:, 7:8], in1=fnw_bc.t, op0=ALU.mult, op1=ALU.mult), [ut.b, st.b, fnw_bc.b], [ut.b])
            ob = Buf()
            dma(out[c * 128:(c + 1) * 128, :], ut.t, [ut.b], [ob])
            used.append(ob)

        make_hT(hTe[0], xl, 0, False, False, 0, 0, halo=False)
        for c in range(n_own):
            eg = []
            if c + 1 < n_own:
                eg = [lambda c=c: make_hT_gen(hTe[(c + 1) % 2], xl, (c + 1) * 128, False, False, 0, 0, halo=False)]
            t2_chunk(hTe[c % 2], c, eg)

    nc = P.finalize(used)
    return P, nc


def _consts():
    c = np.zeros((7, 128, 128), np.float32)
    i = np.arange(128)
    c[0] = np.eye(128)
    c[1] = (i[:, None] <= i[None, :]).astype(np.float32)
    c[2] = np.where(i[None, :] >= i[:, None], 0.0, -30000.0)
    c[3] = 1.0
    c[4] = np.eye(128)[::-1]
    c[5, 0, :] = 1.0
    c[6, 1, :] = 1.0
    return c


def make_in_maps(inp):
    f = lambda a: np.ascontiguousarray(np.asarray(a, dtype=np.float32))
    x = np.asarray(inp["x"], np.float32)
    ctx = np.asarray(inp["ctx"], np.float32)
    c = np.asarray(inp["c"], np.float32)
    w_in = f(inp["w_in"][0])
    cw = np.asarray(inp["ssm_conv_w"][0], np.float32)
    z = np.zeros((1, C_END), np.float32)
    taps_nat = np.concatenate([cw, z], 0)
    taps_rev = np.concatenate([z, cw[::-1]], 0)
    consts = _consts()
    maps = []
    for core in range(8):
        b, half = core // 2, core % 2
        xb = x[b]
        cb = ctx[b]
        if half == 0:
            xl, xr, cl, cr = xb, xb[::-1], cb, cb[::-1]
            tP, tS = taps_nat, taps_rev
            dP, dS = 0, 1
            cc = np.asarray(inp["conf_conv_w"][0], np.float32)
        else:
            xl, xr, cl, cr = xb[::-1], xb, cb[::-1], cb
            tP, tS = taps_rev, taps_nat
            dP, dS = 1, 0
            cc = np.asarray(inp["conf_conv_w"][0], np.float32)[::-1]
        wdt = np.stack([w_in[:, C_END + 32 * dP:C_END + 32 * dP + 32], w_in[:, C_END + 32 * dS:C_END + 32 * dS + 32]], 0)
        m = {
            "xl": f(xl), "xr": f(xr), "cl": f(cl), "cr": f(cr),
            "cvec": f(np.stack([c[b], np.asarray(inp["c_ctx"], np.float32)], 0)),
            "w_mod": f(inp["w_mod"][0]),
            "b_mod2": f(np.stack([inp["b_mod"][0]] * 2, 0)),
            "norm_w2": f(np.stack([inp["norm_w"][0]] * 2, 0)),
            "w_in": w_in,
            "w_dt": f(wdt),
            "taps": f(np.stack([tP, tS], 0)),
            "conv_b": f(inp["ssm_conv_b"][0]),
            "dtb": f(np.stack([inp["dt_bias"][0][dP], inp["dt_bias"][0][dS]], 0)),
            "alog": f(np.stack([inp["a_log"][0][dP], inp["a_log"][0][dS]], 0)),
            "dskip": f(inp["d_skip"][0]),
            "ssm_nw": f(inp["ssm_norm_w"][0]),
            "w_oss": f(inp["w_out_ssm"][0]),
            "cconv": f(cc),
            "cconv_b": f(inp["conf_conv_b"][0]),
            "ln_w": f(inp["conf_ln_w"][0]),
            "ln_b": f(inp["conf_ln_b"][0]),
            "w_oc": f(inp["w_out_conf"][0]),
            "w_o": f(inp["w_out"][0]),
            "fnw": f(inp["final_norm_w"]),
            "consts": consts,
        }
        maps.append(m)
    return maps


def kernel(**inp):
    P, nc = build_program()
    maps = make_in_maps(inp)
    res = run_bass_kernel_spmd(nc, maps, core_ids=list(range(8)))
    outp = np.empty((4, SEQ, D), np.float32)
    for core in range(8):
        b, half = core // 2, core % 2
        o = res.results[core]["out"]
        if half == 0:
            outp[b, :OWN] = o
        else:
            outp[b, OWN:] = o[::-1]
    return outp
```

```python
import numpy as np
from contextlib import ExitStack
import concourse.bass as bass
import concourse.mybir as mybir
from concourse.bass_utils import run_bass_kernel_spmd

F32 = mybir.dt.float32
BF16 = mybir.dt.bfloat16
AF = mybir.ActivationFunctionType
ALU = mybir.AluOpType

ENGINES = ("pe", "act", "dve", "pool", "sp")
N_DMA_SEMS = 8
SEM_EPOCH = 8192

D = 1024
SEQ = 8192
OWN = 4096
CTX = 256
DIN = 2048
NH = 32
HD = 64
NG = 8
DS = 128
C_END = 4096
DT_END = C_END + 64
Z_END = DT_END + DIN
GLU_END = Z_END + 2 * D
CG_END = GLU_END + D
IN_COLS = CG_END + 2 * D
EPS = 1e-6


class Buf:
    __slots__ = ("last_w", "readers", "excl")

    def __init__(self, excl=False):
        self.last_w = None
        self.readers = {}
        self.excl = excl


class Op:
    __slots__ = ("eng", "fn", "deps", "signal", "semkey", "value", "is_dma", "implied", "waits")

    def __init__(self, eng, fn, is_dma=False):
        self.eng = eng
        self.fn = fn
        self.deps = []
        self.signal = is_dma
        self.semkey = None
        self.value = None
        self.is_dma = is_dma
        self.implied = None
        self.waits = None


class T:
    __slots__ = ("t", "b")

    def __init__(self, t):
        self.t = t
        self.b = Buf()


class Prog:
    def __init__(self):
        self.nc = bass.Bass("TRN2", target_bir_lowering=False)
        self.stack = ExitStack()
        self.order = []
        self.ndma = 0
        self.dmas = []

    def dram(self, name, shape, dtype, kind="Internal"):
        return self.nc.dram_tensor(name, list(shape), dtype, kind=kind).ap()

    def sb(self, name, shape, dtype=F32):
        return T(self.stack.enter_context(self.nc.sbuf_tensor(name, list(shape), dtype)))

    def ps(self, name, shape, dtype=F32):
        t = T(self.stack.enter_context(self.nc.psum_tensor(name, list(shape), dtype)))
        t.b.excl = True
        return t

    def _add(self, eng, fn, reads, writes, is_dma=False, pe_acc=False):
        op = Op(eng, fn, is_dma)
        deps = []
        if any(b.excl for b in reads):
            writes = list(writes) + [b for b in reads if b.excl and b not in writes]
            reads = [b for b in reads if not b.excl]
        for b in reads:
            if b.last_w is not None:
                deps.append(b.last_w)
        for b in writes:
            if b.last_w is not None:
                if not (pe_acc and b.last_w.eng == "pe" and not b.last_w.is_dma and not b.readers):
                    deps.append(b.last_w)
            deps.extend(b.readers.values())
        seen = set()
        for d in deps:
            if id(d) not in seen and d is not op:
                seen.add(id(d))
                d.signal = True
                op.deps.append(d)
        for b in reads:
            if is_dma:
                self.ndma += 1
                b.readers[("dma", self.ndma)] = op
            else:
                b.readers[eng] = op
        for b in writes:
            b.last_w = op
            b.readers = {}
        self.order.append(op)
        return op

    def pe(self, fn, reads, writes, acc=False):
        return self._add("pe", fn, reads, writes, pe_acc=acc)

    def act(self, fn, reads, writes):
        return self._add("act", fn, reads, writes)

    def dve(self, fn, reads, writes):
        return self._add("dve", fn, reads, writes)

    def pool(self, fn, reads, writes):
        return self._add("pool", fn, reads, writes)

    def dma(self, q, fn, reads, writes):
        op = self._add(q, fn, reads, writes, is_dma=True)
        self.dmas.append(op)
        return op

    def barrier(self, tiny, pe_bufs=()):
        xs = []
        for e in ("pe", "act", "dve", "pool"):
            op = self._add(e, tiny[e], [], list(pe_bufs) if e == "pe" else [])
            op.signal = True
            xs.append(op)
        dm = list(self.dmas)
        self.dmas = []
        for e in ("pe", "act", "dve", "pool", "sp"):
            op = self._add(e, tiny[e] if e != "sp" else None, [], list(pe_bufs) if e == "pe" else [])
            for d in xs + dm:
                d.signal = True
                op.deps.append(d)

    def finalize(self, final_bufs):
        nc = self.nc
        st = self.stack
        self._add("sp", None, final_bufs, [])
        cnt = {e: 0 for e in ENGINES}
        nd = {e: 0 for e in ENGINES}
        dcount = {}
        prev_on = {}
        for op in self.order:
            e = op.eng
            if op.is_dma:
                s = nd[e] % N_DMA_SEMS
                nd[e] += 1
                key = ("d", e, s)
                dcount[key] = dcount.get(key, 0) + 16
                op.semkey = key
                op.value = dcount[key]
                if key in prev_on:
                    op.deps.append(prev_on[key])
                prev_on[key] = op
            elif op.signal:
                op.semkey = ("c", e, cnt[e] // SEM_EPOCH)
                op.value = cnt[e] % SEM_EPOCH + 1
                cnt[e] += 1
        known = {e: {} for e in ENGINES}
        nw = 0
        for op in self.order:
            kn = known[op.eng]
            need = {}
            for d in op.deps:
                if kn.get(d.semkey, 0) >= d.value:
                    continue
                if d.semkey not in need or need[d.semkey].value < d.value:
                    need[d.semkey] = d
            waits = []
            for k, d in need.items():
                if kn.get(k, 0) >= d.value:
                    continue
                waits.append((k, d.value))
                kn[k] = d.value
                if d.implied:
                    for kk, vv in d.implied.items():
                        if kn.get(kk, 0) < vv:
                            kn[kk] = vv
            op.waits = waits
            nw += len(waits)
            if op.signal:
                op.implied = dict(kn)
        self.n_waits = nw
        self.counts = cnt
        sems = {}
        for e in ENGINES:
            for ep in range(cnt[e] // SEM_EPOCH + 1):
                sems[("c", e, ep)] = st.enter_context(nc.semaphore("c_%s%d" % (e, ep)))
        for e in ("sp", "act", "pool"):
            for i in range(N_DMA_SEMS):
                sems[("d", e, i)] = st.enter_context(nc.semaphore("d_%s%d" % (e, i)))
        per = {e: [op for op in self.order if op.eng == e] for e in ENGINES}
        block = st.enter_context(nc.Block())

        def run(e, eng):
            for op in per[e]:
                for (k, v) in op.waits:
                    eng.wait_ge(sems[k], v)
                if op.fn is None:
                    continue
                ins = op.fn(eng)
                if op.signal:
                    ins.then_inc(sems[op.semkey], 16 if op.is_dma else 1)

        @block.tensor
        def _(eng):
            run("pe", eng)

        @block.vector
        def _(eng):
            run("dve", eng)

        @block.scalar
        def _(eng):
            run("act", eng)

        @block.gpsimd
        def _(eng):
            run("pool", eng)

        @block.sync
        def _(eng):
            run("sp", eng)

        return nc


class V:
    __slots__ = ("t", "b")

    def __init__(self, ap, b=None):
        self.t = ap
        self.b = b if b is not None else Buf()


class Arena:
    def __init__(self, tile_f32, n):
        self.tile = tile_f32
        self.n = n
        self.off = 0

    def f32(self, p, cols):
        o = self.off
        self.off += cols
        assert self.off <= self.n, ("arena overflow", self.off, self.n)
        return V(self.tile[0:p, o:o + cols])

    def bf16(self, p, cols):
        w = (cols + 1) // 2
        o = self.off
        self.off += w
        assert self.off <= self.n, ("arena overflow", self.off, self.n)
        return V(self.tile[0:p, o:o + w].bitcast(BF16)[:, 0:cols])


def run_interleaved(gen_fns, max_active, start_every=1):
    active = []
    nxt = 0
    rounds = 0
    while active or nxt < len(gen_fns):
        if nxt < len(gen_fns) and len(active) < max_active and rounds % start_every == 0:
            active.append(gen_fns[nxt]())
            nxt += 1
        for gen in list(active):
            try:
                next(gen)
            except StopIteration:
                active.remove(gen)
        rounds += 1


def build_program(n_own=32, n_oth=32, n_ctx=2, sweeps="SP12", use_yS=True, dbg=False):
    P = Prog()
    nc = P.nc
    EI = "ExternalInput"
    xl = P.dram("xl", [SEQ, D], F32, EI)
    xr = P.dram("xr", [SEQ, D], F32, EI)
    cl = P.dram("cl", [CTX, D], F32, EI)
    cr = P.dram("cr", [CTX, D], F32, EI)
    cvec = P.dram("cvec", [2, D], F32, EI)
    w_mod = P.dram("w_mod", [D, 3 * D], F32, EI)
    b_mod2 = P.dram("b_mod2", [2, 3 * D], F32, EI)
    norm_w2 = P.dram("norm_w2", [2, D], F32, EI)
    w_in = P.dram("w_in", [D, IN_COLS], F32, EI)
    w_dt = P.dram("w_dt", [2, D, NH], F32, EI)
    taps = P.dram("taps", [2, 5, C_END], F32, EI)
    conv_b = P.dram("conv_b", [C_END], F32, EI)
    dtb = P.dram("dtb", [2, NH], F32, EI)
    alog = P.dram("alog", [2, NH], F32, EI)
    dskip = P.dram("dskip", [NH], F32, EI)
    ssm_nw = P.dram("ssm_nw", [DIN], F32, EI)
    w_oss = P.dram("w_oss", [DIN, D], F32, EI)
    cconv = P.dram("cconv", [31, D], F32, EI)
    cconv_b = P.dram("cconv_b", [D], F32, EI)
    ln_w = P.dram("ln_w", [D], F32, EI)
    ln_b = P.dram("ln_b", [D], F32, EI)
    w_oc = P.dram("w_oc", [D, D], F32, EI)
    w_o = P.dram("w_o", [D, D], F32, EI)
    fnw = P.dram("fnw", [D], F32, EI)
    consts = P.dram("consts", [7, 128, 128], F32, EI)
    out = P.dram("out", [OWN, D], F32, "ExternalOutput")
    dk = "ExternalOutput" if dbg else "Internal"
    yS_d = P.dram("yS_d", [32, 128, DIN], BF16, dk)
    yt_d = P.dram("yt_d", [32, 128, DIN], F32, dk)
    bs_d = P.dram("bs_d", [32, 128, D], F32, dk)
    if dbg:
        dbg_xbc = P.dram("dbg_xbc", [128, 32, 128], BF16, "ExternalOutput")
        dbg_sm = P.dram("dbg_sm", [11, 128, NH], F32, "ExternalOutput")
        dbg_hT = P.dram("dbg_hT", [128, 8, 132], BF16, "ExternalOutput")
    dbg_done = [False]
    used = []

    def dma(out_ap, in_ap, reads, writes, queue="sp", slow=False):
        if slow:
            return P.dma(queue, lambda e: e.dma_start(out=out_ap, in_=in_ap, allow_slow_non_contiguous=True), reads, writes)
        return P.dma(queue, lambda e: e.dma_start(out=out_ap, in_=in_ap), reads, writes)

    ident_f = P.sb("ident_f", [128, 128])
    UP = P.sb("UP", [128, 128])
    negP = P.sb("negP", [128, 128])
    ones_f = P.sb("ones_f", [128, 128])
    J_f = P.sb("J_f", [128, 128])
    sel0 = P.sb("sel0", [2, 128])
    ident_b = P.sb("ident_b", [128, 128], BF16)
    dummy = P.sb("dummy_t", [128, 8])
    for i, tl in enumerate([ident_f, UP, negP, ones_f, J_f]):
        dma(tl.t[:], consts[i, :, :], [], [tl.b])
    dma(sel0.t[:], consts[5, 0:2, :], [], [sel0.b])
    P.dve(lambda e: e.tensor_copy(out=ident_b.t[:], in_=ident_f.t[:]), [ident_f.b], [ident_b.b])
    J_b = P.sb("J_b", [128, 128], BF16)
    P.dve(lambda e: e.tensor_copy(out=J_b.t[:], in_=J_f.t[:]), [J_f.b], [J_b.b])
    negP4b = P.sb("negP4b", [128, 512], BF16)
    P.dve(lambda e: e.tensor_copy(out=negP4b.t[:].rearrange("p (h l) -> p h l", l=128), in_=negP.t[:].unsqueeze(1).broadcast_to([128, 4, 128])), [negP.b], [negP4b.b])

    PS = [P.ps("ps%d" % i, [128, 512]) for i in range(7)]
    psT = P.ps("psT", [128, 1024], BF16)

    tiny = {
        "pe": lambda e: e.matmul(PS[2].t[0:2, 0:2], lhsT=ident_f.t[0:2, 0:2], rhs=ident_f.t[0:2, 0:2], start=True, stop=True),
        "act": lambda e: e.copy(out=dummy.t[0:1, 0:1], in_=ident_f.t[0:1, 0:1]),
        "dve": lambda e: e.memset(dummy.t[0:1, 2:3], 0.0),
        "pool": lambda e: e.memset(dummy.t[0:1, 4:5], 0.0),
    }

    WA = P.sb("WA", [128, 65536], BF16)
    bWlo = Buf()
    bWhi = Buf()
    WKN = 17408
    WK = P.sb("WK", [128, WKN])
    ar = Arena(WK.t, WKN)

    def wload(col0, src_ap, kb, n, buf, step=4):
        view = WA.t[:, col0:col0 + kb * n].rearrange("p (k n) -> p k n", n=n)
        for k0 in range(0, kb, step):
            dma(view[:, k0:k0 + step, :], src_ap[:, k0:k0 + step, :], [], [buf], queue="pool")
        return view

    scT = P.sb("scT", [128, 8, 2])
    for j in range(2):
        cTj = P.sb("cT%d" % j, [128, 8])
        dma(cTj.t[:], cvec[j].rearrange("(k p) -> p k", p=128), [], [cTj.b], slow=True)
        P.act(lambda e, j=j, cTj=cTj: e.activation(out=scT.t[:, :, j], in_=cTj.t[:], func=AF.Silu), [cTj.b], [scT.b])
    wm = ar.f32(128, 8192)
    wm3 = wm.t.rearrange("p (k n) -> p k n", n=1024)
    modrow = ar.f32(2, 3 * D)
    bmod = ar.f32(2, 3 * D)
    nw2 = ar.f32(2, D)
    Arow = ar.f32(2, D)
    dma(bmod.t, b_mod2[:, :], [], [bmod.b])
    dma(nw2.t, norm_w2[:, :], [], [nw2.b])
    for piece in range(3):
        dma(wm3, w_mod.rearrange("(k p) n -> p k n", p=128)[:, :, piece * 1024:(piece + 1) * 1024], [], [wm.b])
        for hf in range(2):
            pst = PS[hf]
            for k in range(8):
                P.pe(lambda e, k=k, hf=hf, pst=pst: e.matmul(pst.t[0:2, :], lhsT=scT.t[:, k, :], rhs=wm3[:, k, hf * 512:(hf + 1) * 512],
                                                             start=(k == 0), stop=(k == 7)),
                     [scT.b, wm.b], [pst.b], acc=True)
            c0 = piece * 1024 + hf * 512
            P.dve(lambda e, pst=pst, c0=c0: e.tensor_tensor(out=modrow.t[:, c0:c0 + 512], in0=pst.t[0:2, :], in1=bmod.t[:, c0:c0 + 512], op=ALU.add),
                  [pst.b, bmod.b], [modrow.b])
    P.dve(lambda e: e.scalar_tensor_tensor(out=Arow.t, in0=modrow.t[:, D:2 * D], scalar=1.0, in1=nw2.t, op0=ALU.add, op1=ALU.mult),
          [modrow.b, nw2.b], [Arow.b])
    Afm = [P.sb("Afm%d" % j, [128, 8]) for j in range(2)]
    Sfm = [P.sb("Sfm%d" % j, [128, 8]) for j in range(2)]
    for (src, dst) in ((Arow, Afm), (modrow, Sfm)):
        for k in range(8):
            P.pe(lambda e, src=src, k=k: e.transpose(out=PS[2].t[:, 2 * k:2 * k + 2], in_=src.t[0:2, k * 128:(k + 1) * 128], identity=ident_f.t[0:2, 0:2]),
                 [src.b, ident_f.b], [PS[2].b], acc=True)
        for j in range(2):
            P.dve(lambda e, j=j, dst=dst: e.tensor_copy(out=dst[j].t[:], in_=PS[2].t[:, 0:16].rearrange("p (k j) -> p k j", j=2)[:, :, j]),
                  [PS[2].b], [dst[j].b])
    gate_d = P.dram("gate_d", [1, D], F32)
    bgate = Buf()
    dma(gate_d[:, :], modrow.t[0:1, 2 * D:3 * D], [modrow.b], [bgate])

    def bc_load(name, vec_ap, n):
        tl = P.sb(name, [128, n])
        dma(tl.t[:], vec_ap.partition_broadcast(128), [], [tl.b])
        return tl

    a_bc = []
    dtb_bc = []
    for j in range(2):
        al = bc_load("alog%d" % j, alog[j, :], NH)
        a = P.sb("a_bc%d" % j, [128, NH])
        P.act(lambda e, al=al, a=a: e.activation(out=a.t[:], in_=al.t[:], func=AF.Exp), [al.b], [a.b])
        P.dve(lambda e, a=a: e.tensor_scalar(out=a.t[:], in0=a.t[:], scalar1=-1.0, scalar2=None, op0=ALU.mult), [a.b], [a.b])
        a_bc.append(a)
        dtb_bc.append(bc_load("dtb%d" % j, dtb[j, :], NH))
    dsk_bc = bc_load("dsk", dskip, NH)
    tapsT = []
    for j in range(2):
        tl = P.sb("taps%d" % j, [128, 5, 32])
        for o in range(5):
            dma(tl.t[:, o, :], taps[j, o].rearrange("(cb p) -> p cb", p=128), [], [tl.b], slow=True)
        tapsT.append(tl)
    convb = P.sb("convb", [128, 32])
    dma(convb.t[:], conv_b.rearrange("(cb p) -> p cb", p=128), [], [convb.b], slow=True)
    wdt = []
    for j in range(2):
        tl = P.sb("wdt%d" % j, [128, 8, NH], BF16)
        dma(tl.t[:], w_dt[j].rearrange("(k p) n -> p k n", p=128), [], [tl.b], queue="pool")
        wdt.append(tl)

    w_in_v = w_in.rearrange("(k p) n -> p k n", p=128)
    wxbc = wload(0, w_in_v[:, :, 0:C_END], 8, C_END, bWlo)

    P.barrier(tiny, [PS[2].b])
    ar.off = 0
    xt0 = ar.f32(128, D)
    xt = [xt0, xt0]
    xn = ar.f32(128, D)
    xh = ar.f32(4, D)
    xhn = xh
    ss = ar.f32(128, 2)
    hTe = [ar.bf16(128, 8 * 132) for i in range(2)]
    for h_ in hTe:
        h_.t = h_.t.rearrange("p (k n) -> p k n", n=132)
    AR_BASE = ar.off

    def make_hT_gen(dst, src, t0, lo_ok, hi_ok, j, slot, halo=True):
        x_t = xt[slot]
        dma(x_t.t, src[t0:t0 + 128, :], [], [x_t.b])
        P.pool(lambda e: e.memset(ss.t, 0.0), [], [ss.b])
        P.act(lambda e: e.activation(out=xn.t, in_=x_t.t, func=AF.Square, accum_out=ss.t[:, 0:1]), [x_t.b], [xn.b, ss.b])
        if halo:
            P.pool(lambda e: e.memset(xh.t, 0.0), [], [xh.b])
            if lo_ok:
                dma(xh.t[0:2, :], src[t0 - 2:t0, :], [], [xh.b])
            if hi_ok:
                dma(xh.t[2:4, :], src[t0 + 128:t0 + 130, :], [], [xh.b])
            P.act(lambda e: e.activation(out=xn.t[0:4, :], in_=xh.t, func=AF.Square, accum_out=ss.t[0:4, 1:2]), [xh.b], [xn.b, ss.b])
        yield
        P.dve(lambda e: e.tensor_scalar(out=ss.t, in0=ss.t, scalar1=1.0 / D, scalar2=EPS, op0=ALU.mult, op1=ALU.add), [ss.b], [ss.b])
        yield
        P.act(lambda e: e.activation(out=ss.t, in_=ss.t, func=AF.Ln), [ss.b], [ss.b])
        yield
        P.act(lambda e: e.activation(out=ss.t, in_=ss.t, func=AF.Exp, scale=-0.5), [ss.b], [ss.b])
        yield
        P.dve(lambda e: e.tensor_scalar(out=xn.t, in0=x_t.t, scalar1=ss.t[:, 0:1], scalar2=None, op0=ALU.mult), [x_t.b, ss.b], [xn.b])
        yield
        for hf in range(2):
            pst = PS[hf]
            yield
            for kk in range(4):
                k = hf * 4 + kk
                P.pe(lambda e, k=k, kk=kk, pst=pst: e.transpose(out=pst.t[:, kk * 128:(kk + 1) * 128], in_=xn.t[:, k * 128:(k + 1) * 128], identity=ident_f.t[:]),
                     [xn.b, ident_f.b], [pst.b], acc=True)
            for kk in range(4):
                k = hf * 4 + kk
                P.act(lambda e, k=k, kk=kk, pst=pst: e.activation(out=dst.t[:, k, 2:130], in_=pst.t[:, kk * 128:(kk + 1) * 128], func=AF.Identity,
                                                                  scale=Afm[j].t[:, k:k + 1], bias=Sfm[j].t[:, k:k + 1]),
                      [pst.b, Afm[j].b, Sfm[j].b], [dst.b])
        yield
        if halo:
            P.dve(lambda e: e.tensor_scalar(out=xhn.t, in0=xh.t, scalar1=ss.t[0:4, 1:2], scalar2=None, op0=ALU.mult), [xh.b, ss.b], [xhn.b])
            pst = PS[2]
            for k in range(8):
                P.pe(lambda e, k=k: e.transpose(out=pst.t[:, 128 + k * 4:128 + (k + 1) * 4], in_=xhn.t[0:4, k * 128:(k + 1) * 128], identity=ident_f.t[0:4, 0:4]),
                     [xhn.b, ident_f.b], [pst.b], acc=True)
            yield
            for k in range(8):
                yield
                P.act(lambda e, k=k: e.activation(out=dst.t[:, k, 0:2], in_=pst.t[:, 128 + k * 4:128 + k * 4 + 2], func=AF.Identity,
                                                  scale=Afm[j].t[:, k:k + 1], bias=Sfm[j].t[:, k:k + 1]),
                      [pst.b, Afm[j].b, Sfm[j].b], [dst.b])
                P.act(lambda e, k=k: e.activation(out=dst.t[:, k, 130:132], in_=pst.t[:, 128 + k * 4 + 2:128 + k * 4 + 4], func=AF.Identity,
                                                  scale=Afm[j].t[:, k:k + 1], bias=Sfm[j].t[:, k:k + 1]),
                      [pst.b, Afm[j].b, Sfm[j].b], [dst.b])
            if not lo_ok:
                P.dve(lambda e: e.memset(dst.t[:, :, 0:2], 0.0), [], [dst.b])
            if not hi_ok:
                P.dve(lambda e: e.memset(dst.t[:, :, 130:132], 0.0), [], [dst.b])

    def make_hT(*a_, **k_):
        for _ in make_hT_gen(*a_, **k_):
            pass

    ar2 = Arena(WA.t[:, 32768:65536].bitcast(F32), 16384)
    acc = [ar.f32(128, 128) for i in range(8)]
    xbcT = ar2.bf16(128, 32 * 128)
    xbcT.t = xbcT.t.rearrange("p (c n) -> p c n", n=128)
    bxb = [Buf() for _ in range(32)]
    xs_tok = ar2.bf16(128, DIN)
    B_tok = ar2.bf16(128, 1024)
    xsD = ar2.bf16(128, DIN)
    sm = {n: ar.f32(128, NH) for n in ("x1", "ab", "e", "l1", "dt", "dta", "la", "cd", "d1", "e1", "we", "ela")}
    xsdt = ar2.bf16(128, DIN)
    Zs = [ar2.bf16(128, 256) for i in range(2)]
    state = ar2.f32(128, DIN)
    state_b = ar2.bf16(128, DIN)
    bst = [Buf() for _ in range(NG)]
    bstb = [Buf() for _ in range(NG)]

    def v3(v, n):
        v.t = v.t.rearrange("p (h l) -> p h l", l=n)
        return v
    dtaU = [v3(ar.f32(128, 512), 128) for i in range(2)]
    diff = [v3(ar.f32(128, 512), 128) for i in range(2)]
    ela = [v3(ar.f32(128, 512), 128) for i in range(2)]
    mixT = [v3(ar.bf16(128, 512), 128) for i in range(2)]
    CS = [v3(ar.bf16(128, 512), 128) for i in range(2)]
    xsw = [ar.bf16(128, 256) for i in range(2)]
    y_g = [ar.f32(128, 256) for i in range(2)]
    yS_g = [ar.bf16(128, 256) for i in range(2)]
    y_gb = [ar.bf16(128, 256) for i in range(2)]
    psPRE = [PS[3], PS[4], PS[5], PS[6]]
    psLs = [PS[5], PS[3]]
    psYs = [PS[6], PS[4]]
    pscs = [PS[0], PS[2]]

    def ssd_chunk(hT, j, full, p_sweep=False, ys_src=None, y_dst=None, ys_buf=None, yd_buf=None, extra_gens=()):
        nblk = 32 if full else 24
        tp = tapsT[j]
        s = sm
        def dt_gen():
            pm = PS[2]
            for k in range(8):
                P.pe(lambda e, k=k: e.matmul(pm.t[:, 0:32], lhsT=hT.t[:, k, 2:130], rhs=wdt[j].t[:, k, :], start=(k == 0), stop=(k == 7)),
                     [hT.b, wdt[j].b], [pm.b], acc=True)
            s = sm
            P.dve(lambda e: e.tensor_tensor(out=s["x1"].t, in0=pm.t[:, 0:32], in1=dtb_bc[j].t[:], op=ALU.add), [pm.b, dtb_bc[j].b], [s["x1"].b])
            yield
            P.dve(lambda e: e.scalar_tensor_tensor(out=s["ab"].t, in0=s["x1"].t, scalar=-1.0, in1=s["x1"].t, op0=ALU.mult, op1=ALU.max), [s["x1"].b], [s["ab"].b])
            yield
            P.act(lambda e: e.activation(out=s["e"].t, in_=s["ab"].t, func=AF.Exp, scale=-1.0), [s["ab"].b], [s["e"].b])
            yield
            P.dve(lambda e: e.tensor_scalar(out=s["d1"].t, in0=s["e"].t, scalar1=2.0, scalar2=None, op0=ALU.add), [s["e"].b], [s["d1"].b])
            yield
            P.dve(lambda e: e.reciprocal(out=s["d1"].t, in_=s["d1"].t), [s["d1"].b], [s["d1"].b])
            yield
            P.dve(lambda e: e.tensor_tensor(out=s["e1"].t, in0=s["e"].t, in1=s["d1"].t, op=ALU.mult), [s["e"].b, s["d1"].b], [s["e1"].b])
            P.dve(lambda e: e.tensor_tensor(out=s["d1"].t, in0=s["e1"].t, in1=s["e1"].t, op=ALU.mult), [s["e1"].b], [s["d1"].b])
            P.dve(lambda e: e.tensor_scalar(out=s["l1"].t, in0=s["d1"].t, scalar1=1.0 / 11, scalar2=1.0 / 9, op0=ALU.mult, op1=ALU.add), [s["d1"].b], [s["l1"].b])
            yield
            for cst in (1.0 / 7, 1.0 / 5, 1.0 / 3, 1.0):
                P.dve(lambda e: e.tensor_tensor(out=s["l1"].t, in0=s["l1"].t, in1=s["d1"].t, op=ALU.mult), [s["l1"].b, s["d1"].b], [s["l1"].b])
                P.dve(lambda e, cst=cst: e.tensor_scalar(out=s["l1"].t, in0=s["l1"].t, scalar1=cst, scalar2=None, op0=ALU.add), [s["l1"].b], [s["l1"].b])
            P.dve(lambda e: e.scalar_tensor_tensor(out=s["l1"].t, in0=s["l1"].t, scalar=2.0, in1=s["e1"].t, op0=ALU.mult, op1=ALU.mult), [s["l1"].b, s["e1"].b], [s["l1"].b])
            yield
            P.dve(lambda e: e.scalar_tensor_tensor(out=s["dt"].t, in0=s["x1"].t, scalar=0.0, in1=s["l1"].t, op0=ALU.max, op1=ALU.add),
                  [s["x1"].b, s["l1"].b], [s["dt"].b])
            P.dve(lambda e: e.tensor_tensor(out=s["dta"].t, in0=s["dt"].t, in1=a_bc[j].t[:], op=ALU.mult), [s["dt"].b, a_bc[j].b], [s["dta"].b])
            yield
            P.pe(lambda e: e.matmul(pm.t[:, 32:64], lhsT=UP.t[:], rhs=s["dta"].t, start=True, stop=True), [UP.b, s["dta"].b], [pm.b])
            yield
            P.pe(lambda e: e.matmul(pm.t[:, 64:96], lhsT=ones_f.t[:], rhs=s["dta"].t, start=True, stop=True), [ones_f.b, s["dta"].b], [pm.b], acc=True)
            P.act(lambda e: e.copy(out=s["la"].t, in_=pm.t[:, 32:64]), [pm.b], [s["la"].b])
            yield
            P.act(lambda e: e.activation(out=s["ela"].t, in_=s["la"].t, func=AF.Exp), [s["la"].b], [s["ela"].b])
            yield
            P.act(lambda e: e.activation(out=s["cd"].t, in_=pm.t[:, 64:96], func=AF.Exp), [pm.b], [s["cd"].b])
            yield
            P.dve(lambda e: e.tensor_tensor(out=s["d1"].t, in0=pm.t[:, 64:96], in1=s["la"].t, op=ALU.subtract), [pm.b, s["la"].b], [s["d1"].b])
            yield
            P.act(lambda e: e.activation(out=s["e1"].t, in_=s["d1"].t, func=AF.Exp), [s["d1"].b], [s["e1"].b])
            yield
            P.dve(lambda e: e.tensor_tensor(out=s["we"].t, in0=s["e1"].t, in1=s["dt"].t, op=ALU.mult), [s["e1"].b, s["dt"].b], [s["we"].b])
            yield
            yield

        def blk_gen(cb):
            pst = psPRE[cb % 4]
            o0 = 0
            for k in range(8):
                P.pe(lambda e, k=k: e.matmul(pst.t[:, o0:o0 + 132], lhsT=wxbc[:, k, cb * 128:(cb + 1) * 128], rhs=hT.t[:, k, :],
                                             start=(k == 0), stop=(k == 7)),
                     [bWlo, hT.b], [pst.b], acc=True)
            a_t = acc[cb % len(acc)]
            P.act(lambda e: e.activation(out=a_t.t, in_=pst.t[:, o0:o0 + 128], func=AF.Identity,
                                         scale=tp.t[:, 0, cb:cb + 1], bias=convb.t[:, cb:cb + 1]),
                  [pst.b, tp.b, convb.b], [a_t.b])
            yield
            for o in range(1, 5):
                P.dve(lambda e, o=o: e.scalar_tensor_tensor(out=a_t.t, in0=pst.t[:, o0 + o:o0 + o + 128], scalar=tp.t[:, o, cb:cb + 1],
                                                            in1=a_t.t, op0=ALU.mult, op1=ALU.add),
                      [pst.b, tp.b, a_t.b], [a_t.b])
                yield
            P.act(lambda e: e.activation(out=xbcT.t[:, cb, :], in_=a_t.t, func=AF.Silu), [a_t.b], [bxb[cb]])
            yield

        def tr_gen(c0, dst, db):
            for i in range(8):
                P.pe(lambda e, i=i: e.transpose(out=psT.t[:, i * 128:(i + 1) * 128], in_=xbcT.t[:, c0 + i, :], identity=ident_b.t[:]),
                     [bxb[c0 + i], ident_b.b], [psT.b], acc=True)
            yield
            P.dve(lambda e: e.tensor_copy(out=dst, in_=psT.t[:, :]), [psT.b], [db])
            yield

        tr_list = [(lambda: tr_gen(0, xs_tok.t[:, 0:1024], xs_tok.b)), (lambda: tr_gen(8, xs_tok.t[:, 1024:2048], xs_tok.b)),
                   (lambda: tr_gen(16, B_tok.t[:, :], B_tok.b))]
        run_interleaved([dt_gen] + list(extra_gens) + [(lambda cb=cb: blk_gen(cb)) for cb in range(nblk)] + (tr_list if full else tr_list[:2]),
                        max_active=5 + len(extra_gens))
        if not full:
            run_interleaved(tr_list[2:], max_active=1)
        if full and dbg and not dbg_done[0]:
            dbg_done[0] = True
            ob = Buf()
            dma(dbg_xbc[:, :, :], xbcT.t, bxb, [ob])
            used.append(ob)
            for i_, n_ in enumerate(("x1", "ab", "e", "l1", "dt", "dta", "la", "cd", "d1", "e1", "we")):
                ob = Buf()
                dma(dbg_sm[i_], s[n_].t, [s[n_].b], [ob])
                used.append(ob)
            ob = Buf()
            dma(dbg_hT[:, :, :], hT.t, [hT.b], [ob])
            used.append(ob)
        if full:
            P.pool(lambda e: e.tensor_tensor(out=xsdt.t.rearrange("p (h d) -> p h d", d=HD), in0=xs_tok.t.rearrange("p (h d) -> p h d", d=HD),
                                             in1=s["dt"].t.unsqueeze(2).broadcast_to([128, NH, HD]), op=ALU.mult),
                   [xs_tok.b, s["dt"].b], [xsdt.b])
        if full and p_sweep:
            P.dve(lambda e: e.tensor_tensor(out=xsD.t.rearrange("p (h d) -> p h d", d=HD), in0=xs_tok.t.rearrange("p (h d) -> p h d", d=HD),
                                            in1=dsk_bc.t[:].unsqueeze(2).broadcast_to([128, NH, HD]), op=ALU.mult),
                  [xs_tok.b, dsk_bc.b], [xsD.b])
        def group_gen(g):
            i2 = g % 2
            pL = psLs[i2]
            pY = psYs[i2]
            psc = pscs[i2]
            if full:
                if p_sweep and use_yS:
                    dma(yS_g[i2].t, ys_src[:, g * 256:(g + 1) * 256], [ys_buf], [yS_g[i2].b])
                P.pool(lambda e: e.tensor_tensor(out=dtaU[i2].t, in0=UP.t[:].unsqueeze(1).broadcast_to([128, 4, 128]),
                                                 in1=s["dta"].t[:, 4 * g:4 * g + 4].unsqueeze(2).broadcast_to([128, 4, 128]), op=ALU.mult),
                       [UP.b, s["dta"].b], [dtaU[i2].b])
                P.pe(lambda e: e.matmul(psc.t[:, 0:128], lhsT=xbcT.t[:, 16 + g, :], rhs=xbcT.t[:, 24 + g, :], start=True, stop=True),
                     [bxb[16 + g], bxb[24 + g]], [psc.b])
                P.pe(lambda e: e.matmul(psc.t[:, 128:384], lhsT=xbcT.t[:, 24 + g, :], rhs=state_b.t[:, g * 256:(g + 1) * 256], start=True, stop=True),
                     [bxb[24 + g], bstb[g]], [psc.b], acc=True)
                yield
                P.pe(lambda e: e.matmul(pL.t[:, :], lhsT=ones_f.t[:], rhs=dtaU[i2].t.rearrange("p h l -> p (h l)"), start=True, stop=False, skip_group_check=True),
                     [ones_f.b, dtaU[i2].b], [pL.b])
                P.pe(lambda e: e.matmul(pL.t[:, :], lhsT=ident_b.t[:], rhs=negP4b.t[:], start=False, stop=True, skip_group_check=True),
                     [ident_b.b, negP4b.b], [pL.b], acc=True)
                yield
                P.dve(lambda e: e.tensor_tensor(out=diff[i2].t, in0=pL.t[:, :].rearrange("p (h l) -> p h l", l=128),
                                                in1=s["la"].t[:, 4 * g:4 * g + 4].unsqueeze(2).broadcast_to([128, 4, 128]), op=ALU.subtract),
                      [pL.b, s["la"].b], [diff[i2].b])
                yield
                P.act(lambda e: e.activation(out=diff[i2].t, in_=diff[i2].t, func=AF.Exp), [diff[i2].b], [diff[i2].b])
                P.dve(lambda e: e.tensor_tensor(out=Zs[i2].t.rearrange("p (h d) -> p h d", d=HD), in0=psc.t[:, 128:384].rearrange("p (h d) -> p h d", d=HD),
                                                in1=s["ela"].t[:, 4 * g:4 * g + 4].unsqueeze(2).broadcast_to([128, 4, HD]), op=ALU.mult),
                      [psc.b, s["ela"].b], [Zs[i2].b])
                yield
                P.dve(lambda e: e.tensor_tensor(out=mixT[i2].t, in0=diff[i2].t, in1=psc.t[:, 0:128].unsqueeze(1).broadcast_to([128, 4, 128]), op=ALU.mult),
                      [diff[i2].b, psc.b], [mixT[i2].b])
                yield
                P.pe(lambda e: e.matmul(pY.t[:, 0:256], lhsT=ident_b.t[:], rhs=Zs[i2].t, start=True, stop=False, skip_group_check=True),
                     [ident_b.b, Zs[i2].b], [pY.b])
                if p_sweep:
                    P.pe(lambda e: e.matmul(pY.t[:, 0:256], lhsT=ident_b.t[:], rhs=xsD.t[:, g * 256:(g + 1) * 256], start=False, stop=False, skip_group_check=True),
                         [ident_b.b, xsD.b], [pY.b], acc=True)
                    if use_yS:
                        P.pe(lambda e: e.matmul(pY.t[:, 0:256], lhsT=J_b.t[:], rhs=yS_g[i2].t, start=False, stop=False, skip_group_check=True),
                             [J_b.b, yS_g[i2].b], [pY.b], acc=True)
                for h in range(4):
                    hh = 4 * g + h
                    P.pe(lambda e, h=h, hh=hh: e.matmul(pY.t[:, h * 64:(h + 1) * 64], lhsT=mixT[i2].t[:, h, :], rhs=xsdt.t[:, hh * 64:(hh + 1) * 64],
                                                       start=False, stop=True, skip_group_check=True),
                         [mixT[i2].b, xsdt.b], [pY.b], acc=True)
                yield
                yo = y_g[i2] if p_sweep else y_gb[i2]
                P.act(lambda e: e.copy(out=yo.t, in_=pY.t[:, 0:256]), [pY.b], [yo.b])
                dma(y_dst[:, g * 256:(g + 1) * 256], yo.t, [yo.b], [yd_buf])
                yield
            P.pool(lambda e: e.tensor_tensor(out=xsw[i2].t.rearrange("p (h d) -> p h d", d=HD), in0=xs_tok.t[:, g * 256:(g + 1) * 256].rearrange("p (h d) -> p h d", d=HD),
                                             in1=s["we"].t[:, 4 * g:4 * g + 4].unsqueeze(2).broadcast_to([128, 4, HD]), op=ALU.mult),
                   [xs_tok.b, s["we"].b], [xsw[i2].b])
            yield
            pct = PS[1]
            P.pe(lambda e: e.matmul(pct.t[:, i2 * 256:(i2 + 1) * 256], lhsT=B_tok.t[:, g * 128:(g + 1) * 128], rhs=xsw[i2].t, start=True, stop=True),
                 [B_tok.b, xsw[i2].b], [pct.b])
            yield
            sv = state.t[:, g * 256:(g + 1) * 256]
            P.dve(lambda e: e.tensor_tensor(out=sv.rearrange("p (h d) -> p h d", d=HD), in0=sv.rearrange("p (h d) -> p h d", d=HD),
                                            in1=s["cd"].t[:, 4 * g:4 * g + 4].unsqueeze(2).broadcast_to([128, 4, HD]), op=ALU.mult),
                  [bst[g], s["cd"].b], [bst[g]])
            yield
            P.dve(lambda e: e.tensor_tensor(out=sv, in0=pct.t[:, i2 * 256:(i2 + 1) * 256], in1=sv, op=ALU.add), [pct.b, bst[g]], [bst[g]])
            yield
            P.act(lambda e: e.copy(out=state_b.t[:, g * 256:(g + 1) * 256], in_=sv), [bst[g]], [bstb[g]])
            yield

        skew = 2
        active = []
        nxt = 0
        steps_of_last = 0
        while active or nxt < NG:
            if nxt < NG and len(active) < 2 and (not active or steps_of_last >= skew):
                active.append(group_gen(nxt))
                nxt += 1
                steps_of_last = 0
            for gen in list(active):
                try:
                    next(gen)
                except StopIteration:
                    active.remove(gen)
            steps_of_last += 1

    def reset_state():
        P.pool(lambda e: e.memset(state.t, 0.0), bst, bst)
        P.pool(lambda e: e.memset(state_b.t, 0.0), bstb, bstb)

    bYS = [Buf() for _ in range(32)]
    bYT = [Buf() for _ in range(32)]
    bBS = [Buf() for _ in range(32)]

    def run_sweep(chunks, j, p_sweep):
        make_hT(hTe[0], chunks[0][0], chunks[0][1], chunks[0][2], chunks[0][3], chunks[0][4], 0)
        for i, (src, t0, lo, hi, mj, full, oi) in enumerate(chunks):
            eg = []
            if i + 1 < len(chunks):
                n = chunks[i + 1]
                eg = [lambda n=n, i=i: make_hT_gen(hTe[(i + 1) % 2], n[0], n[1], n[2], n[3], n[4], (i + 1) % 2)]
            if full and p_sweep:
                cS = 31 - oi
                ssd_chunk(hTe[i % 2], j, True, True, ys_src=yS_d[cS], y_dst=yt_d[oi], ys_buf=bYS[cS], yd_buf=bYT[oi], extra_gens=eg)
                used.append(bYT[oi])
            elif full:
                ssd_chunk(hTe[i % 2], j, True, False, y_dst=yS_d[oi], yd_buf=bYS[oi], extra_gens=eg)
                used.append(bYS[oi])
            else:
                ssd_chunk(hTe[i % 2], j, False, extra_gens=eg)

    if "S" in sweeps:
        reset_state()
        ch = []
        for c in range(n_ctx):
            ch.append((cr, c * 128, c > 0, c < CTX // 128 - 1, 1, False, None))
        for c in range(32 - n_oth, 32):
            ch.append((xr, c * 128, c > 0, True, 0, False, None))
        for c in range(n_own):
            ch.append((xr, OWN + c * 128, True, (OWN + c * 128 + 128) < SEQ, 0, True, c))
        run_sweep(ch, 1, False)
    if "P" in sweeps:
        reset_state()
        ch = []
        for c in range(n_ctx):
            ch.append((cl, c * 128, c > 0, c < CTX // 128 - 1, 1, False, None))
        for c in range(n_own):
            ch.append((xl, c * 128, c > 0, True, 0, True, c))
        run_sweep(ch, 0, True)

    if "1" in sweeps:
        P.barrier(tiny, [PS[2].b])
        ar.off = AR_BASE
        wz = wload(32768, w_in_v[:, :, DT_END:Z_END], 8, DIN, bWhi)
        woss = wload(49152, w_oss.rearrange("(k p) n -> p k n", p=128), 16, D, bWhi, step=8)
        wglu = wload(0, w_in_v[:, :, Z_END:GLU_END], 8, 2 * D, bWlo)
        wgt = wload(16384, w_in_v[:, :, CG_END:IN_COLS], 8, 2 * D, bWlo)
        zs = ar.f32(128, DIN)
        y_in = ar.f32(128, DIN)
        ssg = ar.f32(128, 8)
        yn = ar.bf16(128, DIN)
        ynT = ar.bf16(128, 16 * 128)
        ynT.t = ynT.t.rearrange("p (k n) -> p k n", n=128)
        bs_sb = ar.f32(128, D)
        normw_bc = ar.f32(128, DIN)
        dma(normw_bc.t, ssm_nw.partition_broadcast(128), [], [normw_bc.b])

        def t1_gen(hT, c):
            for qd in range(4):
                pz = PS[3 + qd]
                for k in range(8):
                    P.pe(lambda e, k=k, qd=qd, pz=pz: e.matmul(pz.t[:, :], lhsT=hT.t[:, k, 2:130], rhs=wz[:, k, qd * 512:(qd + 1) * 512], start=(k == 0), stop=(k == 7)),
                         [hT.b, bWhi], [pz.b], acc=True)
                P.act(lambda e, qd=qd, pz=pz: e.activation(out=zs.t[:, qd * 512:(qd + 1) * 512], in_=pz.t[:, :], func=AF.Silu), [pz.b], [zs.b])
                yield
            dma(y_in.t, yt_d[c], [bYT[c]], [y_in.b])
            P.dve(lambda e: e.tensor_tensor(out=y_in.t, in0=y_in.t, in1=zs.t, op=ALU.mult), [y_in.b, zs.b], [y_in.b])
            yield
            P.pool(lambda e: e.memset(ssg.t, 0.0), [], [ssg.b])
            for g in range(NG):
                P.act(lambda e, g=g: e.activation(out=zs.t[:, g * 256:(g + 1) * 256], in_=y_in.t[:, g * 256:(g + 1) * 256], func=AF.Square, accum_out=ssg.t[:, g:g + 1]),
                      [y_in.b], [zs.b, ssg.b])
                yield
            P.dve(lambda e: e.tensor_scalar(out=ssg.t, in0=ssg.t, scalar1=1.0 / 256, scalar2=EPS, op0=ALU.mult, op1=ALU.add), [ssg.b], [ssg.b])
            P.act(lambda e: e.activation(out=ssg.t, in_=ssg.t, func=AF.Ln), [ssg.b], [ssg.b])
            P.act(lambda e: e.activation(out=ssg.t, in_=ssg.t, func=AF.Exp, scale=-0.5), [ssg.b], [ssg.b])
            yield
            for g in range(NG):
                yield
                P.dve(lambda e, g=g: e.scalar_tensor_tensor(out=yn.t[:, g * 256:(g + 1) * 256], in0=y_in.t[:, g * 256:(g + 1) * 256], scalar=ssg.t[:, g:g + 1],
                                                            in1=normw_bc.t[:, g * 256:(g + 1) * 256], op0=ALU.mult, op1=ALU.mult),
                      [y_in.b, ssg.b, normw_bc.b], [yn.b])
            for ps_ in range(2):
                for i in range(8):
                    kb = ps_ * 8 + i
                    P.pe(lambda e, kb=kb, i=i: e.transpose(out=psT.t[:, i * 128:(i + 1) * 128], in_=yn.t[:, kb * 128:(kb + 1) * 128], identity=ident_b.t[:]),
                         [yn.b, ident_b.b], [psT.b], acc=True)
                yield
                P.dve(lambda e, ps_=ps_: e.tensor_copy(out=ynT.t[:, ps_ * 8:(ps_ + 1) * 8, :].rearrange("p k n -> p (k n)"), in_=psT.t[:, :]), [psT.b], [ynT.b])
                yield
            for hf in range(2):
                po = PS[2] if hf == 0 else PS[3]
                for kb in range(16):
                    P.pe(lambda e, kb=kb, hf=hf, po=po: e.matmul(po.t[:, :], lhsT=ynT.t[:, kb, :], rhs=woss[:, kb, hf * 512:(hf + 1) * 512], start=(kb == 0), stop=(kb == 15)),
                         [ynT.b, bWhi], [po.b], acc=True)
                yield
                P.act(lambda e, hf=hf, po=po: e.copy(out=bs_sb.t[:, hf * 512:(hf + 1) * 512], in_=po.t[:, :]), [po.b], [bs_sb.b])
                yield
            dma(bs_d[c], bs_sb.t, [bs_sb.b], [bBS[c]])
            yield

        make_hT(hTe[0], xl, 0, False, False, 0, 0, halo=False)
        for c in range(n_own):
            gl = [lambda c=c: t1_gen(hTe[c % 2], c)]
            if c + 1 < n_own:
                gl.append(lambda c=c: make_hT_gen(hTe[(c + 1) % 2], xl, (c + 1) * 128, False, False, 0, 0, halo=False))
            run_interleaved(gl, max_active=2)
            if "2" not in sweeps:
                used.append(bBS[c])

    if "2" in sweeps:
        P.barrier(tiny, [PS[2].b])
        ar.off = AR_BASE
        wcg = wload(32768, w_in_v[:, :, GLU_END:CG_END], 8, D, bWhi)
        woc = wload(40960, w_oc.rearrange("(k p) n -> p k n", p=128), 8, D, bWhi)
        wo = wload(49152, w_o.rearrange("(k p) n -> p k n", p=128), 8, D, bWhi)
        sg = [ar.f32(128, 128) for _ in range(4)]
        bup = [Buf() for _ in range(8)]
        bca = [Buf() for _ in range(8)]
        bcg = [Buf() for _ in range(8)]
        u_pad = ar.f32(128, 8 * 2 * 94)
        u_pad.t = u_pad.t.rearrange("p (c r n) -> p c r n", r=2, n=94)
        cacc = ar.f32(128, 8 * 128)
        cacc.t = cacc.t.rearrange("p (c n) -> p c n", n=128)
        ut = ar.f32(128, D)
        st = ar.f32(128, 8)
        suT = ar.f32(128, 8 * 128)
        cgT = ar.f32(128, 8 * 128)
        vT = ar.bf16(128, 8 * 128)
        vT.t = vT.t.rearrange("p (c n) -> p c n", n=128)
        gs = ar.f32(128, D)
        bs_in = ar.f32(128, D)
        mrg = ar.bf16(128, D)
        mT = ar.bf16(128, 8 * 128)
        mT.t = mT.t.rearrange("p (c n) -> p c n", n=128)
        gate_bc = ar.f32(128, D)
        fnw_bc = ar.f32(128, D)
        cw = ar.f32(128, 31 * 8)
        cw.t = cw.t.rearrange("p (k c) -> p k c", c=8)
        cwb = ar.f32(128, 8)
        lnw_fm = ar.f32(128, 8)
        lnb_fm = ar.f32(128, 8)
        dma(gate_bc.t, gate_d[0].partition_broadcast(128), [bgate], [gate_bc.b])
        dma(fnw_bc.t, fnw.partition_broadcast(128), [], [fnw_bc.b])
        for k in range(31):
            dma(cw.t[:, k, :], cconv[k].rearrange("(cb p) -> p cb", p=128), [], [cw.b], slow=True)
        dma(cwb.t, cconv_b.rearrange("(cb p) -> p cb", p=128), [], [cwb.b], slow=True)
        dma(lnw_fm.t, ln_w.rearrange("(cb p) -> p cb", p=128), [], [lnw_fm.b], slow=True)
        dma(lnb_fm.t, ln_b.rearrange("(cb p) -> p cb", p=128), [], [lnb_fm.b], slow=True)
        P.pool(lambda e: e.memset(u_pad.t, 0.0), [], bup)
        NPE = 8
        u_pb = ar.bf16(128, 8 * 2 * 94)
        u_pb.t = u_pb.t.rearrange("p (c r n) -> p c r n", r=2, n=94)
        bupb = [Buf() for _ in range(8)]
        P.pool(lambda e: e.memset(u_pb.t, 0.0), [], bupb)
        dg = WA.t[:, 57344:57344 + NPE * 8 * 128].rearrange("p (k c n) -> p k c n", c=8, n=128)
        bdg = Buf()
        for k in range(NPE):
            for cb in range(8):
                if (k * 8 + cb) % 2 == 0:
                    P.act(lambda e, k=k, cb=cb: e.activation(out=dg[:, k, cb, :], in_=ident_b.t[:], func=AF.Identity, scale=cw.t[:, k, cb:cb + 1]),
                          [ident_b.b, cw.b], [bdg])
                else:
                    P.pool(lambda e, k=k, cb=cb: e.tensor_scalar(out=dg[:, k, cb, :], in0=ident_b.t[:], scalar1=cw.t[:, k, cb:cb + 1], scalar2=None, op0=ALU.mult),
                           [ident_b.b, cw.b], [bdg])

        def t2_chunk(hT, c, eg=()):
            def cblk_gen(cb):
                pg = PS[3 + cb % 4]
                for (o0, wv, c0) in ((0, wglu, cb * 128), (128, wglu, D + cb * 128), (256, wcg, cb * 128)):
                    wb = bWlo if wv is wglu else bWhi
                    for k in range(8):
                        P.pe(lambda e, k=k, o0=o0, wv=wv, c0=c0: e.matmul(pg.t[:, o0:o0 + 128], lhsT=wv[:, k, c0:c0 + 128], rhs=hT.t[:, k, 2:130], start=(k == 0), stop=(k == 7)),
                             [hT.b, wb], [pg.b], acc=True)
                sgt = sg[cb % 4]
                P.act(lambda e: e.activation(out=sgt.t, in_=pg.t[:, 128:256], func=AF.Sigmoid), [pg.b], [sgt.b])
                P.act(lambda e: e.activation(out=cgT.t[:, cb * 128:(cb + 1) * 128], in_=pg.t[:, 256:384], func=AF.Silu), [pg.b], [bcg[cb]])
                yield
                P.dve(lambda e: e.tensor_tensor(out=u_pad.t[:, cb, :, 15:79], in0=pg.t[:, 0:128].rearrange("p (r n) -> p r n", n=64),
                                                in1=sgt.t.rearrange("p (r n) -> p r n", n=64), op=ALU.mult),
                      [pg.b, sgt.b], [bup[cb]])
                yield
                P.act(lambda e: e.copy(out=u_pb.t[:, cb, :, 15:79], in_=u_pad.t[:, cb, :, 15:79]), [bup[cb]], [bupb[cb]])
                yield
                cv = cacc.t[:, cb, :].rearrange("p (r n) -> p r n", n=64)
                pcv = pg.t[:, 384:512].rearrange("p (r n) -> p r n", n=64)
                for k in range(NPE):
                    P.pe(lambda e, k=k: e.matmul(pcv, lhsT=dg[:, k, cb, :], rhs=u_pb.t[:, cb, :, k:k + 64], start=(k == 0), stop=(k == NPE - 1)),
                         [bdg, bupb[cb]], [pg.b], acc=True)
                yield
                P.act(lambda e: e.activation(out=cv, in_=pcv, func=AF.Identity, bias=cwb.t[:, cb:cb + 1]),
                      [pg.b, cwb.b], [bca[cb]])
                yield
                for k in range(NPE, 31):
                    P.dve(lambda e, k=k: e.scalar_tensor_tensor(out=cv, in0=u_pad.t[:, cb, :, k:k + 64], scalar=cw.t[:, k, cb:cb + 1], in1=cv, op0=ALU.mult, op1=ALU.add),
                          [bup[cb], cw.b, bca[cb]], [bca[cb]])
                    yield

            run_interleaved(list(eg) + [(lambda cb=cb: cblk_gen(cb)) for cb in range(8)], max_active=4 + len(eg))
            P.pool(lambda e: e.memset(st.t, 0.0), [], [st.b])
            for hf in range(2):
                pu = PS[hf]
                for i in range(4):
                    cb = hf * 4 + i
                    P.pe(lambda e, cb=cb, i=i, pu=pu: e.transpose(out=pu.t[:, i * 128:(i + 1) * 128], in_=cacc.t[:, cb, :], identity=ident_f.t[:]),
                         [bca[cb], ident_f.b], [pu.b], acc=True)
                P.act(lambda e, hf=hf, pu=pu: e.activation(out=ut.t[:, hf * 512:(hf + 1) * 512], in_=pu.t[:, :], func=AF.Identity, accum_out=st.t[:, hf:hf + 1]),
                      [pu.b], [ut.b, st.b])
            P.act(lambda e: e.activation(out=gs.t, in_=ut.t, func=AF.Square, accum_out=st.t[:, 2:3]), [ut.b], [gs.b, st.b])
            P.dve(lambda e: e.tensor_tensor(out=st.t[:, 3:4], in0=st.t[:, 0:1], in1=st.t[:, 1:2], op=ALU.add), [st.b], [st.b])
            P.dve(lambda e: e.tensor_scalar(out=st.t[:, 3:4], in0=st.t[:, 3:4], scalar1=1.0 / D, scalar2=None, op0=ALU.mult), [st.b], [st.b])
            P.dve(lambda e: e.tensor_tensor(out=st.t[:, 4:5], in0=st.t[:, 3:4], in1=st.t[:, 3:4], op=ALU.mult), [st.b], [st.b])
            P.dve(lambda e: e.scalar_tensor_tensor(out=st.t[:, 5:6], in0=st.t[:, 2:3], scalar=1.0 / D, in1=st.t[:, 4:5], op0=ALU.mult, op1=ALU.subtract), [st.b], [st.b])
            P.dve(lambda e: e.tensor_scalar(out=st.t[:, 5:6], in0=st.t[:, 5:6], scalar1=EPS, scalar2=None, op0=ALU.add), [st.b], [st.b])
            P.act(lambda e: e.activation(out=st.t[:, 5:6], in_=st.t[:, 5:6], func=AF.Ln), [st.b], [st.b])
            P.act(lambda e: e.activation(out=st.t[:, 5:6], in_=st.t[:, 5:6], func=AF.Exp, scale=-0.5), [st.b], [st.b])
            P.dve(lambda e: e.scalar_tensor_tensor(out=st.t[:, 6:7], in0=st.t[:, 3:4], scalar=-1.0, in1=st.t[:, 5:6], op0=ALU.mult, op1=ALU.mult), [st.b], [st.b])
            P.dve(lambda e: e.tensor_scalar(out=ut.t, in0=ut.t, scalar1=st.t[:, 5:6], scalar2=st.t[:, 6:7], op0=ALU.mult, op1=ALU.add), [ut.b, st.b], [ut.b])
            for hf in range(2):
                pb = PS[5 + hf]
                for i in range(4):
                    cb = hf * 4 + i
                    P.pe(lambda e, cb=cb, i=i, pb=pb: e.transpose(out=pb.t[:, i * 128:(i + 1) * 128], in_=ut.t[:, cb * 128:(cb + 1) * 128], identity=ident_f.t[:]),
                         [ut.b, ident_f.b], [pb.b], acc=True)
                for i in range(4):
                    cb = hf * 4 + i
                    P.act(lambda e, cb=cb, i=i, pb=pb: e.activation(out=suT.t[:, cb * 128:(cb + 1) * 128], in_=pb.t[:, i * 128:(i + 1) * 128], func=AF.Silu,
                                                                    scale=lnw_fm.t[:, cb:cb + 1], bias=lnb_fm.t[:, cb:cb + 1]),
                          [pb.b, lnw_fm.b, lnb_fm.b], [suT.b])
            P.dve(lambda e: e.tensor_tensor(out=vT.t.rearrange("p c n -> p (c n)"), in0=suT.t, in1=cgT.t, op=ALU.mult), [suT.b] + bcg, [vT.b])
            for hf in range(2):
                pc = PS[hf]
                for kb in range(8):
                    P.pe(lambda e, kb=kb, hf=hf, pc=pc: e.matmul(pc.t[:, :], lhsT=vT.t[:, kb, :], rhs=woc[:, kb, hf * 512:(hf + 1) * 512], start=(kb == 0), stop=(kb == 7)),
                         [vT.b, bWhi], [pc.b], acc=True)
            dma(bs_in.t, bs_d[c], [bBS[c]], [bs_in.b])
            for gi in range(2):
                for hf in range(2):
                    pq = PS[5 + hf]
                    for k in range(8):
                        P.pe(lambda e, k=k, gi=gi, hf=hf, pq=pq: e.matmul(pq.t[:, :], lhsT=hT.t[:, k, 2:130], rhs=wgt[:, k, gi * D + hf * 512:gi * D + (hf + 1) * 512],
                                                                         start=(k == 0), stop=(k == 7)),
                             [hT.b, bWlo], [pq.b], acc=True)
                    P.act(lambda e, hf=hf, pq=pq: e.activation(out=gs.t[:, hf * 512:(hf + 1) * 512], in_=pq.t[:, :], func=AF.Sigmoid), [pq.b], [gs.b])
                if gi == 0:
                    P.dve(lambda e: e.tensor_tensor(out=bs_in.t, in0=bs_in.t, in1=gs.t, op=ALU.mult), [bs_in.b, gs.b], [bs_in.b])
                else:
                    for hf in range(2):
                        P.dve(lambda e, hf=hf: e.tensor_tensor(out=gs.t[:, hf * 512:(hf + 1) * 512], in0=PS[hf].t[:, :], in1=gs.t[:, hf * 512:(hf + 1) * 512], op=ALU.mult),
                              [PS[hf].b, gs.b], [gs.b])
                    P.dve(lambda e: e.tensor_tensor(out=mrg.t, in0=bs_in.t, in1=gs.t, op=ALU.add), [bs_in.b, gs.b], [mrg.b])
            for i in range(8):
                P.pe(lambda e, i=i: e.transpose(out=psT.t[:, i * 128:(i + 1) * 128], in_=mrg.t[:, i * 128:(i + 1) * 128], identity=ident_b.t[:]),
                     [mrg.b, ident_b.b], [psT.b], acc=True)
            P.dve(lambda e: e.tensor_copy(out=mT.t.rearrange("p c n -> p (c n)"), in_=psT.t[:, :]), [psT.b], [mT.b])
            dma(suT.t, xl[c * 128:(c + 1) * 128, :], [], [suT.b])
            for hf in range(2):
                po = PS[3 + hf]
                for kb in range(8):
                    P.pe(lambda e, kb=kb, hf=hf, po=po: e.matmul(po.t[:, :], lhsT=mT.t[:, kb, :], rhs=wo[:, kb, hf * 512:(hf + 1) * 512], start=(kb == 0), stop=(kb == 7)),
                         [mT.b, bWhi], [po.b], acc=True)
                P.dve(lambda e, hf=hf, po=po: e.tensor_tensor(out=ut.t[:, hf * 512:(hf + 1) * 512], in0=po.t[:, :], in1=gate_bc.t[:, hf * 512:(hf + 1) * 512], op=ALU.mult),
                      [po.b, gate_bc.b], [ut.b])
            P.dve(lambda e: e.tensor_tensor(out=ut.t, in0=ut.t, in1=suT.t, op=ALU.add), [ut.b, suT.b], [ut.b])
            P.pool(lambda e: e.memset(st.t[:, 7:8], 0.0), [], [st.b])
            P.act(lambda e: e.activation(out=gs.t, in_=ut.t, func=AF.Square, accum_out=st.t[:, 7:8]), [ut.b], [gs.b, st.b])
            P.dve(lambda e: e.tensor_scalar(out=st.t[:, 7:8], in0=st.t[:, 7:8], scalar1=1.0 / D, scalar2=EPS, op0=ALU.mult, op1=ALU.add), [st.b], [st.b])
            P.act(lambda e: e.activation(out=st.t[:, 7:8], in_=st.t[:, 7:8], func=AF.Ln), [st.b], [st.b])
            P.act(lambda e: e.activation(out=st.t[:, 7:8], in_=st.t[:, 7:8], func=AF.Exp, scale=-0.5), [st.b], [st.b])
            P.dve(lambda e: e.scalar_tensor_tensor(out=ut.t, in0=ut.t, scalar=st.t[:, 7:8], in1=fnw_bc.t, op0=ALU.mult, op1=ALU.mult), [ut.b, st.b, fnw_bc.b], [ut.b])
            ob = Buf()
            dma(out[c * 128:(c + 1) * 128, :], ut.t, [ut.b], [ob])
            used.append(ob)

        make_hT(hTe[0], xl, 0, False, False, 0, 0, halo=False)
        for c in range(n_own):
            eg = []
            if c + 1 < n_own:
                eg = [lambda c=c: make_hT_gen(hTe[(c + 1) % 2], xl, (c + 1) * 128, False, False, 0, 0, halo=False)]
            t2_chunk(hTe[c % 2], c, eg)

    nc = P.finalize(used)
    return P, nc


def _consts():
    c = np.zeros((7, 128, 128), np.float32)
    i = np.arange(128)
    c[0] = np.eye(128)
    c[1] = (i[:, None] <= i[None, :]).astype(np.float32)
    c[2] = np.where(i[None, :] >= i[:, None], 0.0, -30000.0)
    c[3] = 1.0
    c[4] = np.eye(128)[::-1]
    c[5, 0, :] = 1.0
    c[6, 1, :] = 1.0
    return c


def make_in_maps(inp):
    f = lambda a: np.ascontiguousarray(np.asarray(a, dtype=np.float32))
    x = np.asarray(inp["x"], np.float32)
    ctx = np.asarray(inp["ctx"], np.float32)
    c = np.asarray(inp["c"], np.float32)
    w_in = f(inp["w_in"][0])
    cw = np.asarray(inp["ssm_conv_w"][0], np.float32)
    z = np.zeros((1, C_END), np.float32)
    taps_nat = np.concatenate([cw, z], 0)
    taps_rev = np.concatenate([z, cw[::-1]], 0)
    consts = _consts()
    maps = []
    for core in range(8):
        b, half = core // 2, core % 2
        xb = x[b]
        cb = ctx[b]
        if half == 0:
            xl, xr, cl, cr = xb, xb[::-1], cb, cb[::-1]
            tP, tS = taps_nat, taps_rev
            dP, dS = 0, 1
            cc = np.asarray(inp["conf_conv_w"][0], np.float32)
        else:
            xl, xr, cl, cr = xb[::-1], xb, cb[::-1], cb
            tP, tS = taps_rev, taps_nat
            dP, dS = 1, 0
            cc = np.asarray(inp["conf_conv_w"][0], np.float32)[::-1]
        wdt = np.stack([w_in[:, C_END + 32 * dP:C_END + 32 * dP + 32], w_in[:, C_END + 32 * dS:C_END + 32 * dS + 32]], 0)
        m = {
            "xl": f(xl), "xr": f(xr), "cl": f(cl), "cr": f(cr),
            "cvec": f(np.stack([c[b], np.asarray(inp["c_ctx"], np.float32)], 0)),
            "w_mod": f(inp["w_mod"][0]),
            "b_mod2": f(np.stack([inp["b_mod"][0]] * 2, 0)),
            "norm_w2": f(np.stack([inp["norm_w"][0]] * 2, 0)),
            "w_in": w_in,
            "w_dt": f(wdt),
            "taps": f(np.stack([tP, tS], 0)),
            "conv_b": f(inp["ssm_conv_b"][0]),
            "dtb": f(np.stack([inp["dt_bias"][0][dP], inp["dt_bias"][0][dS]], 0)),
            "alog": f(np.stack([inp["a_log"][0][dP], inp["a_log"][0][dS]], 0)),
            "dskip": f(inp["d_skip"][0]),
            "ssm_nw": f(inp["ssm_norm_w"][0]),
            "w_oss": f(inp["w_out_ssm"][0]),
            "cconv": f(cc),
            "cconv_b": f(inp["conf_conv_b"][0]),
            "ln_w": f(inp["conf_ln_w"][0]),
            "ln_b": f(inp["conf_ln_b"][0]),
            "w_oc": f(inp["w_out_conf"][0]),
            "w_o": f(inp["w_out"][0]),
            "fnw": f(inp["final_norm_w"]),
            "consts": consts,
        }
        maps.append(m)
    return maps


def kernel(**inp):
    P, nc = build_program()
    maps = make_in_maps(inp)
    res = run_bass_kernel_spmd(nc, maps, core_ids=list(range(8)))
    outp = np.empty((4, SEQ, D), np.float32)
    for core in range(8):
        b, half = core // 2, core % 2
        o = res.results[core]["out"]
        if half == 0:
            outp[b, :OWN] = o
        else:
            outp[b, OWN:] = o[::-1]
    return outp
```

```python
import numpy as np
from contextlib import ExitStack
import concourse.bass as bass
import concourse.mybir as mybir
from concourse.bass_utils import run_bass_kernel_spmd

F32 = mybir.dt.float32
BF16 = mybir.dt.bfloat16
AF = mybir.ActivationFunctionType
ALU = mybir.AluOpType

ENGINES = ("pe", "act", "dve", "pool", "sp")
N_DMA_SEMS = 8
SEM_EPOCH = 8192

D = 1024
SEQ = 8192
OWN = 4096
CTX = 256
DIN = 2048
NH = 32
HD = 64
NG = 8
DS = 128
C_END = 4096
DT_END = C_END + 64
Z_END = DT_END + DIN
GLU_END = Z_END + 2 * D
CG_END = GLU_END + D
IN_COLS = CG_END + 2 * D
EPS = 1e-6


class Buf:
    __slots__ = ("last_w", "readers", "excl")

    def __init__(self, excl=False):
        self.last_w = None
        self.readers = {}
        self.excl = excl


class Op:
    __slots__ = ("eng", "fn", "deps", "signal", "semkey", "value", "is_dma", "implied", "waits")

    def __init__(self, eng, fn, is_dma=False):
        self.eng = eng
        self.fn = fn
        self.deps = []
        self.signal = is_dma
        self.semkey = None
        self.value = None
        self.is_dma = is_dma
        self.implied = None
        self.waits = None


class T:
    __slots__ = ("t", "b")

    def __init__(self, t):
        self.t = t
        self.b = Buf()


class Prog:
    def __init__(self):
        self.nc = bass.Bass("TRN2", target_bir_lowering=False)
        self.stack = ExitStack()
        self.order = []
        self.ndma = 0
        self.dmas = []

    def dram(self, name, shape, dtype, kind="Internal"):
        return self.nc.dram_tensor(name, list(shape), dtype, kind=kind).ap()

    def sb(self, name, shape, dtype=F32):
        return T(self.stack.enter_context(self.nc.sbuf_tensor(name, list(shape), dtype)))

    def ps(self, name, shape, dtype=F32):
        t = T(self.stack.enter_context(self.nc.psum_tensor(name, list(shape), dtype)))
        t.b.excl = True
        return t

    def _add(self, eng, fn, reads, writes, is_dma=False, pe_acc=False):
        op = Op(eng, fn, is_dma)
        deps = []
        if any(b.excl for b in reads):
            writes = list(writes) + [b for b in reads if b.excl and b not in writes]
            reads = [b for b in reads if not b.excl]
        for b in reads:
            if b.last_w is not None:
                deps.append(b.last_w)
        for b in writes:
            if b.last_w is not None:
                if not (pe_acc and b.last_w.eng == "pe" and not b.last_w.is_dma and not b.readers):
                    deps.append(b.last_w)
            deps.extend(b.readers.values())
        seen = set()
        for d in deps:
            if id(d) not in seen and d is not op:
                seen.add(id(d))
                d.signal = True
                op.deps.append(d)
        for b in reads:
            if is_dma:
                self.ndma += 1
                b.readers[("dma", self.ndma)] = op
            else:
                b.readers[eng] = op
        for b in writes:
            b.last_w = op
            b.readers = {}
        self.order.append(op)
        return op

    def pe(self, fn, reads, writes, acc=False):
        return self._add("pe", fn, reads, writes, pe_acc=acc)

    def act(self, fn, reads, writes):
        return self._add("act", fn, reads, writes)

    def dve(self, fn, reads, writes):
        return self._add("dve", fn, reads, writes)

    def pool(self, fn, reads, writes):
        return self._add("pool", fn, reads, writes)

    def dma(self, q, fn, reads, writes):
        op = self._add(q, fn, reads, writes, is_dma=True)
        self.dmas.append(op)
        return op

    def barrier(self, tiny, pe_bufs=()):
        xs = []
        for e in ("pe", "act", "dve", "pool"):
            op = self._add(e, tiny[e], [], list(pe_bufs) if e == "pe" else [])
            op.signal = True
            xs.append(op)
        dm = list(self.dmas)
        self.dmas = []
        for e in ("pe", "act", "dve", "pool", "sp"):
            op = self._add(e, tiny[e] if e != "sp" else None, [], list(pe_bufs) if e == "pe" else [])
            for d in xs + dm:
                d.signal = True
                op.deps.append(d)

    def finalize(self, final_bufs):
        nc = self.nc
        st = self.stack
        self._add("sp", None, final_bufs, [])
        cnt = {e: 0 for e in ENGINES}
        nd = {e: 0 for e in ENGINES}
        dcount = {}
        prev_on = {}
        for op in self.order:
            e = op.eng
            if op.is_dma:
                s = nd[e] % N_DMA_SEMS
                nd[e] += 1
                key = ("d", e, s)
                dcount[key] = dcount.get(key, 0) + 16
                op.semkey = key
                op.value = dcount[key]
                if key in prev_on:
                    op.deps.append(prev_on[key])
                prev_on[key] = op
            elif op.signal:
                op.semkey = ("c", e, cnt[e] // SEM_EPOCH)
                op.value = cnt[e] % SEM_EPOCH + 1
                cnt[e] += 1
        known = {e: {} for e in ENGINES}
        nw = 0
        for op in self.order:
            kn = known[op.eng]
            need = {}
            for d in op.deps:
                if kn.get(d.semkey, 0) >= d.value:
                    continue
                if d.semkey not in need or need[d.semkey].value < d.value:
                    need[d.semkey] = d
            waits = []
            for k, d in need.items():
                if kn.get(k, 0) >= d.value:
                    continue
                waits.append((k, d.value))
                kn[k] = d.value
                if d.implied:
                    for kk, vv in d.implied.items():
                        if kn.get(kk, 0) < vv:
                            kn[kk] = vv
            op.waits = waits
            nw += len(waits)
            if op.signal:
                op.implied = dict(kn)
        self.n_waits = nw
        self.counts = cnt
        sems = {}
        for e in ENGINES:
            for ep in range(cnt[e] // SEM_EPOCH + 1):
                sems[("c", e, ep)] = st.enter_context(nc.semaphore("c_%s%d" % (e, ep)))
        for e in ("sp", "act", "pool"):
            for i in range(N_DMA_SEMS):
                sems[("d", e, i)] = st.enter_context(nc.semaphore("d_%s%d" % (e, i)))
        per = {e: [op for op in self.order if op.eng == e] for e in ENGINES}
        block = st.enter_context(nc.Block())

        def run(e, eng):
            for op in per[e]:
                for (k, v) in op.waits:
                    eng.wait_ge(sems[k], v)
                if op.fn is None:
                    continue
                ins = op.fn(eng)
                if op.signal:
                    ins.then_inc(sems[op.semkey], 16 if op.is_dma else 1)

        @block.tensor
        def _(eng):
            run("pe", eng)

        @block.vector
        def _(eng):
            run("dve", eng)

        @block.scalar
        def _(eng):
            run("act", eng)

        @block.gpsimd
        def _(eng):
            run("pool", eng)

        @block.sync
        def _(eng):
            run("sp", eng)

        return nc


class V:
    __slots__ = ("t", "b")

    def __init__(self, ap, b=None):
        self.t = ap
        self.b = b if b is not None else Buf()


class Arena:
    def __init__(self, tile_f32, n):
        self.tile = tile_f32
        self.n = n
        self.off = 0

    def f32(self, p, cols):
        o = self.off
        self.off += cols
        assert self.off <= self.n, ("arena overflow", self.off, self.n)
        return V(self.tile[0:p, o:o + cols])

    def bf16(self, p, cols):
        w = (cols + 1) // 2
        o = self.off
        self.off += w
        assert self.off <= self.n, ("arena overflow", self.off, self.n)
        return V(self.tile[0:p, o:o + w].bitcast(BF16)[:, 0:cols])


def run_interleaved(gen_fns, max_active, start_every=1):
    active = []
    nxt = 0
    rounds = 0
    while active or nxt < len(gen_fns):
        if nxt < len(gen_fns) and len(active) < max_active and rounds % start_every == 0:
            active.append(gen_fns[nxt]())
            nxt += 1
        for gen in list(active):
            try:
                next(gen)
            except StopIteration:
                active.remove(gen)
        rounds += 1


def build_program(n_own=32, n_oth=32, n_ctx=2, sweeps="SP12", use_yS=True, dbg=False):
    P = Prog()
    nc = P.nc
    EI = "ExternalInput"
    xl = P.dram("xl", [SEQ, D], F32, EI)
    xr = P.dram("xr", [SEQ, D], F32, EI)
    cl = P.dram("cl", [CTX, D], F32, EI)
    cr = P.dram("cr", [CTX, D], F32, EI)
    cvec = P.dram("cvec", [2, D], F32, EI)
    w_mod = P.dram("w_mod", [D, 3 * D], F32, EI)
    b_mod2 = P.dram("b_mod2", [2, 3 * D], F32, EI)
    norm_w2 = P.dram("norm_w2", [2, D], F32, EI)
    w_in = P.dram("w_in", [D, IN_COLS], F32, EI)
    w_dt = P.dram("w_dt", [2, D, NH], F32, EI)
    taps = P.dram("taps", [2, 5, C_END], F32, EI)
    conv_b = P.dram("conv_b", [C_END], F32, EI)
    dtb = P.dram("dtb", [2, NH], F32, EI)
    alog = P.dram("alog", [2, NH], F32, EI)
    dskip = P.dram("dskip", [NH], F32, EI)
    ssm_nw = P.dram("ssm_nw", [DIN], F32, EI)
    w_oss = P.dram("w_oss", [DIN, D], F32, EI)
    cconv = P.dram("cconv", [31, D], F32, EI)
    cconv_b = P.dram("cconv_b", [D], F32, EI)
    ln_w = P.dram("ln_w", [D], F32, EI)
    ln_b = P.dram("ln_b", [D], F32, EI)
    w_oc = P.dram("w_oc", [D, D], F32, EI)
    w_o = P.dram("w_o", [D, D], F32, EI)
    fnw = P.dram("fnw", [D], F32, EI)
    consts = P.dram("consts", [7, 128, 128], F32, EI)
    out = P.dram("out", [OWN, D], F32, "ExternalOutput")
    dk = "ExternalOutput" if dbg else "Internal"
    yS_d = P.dram("yS_d", [32, 128, DIN], BF16, dk)
    yt_d = P.dram("yt_d", [32, 128, DIN], F32, dk)
    bs_d = P.dram("bs_d", [32, 128, D], F32, dk)
    if dbg:
        dbg_xbc = P.dram("dbg_xbc", [128, 32, 128], BF16, "ExternalOutput")
        dbg_sm = P.dram("dbg_sm", [11, 128, NH], F32, "ExternalOutput")
        dbg_hT = P.dram("dbg_hT", [128, 8, 132], BF16, "ExternalOutput")
    dbg_done = [False]
    used = []

    def dma(out_ap, in_ap, reads, writes, queue="sp", slow=False):
        if slow:
            return P.dma(queue, lambda e: e.dma_start(out=out_ap, in_=in_ap, allow_slow_non_contiguous=True), reads, writes)
        return P.dma(queue, lambda e: e.dma_start(out=out_ap, in_=in_ap), reads, writes)

    ident_f = P.sb("ident_f", [128, 128])
    UP = P.sb("UP", [128, 128])
    negP = P.sb("negP", [128, 128])
    ones_f = P.sb("ones_f", [128, 128])
    J_f = P.sb("J_f", [128, 128])
    sel0 = P.sb("sel0", [2, 128])
    ident_b = P.sb("ident_b", [128, 128], BF16)
    dummy = P.sb("dummy_t", [128, 8])
    for i, tl in enumerate([ident_f, UP, negP, ones_f, J_f]):
        dma(tl.t[:], consts[i, :, :], [], [tl.b])
    dma(sel0.t[:], consts[5, 0:2, :], [], [sel0.b])
    P.dve(lambda e: e.tensor_copy(out=ident_b.t[:], in_=ident_f.t[:]), [ident_f.b], [ident_b.b])
    J_b = P.sb("J_b", [128, 128], BF16)
    P.dve(lambda e: e.tensor_copy(out=J_b.t[:], in_=J_f.t[:]), [J_f.b], [J_b.b])
    negP4b = P.sb("negP4b", [128, 512], BF16)
    P.dve(lambda e: e.tensor_copy(out=negP4b.t[:].rearrange("p (h l) -> p h l", l=128), in_=negP.t[:].unsqueeze(1).broadcast_to([128, 4, 128])), [negP.b], [negP4b.b])

    PS = [P.ps("ps%d" % i, [128, 512]) for i in range(7)]
    psT = P.ps("psT", [128, 1024], BF16)

    tiny = {
        "pe": lambda e: e.matmul(PS[2].t[0:2, 0:2], lhsT=ident_f.t[0:2, 0:2], rhs=ident_f.t[0:2, 0:2], start=True, stop=True),
        "act": lambda e: e.copy(out=dummy.t[0:1, 0:1], in_=ident_f.t[0:1, 0:1]),
        "dve": lambda e: e.memset(dummy.t[0:1, 2:3], 0.0),
        "pool": lambda e: e.memset(dummy.t[0:1, 4:5], 0.0),
    }

    WA = P.sb("WA", [128, 65536], BF16)
    bWlo = Buf()
    bWhi = Buf()
    WKN = 17408
    WK = P.sb("WK", [128, WKN])
    ar = Arena(WK.t, WKN)

    def wload(col0, src_ap, kb, n, buf, step=4):
        view = WA.t[:, col0:col0 + kb * n].rearrange("p (k n) -> p k n", n=n)
        for k0 in range(0, kb, step):
            dma(view[:, k0:k0 + step, :], src_ap[:, k0:k0 + step, :], [], [buf], queue="pool")
        return view

    scT = P.sb("scT", [128, 8, 2])
    for j in range(2):
        cTj = P.sb("cT%d" % j, [128, 8])
        dma(cTj.t[:], cvec[j].rearrange("(k p) -> p k", p=128), [], [cTj.b], slow=True)
        P.act(lambda e, j=j, cTj=cTj: e.activation(out=scT.t[:, :, j], in_=cTj.t[:], func=AF.Silu), [cTj.b], [scT.b])
    wm = ar.f32(128, 8192)
    wm3 = wm.t.rearrange("p (k n) -> p k n", n=1024)
    modrow = ar.f32(2, 3 * D)
    bmod = ar.f32(2, 3 * D)
    nw2 = ar.f32(2, D)
    Arow = ar.f32(2, D)
    dma(bmod.t, b_mod2[:, :], [], [bmod.b])
    dma(nw2.t, norm_w2[:, :], [], [nw2.b])
    for piece in range(3):
        dma(wm3, w_mod.rearrange("(k p) n -> p k n", p=128)[:, :, piece * 1024:(piece + 1) * 1024], [], [wm.b])
        for hf in range(2):
            pst = PS[hf]
            for k in range(8):
                P.pe(lambda e, k=k, hf=hf, pst=pst: e.matmul(pst.t[0:2, :], lhsT=scT.t[:, k, :], rhs=wm3[:, k, hf * 512:(hf + 1) * 512],
                                                             start=(k == 0), stop=(k == 7)),
                     [scT.b, wm.b], [pst.b], acc=True)
            c0 = piece * 1024 + hf * 512
            P.dve(lambda e, pst=pst, c0=c0: e.tensor_tensor(out=modrow.t[:, c0:c0 + 512], in0=pst.t[0:2, :], in1=bmod.t[:, c0:c0 + 512], op=ALU.add),
                  [pst.b, bmod.b], [modrow.b])
    P.dve(lambda e: e.scalar_tensor_tensor(out=Arow.t, in0=modrow.t[:, D:2 * D], scalar=1.0, in1=nw2.t, op0=ALU.add, op1=ALU.mult),
          [modrow.b, nw2.b], [Arow.b])
    Afm = [P.sb("Afm%d" % j, [128, 8]) for j in range(2)]
    Sfm = [P.sb("Sfm%d" % j, [128, 8]) for j in range(2)]
    for (src, dst) in ((Arow, Afm), (modrow, Sfm)):
        for k in range(8):
            P.pe(lambda e, src=src, k=k: e.transpose(out=PS[2].t[:, 2 * k:2 * k + 2], in_=src.t[0:2, k * 128:(k + 1) * 128], identity=ident_f.t[0:2, 0:2]),
                 [src.b, ident_f.b], [PS[2].b], acc=True)
        for j in range(2):
            P.dve(lambda e, j=j, dst=dst: e.tensor_copy(out=dst[j].t[:], in_=PS[2].t[:, 0:16].rearrange("p (k j) -> p k j", j=2)[:, :, j]),
                  [PS[2].b], [dst[j].b])
    gate_d = P.dram("gate_d", [1, D], F32)
    bgate = Buf()
    dma(gate_d[:, :], modrow.t[0:1, 2 * D:3 * D], [modrow.b], [bgate])

    def bc_load(name, vec_ap, n):
        tl = P.sb(name, [128, n])
        dma(tl.t[:], vec_ap.partition_broadcast(128), [], [tl.b])
        return tl

    a_bc = []
    dtb_bc = []
    for j in range(2):
        al = bc_load("alog%d" % j, alog[j, :], NH)
        a = P.sb("a_bc%d" % j, [128, NH])
        P.act(lambda e, al=al, a=a: e.activation(out=a.t[:], in_=al.t[:], func=AF.Exp), [al.b], [a.b])
        P.dve(lambda e, a=a: e.tensor_scalar(out=a.t[:], in0=a.t[:], scalar1=-1.0, scalar2=None, op0=ALU.mult), [a.b], [a.b])
        a_bc.append(a)
        dtb_bc.append(bc_load("dtb%d" % j, dtb[j, :], NH))
    dsk_bc = bc_load("dsk", dskip, NH)
    tapsT = []
    for j in range(2):
        tl = P.sb("taps%d" % j, [128, 5, 32])
        for o in range(5):
            dma(tl.t[:, o, :], taps[j, o].rearrange("(cb p) -> p cb", p=128), [], [tl.b], slow=True)
        tapsT.append(tl)
    convb = P.sb("convb", [128, 32])
    dma(convb.t[:], conv_b.rearrange("(cb p) -> p cb", p=128), [], [convb.b], slow=True)
    wdt = []
    for j in range(2):
        tl = P.sb("wdt%d" % j, [128, 8, NH], BF16)
        dma(tl.t[:], w_dt[j].rearrange("(k p) n -> p k n", p=128), [], [tl.b], queue="pool")
        wdt.append(tl)

    w_in_v = w_in.rearrange("(k p) n -> p k n", p=128)
    wxbc = wload(0, w_in_v[:, :, 0:C_END], 8, C_END, bWlo)

    P.barrier(tiny, [PS[2].b])
    ar.off = 0
    xt0 = ar.f32(128, D)
    xt = [xt0, xt0]
    xn = ar.f32(128, D)
    xh = ar.f32(4, D)
    xhn = xh
    ss = ar.f32(128, 2)
    hTe = [ar.bf16(128, 8 * 132) for i in range(2)]
    for h_ in hTe:
        h_.t = h_.t.rearrange("p (k n) -> p k n", n=132)
    AR_BASE = ar.off

    def make_hT_gen(dst, src, t0, lo_ok, hi_ok, j, slot, halo=True):
        x_t = xt[slot]
        dma(x_t.t, src[t0:t0 + 128, :], [], [x_t.b])
        P.pool(lambda e: e.memset(ss.t, 0.0), [], [ss.b])
        P.act(lambda e: e.activation(out=xn.t, in_=x_t.t, func=AF.Square, accum_out=ss.t[:, 0:1]), [x_t.b], [xn.b, ss.b])
        if halo:
            P.pool(lambda e: e.memset(xh.t, 0.0), [], [xh.b])
            if lo_ok:
                dma(xh.t[0:2, :], src[t0 - 2:t0, :], [], [xh.b])
            if hi_ok:
                dma(xh.t[2:4, :], src[t0 + 128:t0 + 130, :], [], [xh.b])
            P.act(lambda e: e.activation(out=xn.t[0:4, :], in_=xh.t, func=AF.Square, accum_out=ss.t[0:4, 1:2]), [xh.b], [xn.b, ss.b])
        yield
        P.dve(lambda e: e.tensor_scalar(out=ss.t, in0=ss.t, scalar1=1.0 / D, scalar2=EPS, op0=ALU.mult, op1=ALU.add), [ss.b], [ss.b])
        yield
        P.act(lambda e: e.activation(out=ss.t, in_=ss.t, func=AF.Ln), [ss.b], [ss.b])
        yield
        P.act(lambda e: e.activation(out=ss.t, in_=ss.t, func=AF.Exp, scale=-0.5), [ss.b], [ss.b])
        yield
        P.dve(lambda e: e.tensor_scalar(out=xn.t, in0=x_t.t, scalar1=ss.t[:, 0:1], scalar2=None, op0=ALU.mult), [x_t.b, ss.b], [xn.b])
        yield
        for hf in range(2):
            pst = PS[hf]
            yield
            for kk in range(4):
                k = hf * 4 + kk
                P.pe(lambda e, k=k, kk=kk, pst=pst: e.transpose(out=pst.t[:, kk * 128:(kk + 1) * 128], in_=xn.t[:, k * 128:(k + 1) * 128], identity=ident_f.t[:]),
                     [xn.b, ident_f.b], [pst.b], acc=True)
            for kk in range(4):
                k = hf * 4 + kk
                P.act(lambda e, k=k, kk=kk, pst=pst: e.activation(out=dst.t[:, k, 2:130], in_=pst.t[:, kk * 128:(kk + 1) * 128], func=AF.Identity,
                                                                  scale=Afm[j].t[:, k:k + 1], bias=Sfm[j].t[:, k:k + 1]),
                      [pst.b, Afm[j].b, Sfm[j].b], [dst.b])
        yield
        if halo:
            P.dve(lambda e: e.tensor_scalar(out=xhn.t, in0=xh.t, scalar1=ss.t[0:4, 1:2], scalar2=None, op0=ALU.mult), [xh.b, ss.b], [xhn.b])
            pst = PS[2]
            for k in range(8):
                P.pe(lambda e, k=k: e.transpose(out=pst.t[:, 128 + k * 4:128 + (k + 1) * 4], in_=xhn.t[0:4, k * 128:(k + 1) * 128], identity=ident_f.t[0:4, 0:4]),
                     [xhn.b, ident_f.b], [pst.b], acc=True)
            yield
            for k in range(8):
                yield
                P.act(lambda e, k=k: e.activation(out=dst.t[:, k, 0:2], in_=pst.t[:, 128 + k * 4:128 + k * 4 + 2], func=AF.Identity,
                                                  scale=Afm[j].t[:, k:k + 1], bias=Sfm[j].t[:, k:k + 1]),
                      [pst.b, Afm[j].b, Sfm[j].b], [dst.b])
                P.act(lambda e, k=k: e.activation(out=dst.t[:, k, 130:132], in_=pst.t[:, 128 + k * 4 + 2:128 + k * 4 + 4], func=AF.Identity,
                                                  scale=Afm[j].t[:, k:k + 1], bias=Sfm[j].t[:, k:k + 1]),
                      [pst.b, Afm[j].b, Sfm[j].b], [dst.b])
            if not lo_ok:
                P.dve(lambda e: e.memset(dst.t[:, :, 0:2], 0.0), [], [dst.b])
            if not hi_ok:
                P.dve(lambda e: e.memset(dst.t[:, :, 130:132], 0.0), [], [dst.b])

    def make_hT(*a_, **k_):
        for _ in make_hT_gen(*a_, **k_):
            pass

    ar2 = Arena(WA.t[:, 32768:65536].bitcast(F32), 16384)
    acc = [ar.f32(128, 128) for i in range(8)]
    xbcT = ar2.bf16(128, 32 * 128)
    xbcT.t = xbcT.t.rearrange("p (c n) -> p c n", n=128)
    bxb = [Buf() for _ in range(32)]
    xs_tok = ar2.bf16(128, DIN)
    B_tok = ar2.bf16(128, 1024)
    xsD = ar2.bf16(128, DIN)
    sm = {n: ar.f32(128, NH) for n in ("x1", "ab", "e", "l1", "dt", "dta", "la", "cd", "d1", "e1", "we", "ela")}
    xsdt = ar2.bf16(128, DIN)
    Zs = [ar2.bf16(128, 256) for i in range(2)]
    state = ar2.f32(128, DIN)
    state_b = ar2.bf16(128, DIN)
    bst = [Buf() for _ in range(NG)]
    bstb = [Buf() for _ in range(NG)]

    def v3(v, n):
        v.t = v.t.rearrange("p (h l) -> p h l", l=n)
        return v
    dtaU = [v3(ar.f32(128, 512), 128) for i in range(2)]
    diff = [v3(ar.f32(128, 512), 128) for i in range(2)]
    ela = [v3(ar.f32(128, 512), 128) for i in range(2)]
    mixT = [v3(ar.bf16(128, 512), 128) for i in range(2)]
    CS = [v3(ar.bf16(128, 512), 128) for i in range(2)]
    xsw = [ar.bf16(128, 256) for i in range(2)]
    y_g = [ar.f32(128, 256) for i in range(2)]
    yS_g = [ar.bf16(128, 256) for i in range(2)]
    y_gb = [ar.bf16(128, 256) for i in range(2)]
    psPRE = [PS[3], PS[4], PS[5], PS[6]]
    psLs = [PS[5], PS[3]]
    psYs = [PS[6], PS[4]]
    pscs = [PS[0], PS[2]]

    def ssd_chunk(hT, j, full, p_sweep=False, ys_src=None, y_dst=None, ys_buf=None, yd_buf=None, extra_gens=()):
        nblk = 32 if full else 24
        tp = tapsT[j]
        s = sm
        def dt_gen():
            pm = PS[2]
            for k in range(8):
                P.pe(lambda e, k=k: e.matmul(pm.t[:, 0:32], lhsT=hT.t[:, k, 2:130], rhs=wdt[j].t[:, k, :], start=(k == 0), stop=(k == 7)),
                     [hT.b, wdt[j].b], [pm.b], acc=True)
            s = sm
            P.dve(lambda e: e.tensor_tensor(out=s["x1"].t, in0=pm.t[:, 0:32], in1=dtb_bc[j].t[:], op=ALU.add), [pm.b, dtb_bc[j].b], [s["x1"].b])
            yield
            P.dve(lambda e: e.scalar_tensor_tensor(out=s["ab"].t, in0=s["x1"].t, scalar=-1.0, in1=s["x1"].t, op0=ALU.mult, op1=ALU.max), [s["x1"].b], [s["ab"].b])
            yield
            P.act(lambda e: e.activation(out=s["e"].t, in_=s["ab"].t, func=AF.Exp, scale=-1.0), [s["ab"].b], [s["e"].b])
            yield
            P.dve(lambda e: e.tensor_scalar(out=s["d1"].t, in0=s["e"].t, scalar1=2.0, scalar2=None, op0=ALU.add), [s["e"].b], [s["d1"].b])
            yield
            P.dve(lambda e: e.reciprocal(out=s["d1"].t, in_=s["d1"].t), [s["d1"].b], [s["d1"].b])
            yield
            P.dve(lambda e: e.tensor_tensor(out=s["e1"].t, in0=s["e"].t, in1=s["d1"].t, op=ALU.mult), [s["e"].b, s["d1"].b], [s["e1"].b])
            P.dve(lambda e: e.tensor_tensor(out=s["d1"].t, in0=s["e1"].t, in1=s["e1"].t, op=ALU.mult), [s["e1"].b], [s["d1"].b])
            P.dve(lambda e: e.tensor_scalar(out=s["l1"].t, in0=s["d1"].t, scalar1=1.0 / 11, scalar2=1.0 / 9, op0=ALU.mult, op1=ALU.add), [s["d1"].b], [s["l1"].b])
            yield
            for cst in (1.0 / 7, 1.0 / 5, 1.0 / 3, 1.0):
                P.dve(lambda e: e.tensor_tensor(out=s["l1"].t, in0=s["l1"].t, in1=s["d1"].t, op=ALU.mult), [s["l1"].b, s["d1"].b], [s["l1"].b])
                P.dve(lambda e, cst=cst: e.tensor_scalar(out=s["l1"].t, in0=s["l1"].t, scalar1=cst, scalar2=None, op0=ALU.add), [s["l1"].b], [s["l1"].b])
            P.dve(lambda e: e.scalar_tensor_tensor(out=s["l1"].t, in0=s["l1"].t, scalar=2.0, in1=s["e1"].t, op0=ALU.mult, op1=ALU.mult), [s["l1"].b, s["e1"].b], [s["l1"].b])
            yield
            P.dve(lambda e: e.scalar_tensor_tensor(out=s["dt"].t, in0=s["x1"].t, scalar=0.0, in1=s["l1"].t, op0=ALU.max, op1=ALU.add),
                  [s["x1"].b, s["l1"].b], [s["dt"].b])
            P.dve(lambda e: e.tensor_tensor(out=s["dta"].t, in0=s["dt"].t, in1=a_bc[j].t[:], op=ALU.mult), [s["dt"].b, a_bc[j].b], [s["dta"].b])
            yield
            P.pe(lambda e: e.matmul(pm.t[:, 32:64], lhsT=UP.t[:], rhs=s["dta"].t, start=True, stop=True), [UP.b, s["dta"].b], [pm.b])
            yield
            P.pe(lambda e: e.matmul(pm.t[:, 64:96], lhsT=ones_f.t[:], rhs=s["dta"].t, start=True, stop=True), [ones_f.b, s["dta"].b], [pm.b], acc=True)
            P.act(lambda e: e.copy(out=s["la"].t, in_=pm.t[:, 32:64]), [pm.b], [s["la"].b])
            yield
            P.act(lambda e: e.activation(out=s["ela"].t, in_=s["la"].t, func=AF.Exp), [s["la"].b], [s["ela"].b])
            yield
            P.act(lambda e: e.activation(out=s["cd"].t, in_=pm.t[:, 64:96], func=AF.Exp), [pm.b], [s["cd"].b])
            yield
            P.dve(lambda e: e.tensor_tensor(out=s["d1"].t, in0=pm.t[:, 64:96], in1=s["la"].t, op=ALU.subtract), [pm.b, s["la"].b], [s["d1"].b])
            yield
            P.act(lambda e: e.activation(out=s["e1"].t, in_=s["d1"].t, func=AF.Exp), [s["d1"].b], [s["e1"].b])
            yield
            P.dve(lambda e: e.tensor_tensor(out=s["we"].t, in0=s["e1"].t, in1=s["dt"].t, op=ALU.mult), [s["e1"].b, s["dt"].b], [s["we"].b])
            yield
            yield

        def blk_gen(cb):
            pst = psPRE[cb % 4]
            o0 = 0
            for k in range(8):
                P.pe(lambda e, k=k: e.matmul(pst.t[:, o0:o0 + 132], lhsT=wxbc[:, k, cb * 128:(cb + 1) * 128], rhs=hT.t[:, k, :],
                                             start=(k == 0), stop=(k == 7)),
                     [bWlo, hT.b], [pst.b], acc=True)
            a_t = acc[cb % len(acc)]
            P.act(lambda e: e.activation(out=a_t.t, in_=pst.t[:, o0:o0 + 128], func=AF.Identity,
                                         scale=tp.t[:, 0, cb:cb + 1], bias=convb.t[:, cb:cb + 1]),
                  [pst.b, tp.b, convb.b], [a_t.b])
            yield
            for o in range(1, 5):
                P.dve(lambda e, o=o: e.scalar_tensor_tensor(out=a_t.t, in0=pst.t[:, o0 + o:o0 + o + 128], scalar=tp.t[:, o, cb:cb + 1],
                                                            in1=a_t.t, op0=ALU.mult, op1=ALU.add),
                      [pst.b, tp.b, a_t.b], [a_t.b])
                yield
            P.act(lambda e: e.activation(out=xbcT.t[:, cb, :], in_=a_t.t, func=AF.Silu), [a_t.b], [bxb[cb]])
            yield

        def tr_gen(c0, dst, db):
            for i in range(8):
                P.pe(lambda e, i=i: e.transpose(out=psT.t[:, i * 128:(i + 1) * 128], in_=xbcT.t[:, c0 + i, :], identity=ident_b.t[:]),
                     [bxb[c0 + i], ident_b.b], [psT.b], acc=True)
            yield
            P.dve(lambda e: e.tensor_copy(out=dst, in_=psT.t[:, :]), [psT.b], [db])
            yield

        tr_list = [(lambda: tr_gen(0, xs_tok.t[:, 0:1024], xs_tok.b)), (lambda: tr_gen(8, xs_tok.t[:, 1024:2048], xs_tok.b)),
                   (lambda: tr_gen(16, B_tok.t[:, :], B_tok.b))]
        run_interleaved([dt_gen] + list(extra_gens) + [(lambda cb=cb: blk_gen(cb)) for cb in range(nblk)] + (tr_list if full else tr_list[:2]),
                        max_active=5 + len(extra_gens))
        if not full:
            run_interleaved(tr_list[2:], max_active=1)
        if full and dbg and not dbg_done[0]:
            dbg_done[0] = True
            ob = Buf()
            dma(dbg_xbc[:, :, :], xbcT.t, bxb, [ob])
            used.append(ob)
            for i_, n_ in enumerate(("x1", "ab", "e", "l1", "dt", "dta", "la", "cd", "d1", "e1", "we")):
                ob = Buf()
                dma(dbg_sm[i_], s[n_].t, [s[n_].b], [ob])
                used.append(ob)
            ob = Buf()
            dma(dbg_hT[:, :, :], hT.t, [hT.b], [ob])
            used.append(ob)
        if full:
            P.pool(lambda e: e.tensor_tensor(out=xsdt.t.rearrange("p (h d) -> p h d", d=HD), in0=xs_tok.t.rearrange("p (h d) -> p h d", d=HD),
                                             in1=s["dt"].t.unsqueeze(2).broadcast_to([128, NH, HD]), op=ALU.mult),
                   [xs_tok.b, s["dt"].b], [xsdt.b])
        if full and p_sweep:
            P.dve(lambda e: e.tensor_tensor(out=xsD.t.rearrange("p (h d) -> p h d", d=HD), in0=xs_tok.t.rearrange("p (h d) -> p h d", d=HD),
                                            in1=dsk_bc.t[:].unsqueeze(2).broadcast_to([128, NH, HD]), op=ALU.mult),
                  [xs_tok.b, dsk_bc.b], [xsD.b])
        def group_gen(g):
            i2 = g % 2
            pL = psLs[i2]
            pY = psYs[i2]
            psc = pscs[i2]
            if full:
                if p_sweep and use_yS:
                    dma(yS_g[i2].t, ys_src[:, g * 256:(g + 1) * 256], [ys_buf], [yS_g[i2].b])
                P.pool(lambda e: e.tensor_tensor(out=dtaU[i2].t, in0=UP.t[:].unsqueeze(1).broadcast_to([128, 4, 128]),
                                                 in1=s["dta"].t[:, 4 * g:4 * g + 4].unsqueeze(2).broadcast_to([128, 4, 128]), op=ALU.mult),
                       [UP.b, s["dta"].b], [dtaU[i2].b])
                P.pe(lambda e: e.matmul(psc.t[:, 0:128], lhsT=xbcT.t[:, 16 + g, :], rhs=xbcT.t[:, 24 + g, :], start=True, stop=True),
                     [bxb[16 + g], bxb[24 + g]], [psc.b])
                P.pe(lambda e: e.matmul(psc.t[:, 128:384], lhsT=xbcT.t[:, 24 + g, :], rhs=state_b.t[:, g * 256:(g + 1) * 256], start=True, stop=True),
                     [bxb[24 + g], bstb[g]], [psc.b], acc=True)
                yield
                P.pe(lambda e: e.matmul(pL.t[:, :], lhsT=ones_f.t[:], rhs=dtaU[i2].t.rearrange("p h l -> p (h l)"), start=True, stop=False, skip_group_check=True),
                     [ones_f.b, dtaU[i2].b], [pL.b])
                P.pe(lambda e: e.matmul(pL.t[:, :], lhsT=ident_b.t[:], rhs=negP4b.t[:], start=False, stop=True, skip_group_check=True),
                     [ident_b.b, negP4b.b], [pL.b], acc=True)
                yield
                P.dve(lambda e: e.tensor_tensor(out=diff[i2].t, in0=pL.t[:, :].rearrange("p (h l) -> p h l", l=128),
                                                in1=s["la"].t[:, 4 * g:4 * g + 4].unsqueeze(2).broadcast_to([128, 4, 128]), op=ALU.subtract),
                      [pL.b, s["la"].b], [diff[i2].b])
                yield
                P.act(lambda e: e.activation(out=diff[i2].t, in_=diff[i2].t, func=AF.Exp), [diff[i2].b], [diff[i2].b])
                P.dve(lambda e: e.tensor_tensor(out=Zs[i2].t.rearrange("p (h d) -> p h d", d=HD), in0=psc.t[:, 128:384].rearrange("p (h d) -> p h d", d=HD),
                                                in1=s["ela"].t[:, 4 * g:4 * g + 4].unsqueeze(2).broadcast_to([128, 4, HD]), op=ALU.mult),
                      [psc.b, s["ela"].b], [Zs[i2].b])
                yield
                P.dve(lambda e: e.tensor_tensor(out=mixT[i2].t, in0=diff[i2].t, in1=psc.t[:, 0:128].unsqueeze(1).broadcast_to([128, 4, 128]), op=ALU.mult),
                      [diff[i2].b, psc.b], [mixT[i2].b])
                yield
                P.pe(lambda e: e.matmul(pY.t[:, 0:256], lhsT=ident_b.t[:], rhs=Zs[i2].t, start=True, stop=False, skip_group_check=True),
                     [ident_b.b, Zs[i2].b], [pY.b])
                if p_sweep:
                    P.pe(lambda e: e.matmul(pY.t[:, 0:256], lhsT=ident_b.t[:], rhs=xsD.t[:, g * 256:(g + 1) * 256], start=False, stop=False, skip_group_check=True),
                         [ident_b.b, xsD.b], [pY.b], acc=True)
                    if use_yS:
                        P.pe(lambda e: e.matmul(pY.t[:, 0:256], lhsT=J_b.t[:], rhs=yS_g[i2].t, start=False, stop=False, skip_group_check=True),
                             [J_b.b, yS_g[i2].b], [pY.b], acc=True)
                for h in range(4):
                    hh = 4 * g + h
                    P.pe(lambda e, h=h, hh=hh: e.matmul(pY.t[:, h * 64:(h + 1) * 64], lhsT=mixT[i2].t[:, h, :], rhs=xsdt.t[:, hh * 64:(hh + 1) * 64],
                                                       start=False, stop=True, skip_group_check=True),
                         [mixT[i2].b, xsdt.b], [pY.b], acc=True)
                yield
                yo = y_g[i2] if p_sweep else y_gb[i2]
                P.act(lambda e: e.copy(out=yo.t, in_=pY.t[:, 0:256]), [pY.b], [yo.b])
                dma(y_dst[:, g * 256:(g + 1) * 256], yo.t, [yo.b], [yd_buf])
                yield
            P.pool(lambda e: e.tensor_tensor(out=xsw[i2].t.rearrange("p (h d) -> p h d", d=HD), in0=xs_tok.t[:, g * 256:(g + 1) * 256].rearrange("p (h d) -> p h d", d=HD),
                                             in1=s["we"].t[:, 4 * g:4 * g + 4].unsqueeze(2).broadcast_to([128, 4, HD]), op=ALU.mult),
                   [xs_tok.b, s["we"].b], [xsw[i2].b])
            yield
            pct = PS[1]
            P.pe(lambda e: e.matmul(pct.t[:, i2 * 256:(i2 + 1) * 256], lhsT=B_tok.t[:, g * 128:(g + 1) * 128], rhs=xsw[i2].t, start=True, stop=True),
                 [B_tok.b, xsw[i2].b], [pct.b])
            yield
            sv = state.t[:, g * 256:(g + 1) * 256]
            P.dve(lambda e: e.tensor_tensor(out=sv.rearrange("p (h d) -> p h d", d=HD), in0=sv.rearrange("p (h d) -> p h d", d=HD),
                                            in1=s["cd"].t[:, 4 * g:4 * g + 4].unsqueeze(2).broadcast_to([128, 4, HD]), op=ALU.mult),
                  [bst[g], s["cd"].b], [bst[g]])
            yield
            P.dve(lambda e: e.tensor_tensor(out=sv, in0=pct.t[:, i2 * 256:(i2 + 1) * 256], in1=sv, op=ALU.add), [pct.b, bst[g]], [bst[g]])
            yield
            P.act(lambda e: e.copy(out=state_b.t[:, g * 256:(g + 1) * 256], in_=sv), [bst[g]], [bstb[g]])
            yield

        skew = 1
        active = []
        nxt = 0
        steps_of_last = 0
        while active or nxt < NG:
            if nxt < NG and len(active) < 2 and (not active or steps_of_last >= skew):
                active.append(group_gen(nxt))
                nxt += 1
                steps_of_last = 0
            for gen in list(active):
                try:
                    next(gen)
                except StopIteration:
                    active.remove(gen)
            steps_of_last += 1

    def reset_state():
        P.pool(lambda e: e.memset(state.t, 0.0), bst, bst)
        P.pool(lambda e: e.memset(state_b.t, 0.0), bstb, bstb)

    bYS = [Buf() for _ in range(32)]
    bYT = [Buf() for _ in range(32)]
    bBS = [Buf() for _ in range(32)]

    def run_sweep(chunks, j, p_sweep):
        make_hT(hTe[0], chunks[0][0], chunks[0][1], chunks[0][2], chunks[0][3], chunks[0][4], 0)
        for i, (src, t0, lo, hi, mj, full, oi) in enumerate(chunks):
            eg = []
            if i + 1 < len(chunks):
                n = chunks[i + 1]
                eg = [lambda n=n, i=i: make_hT_gen(hTe[(i + 1) % 2], n[0], n[1], n[2], n[3], n[4], (i + 1) % 2)]
            if full and p_sweep:
                cS = 31 - oi
                ssd_chunk(hTe[i % 2], j, True, True, ys_src=yS_d[cS], y_dst=yt_d[oi], ys_buf=bYS[cS], yd_buf=bYT[oi], extra_gens=eg)
                used.append(bYT[oi])
            elif full:
                ssd_chunk(hTe[i % 2], j, True, False, y_dst=yS_d[oi], yd_buf=bYS[oi], extra_gens=eg)
                used.append(bYS[oi])
            else:
                ssd_chunk(hTe[i % 2], j, False, extra_gens=eg)

    if "S" in sweeps:
        reset_state()
        ch = []
        for c in range(n_ctx):
            ch.append((cr, c * 128, c > 0, c < CTX // 128 - 1, 1, False, None))
        for c in range(32 - n_oth, 32):
            ch.append((xr, c * 128, c > 0, True, 0, False, None))
        for c in range(n_own):
            ch.append((xr, OWN + c * 128, True, (OWN + c * 128 + 128) < SEQ, 0, True, c))
        run_sweep(ch, 1, False)
    if "P" in sweeps:
        reset_state()
        ch = []
        for c in range(n_ctx):
            ch.append((cl, c * 128, c > 0, c < CTX // 128 - 1, 1, False, None))
        for c in range(n_own):
            ch.append((xl, c * 128, c > 0, True, 0, True, c))
        run_sweep(ch, 0, True)

    if "1" in sweeps:
        P.barrier(tiny, [PS[2].b])
        ar.off = AR_BASE
        wz = wload(32768, w_in_v[:, :, DT_END:Z_END], 8, DIN, bWhi)
        woss = wload(49152, w_oss.rearrange("(k p) n -> p k n", p=128), 16, D, bWhi, step=8)
        wglu = wload(0, w_in_v[:, :, Z_END:GLU_END], 8, 2 * D, bWlo)
        wgt = wload(16384, w_in_v[:, :, CG_END:IN_COLS], 8, 2 * D, bWlo)
        zs = ar.f32(128, DIN)
        y_in = ar.f32(128, DIN)
        ssg = ar.f32(128, 8)
        yn = ar.bf16(128, DIN)
        ynT = ar.bf16(128, 16 * 128)
        ynT.t = ynT.t.rearrange("p (k n) -> p k n", n=128)
        bs_sb = ar.f32(128, D)
        normw_bc = ar.f32(128, DIN)
        dma(normw_bc.t, ssm_nw.partition_broadcast(128), [], [normw_bc.b])

        def t1_gen(hT, c):
            for qd in range(4):
                pz = PS[3 + qd]
                for k in range(8):
                    P.pe(lambda e, k=k, qd=qd, pz=pz: e.matmul(pz.t[:, :], lhsT=hT.t[:, k, 2:130], rhs=wz[:, k, qd * 512:(qd + 1) * 512], start=(k == 0), stop=(k == 7)),
                         [hT.b, bWhi], [pz.b], acc=True)
                P.act(lambda e, qd=qd, pz=pz: e.activation(out=zs.t[:, qd * 512:(qd + 1) * 512], in_=pz.t[:, :], func=AF.Silu), [pz.b], [zs.b])
                yield
            dma(y_in.t, yt_d[c], [bYT[c]], [y_in.b])
            P.dve(lambda e: e.tensor_tensor(out=y_in.t, in0=y_in.t, in1=zs.t, op=ALU.mult), [y_in.b, zs.b], [y_in.b])
            yield
            P.pool(lambda e: e.memset(ssg.t, 0.0), [], [ssg.b])
            for g in range(NG):
                P.act(lambda e, g=g: e.activation(out=zs.t[:, g * 256:(g + 1) * 256], in_=y_in.t[:, g * 256:(g + 1) * 256], func=AF.Square, accum_out=ssg.t[:, g:g + 1]),
                      [y_in.b], [zs.b, ssg.b])
                yield
            P.dve(lambda e: e.tensor_scalar(out=ssg.t, in0=ssg.t, scalar1=1.0 / 256, scalar2=EPS, op0=ALU.mult, op1=ALU.add), [ssg.b], [ssg.b])
            P.act(lambda e: e.activation(out=ssg.t, in_=ssg.t, func=AF.Ln), [ssg.b], [ssg.b])
            P.act(lambda e: e.activation(out=ssg.t, in_=ssg.t, func=AF.Exp, scale=-0.5), [ssg.b], [ssg.b])
            yield
            for g in range(NG):
                yield
                P.dve(lambda e, g=g: e.scalar_tensor_tensor(out=yn.t[:, g * 256:(g + 1) * 256], in0=y_in.t[:, g * 256:(g + 1) * 256], scalar=ssg.t[:, g:g + 1],
                                                            in1=normw_bc.t[:, g * 256:(g + 1) * 256], op0=ALU.mult, op1=ALU.mult),
                      [y_in.b, ssg.b, normw_bc.b], [yn.b])
            for ps_ in range(2):
                for i in range(8):
                    kb = ps_ * 8 + i
                    P.pe(lambda e, kb=kb, i=i: e.transpose(out=psT.t[:, i * 128:(i + 1) * 128], in_=yn.t[:, kb * 128:(kb + 1) * 128], identity=ident_b.t[:]),
                         [yn.b, ident_b.b], [psT.b], acc=True)
                yield
                P.dve(lambda e, ps_=ps_: e.tensor_copy(out=ynT.t[:, ps_ * 8:(ps_ + 1) * 8, :].rearrange("p k n -> p (k n)"), in_=psT.t[:, :]), [psT.b], [ynT.b])
                yield
            for hf in range(2):
                po = PS[2] if hf == 0 else PS[3]
                for kb in range(16):
                    P.pe(lambda e, kb=kb, hf=hf, po=po: e.matmul(po.t[:, :], lhsT=ynT.t[:, kb, :], rhs=woss[:, kb, hf * 512:(hf + 1) * 512], start=(kb == 0), stop=(kb == 15)),
                         [ynT.b, bWhi], [po.b], acc=True)
                yield
                P.act(lambda e, hf=hf, po=po: e.copy(out=bs_sb.t[:, hf * 512:(hf + 1) * 512], in_=po.t[:, :]), [po.b], [bs_sb.b])
                yield
            dma(bs_d[c], bs_sb.t, [bs_sb.b], [bBS[c]])
            yield

        make_hT(hTe[0], xl, 0, False, False, 0, 0, halo=False)
        for c in range(n_own):
            gl = [lambda c=c: t1_gen(hTe[c % 2], c)]
            if c + 1 < n_own:
                gl.append(lambda c=c: make_hT_gen(hTe[(c + 1) % 2], xl, (c + 1) * 128, False, False, 0, 0, halo=False))
            run_interleaved(gl, max_active=2)
            if "2" not in sweeps:
                used.append(bBS[c])

    if "2" in sweeps:
        P.barrier(tiny, [PS[2].b])
        ar.off = AR_BASE
        wcg = wload(32768, w_in_v[:, :, GLU_END:CG_END], 8, D, bWhi)
        woc = wload(40960, w_oc.rearrange("(k p) n -> p k n", p=128), 8, D, bWhi)
        wo = wload(49152, w_o.rearrange("(k p) n -> p k n", p=128), 8, D, bWhi)
        sg = [ar.f32(128, 128) for _ in range(4)]
        bup = [Buf() for _ in range(8)]
        bca = [Buf() for _ in range(8)]
        bcg = [Buf() for _ in range(8)]
        u_pad = ar.f32(128, 8 * 2 * 94)
        u_pad.t = u_pad.t.rearrange("p (c r n) -> p c r n", r=2, n=94)
        cacc = ar.f32(128, 8 * 128)
        cacc.t = cacc.t.rearrange("p (c n) -> p c n", n=128)
        ut = ar.f32(128, D)
        st = ar.f32(128, 8)
        suT = ar.f32(128, 8 * 128)
        cgT = ar.f32(128, 8 * 128)
        vT = ar.bf16(128, 8 * 128)
        vT.t = vT.t.rearrange("p (c n) -> p c n", n=128)
        gs = ar.f32(128, D)
        bs_in = ar.f32(128, D)
        mrg = ar.bf16(128, D)
        mT = ar.bf16(128, 8 * 128)
        mT.t = mT.t.rearrange("p (c n) -> p c n", n=128)
        gate_bc = ar.f32(128, D)
        fnw_bc = ar.f32(128, D)
        cw = ar.f32(128, 31 * 8)
        cw.t = cw.t.rearrange("p (k c) -> p k c", c=8)
        cwb = ar.f32(128, 8)
        lnw_fm = ar.f32(128, 8)
        lnb_fm = ar.f32(128, 8)
        dma(gate_bc.t, gate_d[0].partition_broadcast(128), [bgate], [gate_bc.b])
        dma(fnw_bc.t, fnw.partition_broadcast(128), [], [fnw_bc.b])
        for k in range(31):
            dma(cw.t[:, k, :], cconv[k].rearrange("(cb p) -> p cb", p=128), [], [cw.b], slow=True)
        dma(cwb.t, cconv_b.rearrange("(cb p) -> p cb", p=128), [], [cwb.b], slow=True)
        dma(lnw_fm.t, ln_w.rearrange("(cb p) -> p cb", p=128), [], [lnw_fm.b], slow=True)
        dma(lnb_fm.t, ln_b.rearrange("(cb p) -> p cb", p=128), [], [lnb_fm.b], slow=True)
        P.pool(lambda e: e.memset(u_pad.t, 0.0), [], bup)
        NPE = 8
        u_pb = ar.bf16(128, 8 * 2 * 94)
        u_pb.t = u_pb.t.rearrange("p (c r n) -> p c r n", r=2, n=94)
        bupb = [Buf() for _ in range(8)]
        P.pool(lambda e: e.memset(u_pb.t, 0.0), [], bupb)
        dg = WA.t[:, 57344:57344 + NPE * 8 * 128].rearrange("p (k c n) -> p k c n", c=8, n=128)
        bdg = Buf()
        for k in range(NPE):
            for cb in range(8):
                if (k * 8 + cb) % 2 == 0:
                    P.act(lambda e, k=k, cb=cb: e.activation(out=dg[:, k, cb, :], in_=ident_b.t[:], func=AF.Identity, scale=cw.t[:, k, cb:cb + 1]),
                          [ident_b.b, cw.b], [bdg])
                else:
                    P.pool(lambda e, k=k, cb=cb: e.tensor_scalar(out=dg[:, k, cb, :], in0=ident_b.t[:], scalar1=cw.t[:, k, cb:cb + 1], scalar2=None, op0=ALU.mult),
                           [ident_b.b, cw.b], [bdg])

        def t2_chunk(hT, c, eg=()):
            def cblk_gen(cb):
                pg = PS[3 + cb % 4]
                for (o0, wv, c0) in ((0, wglu, cb * 128), (128, wglu, D + cb * 128), (256, wcg, cb * 128)):
                    wb = bWlo if wv is wglu else bWhi
                    for k in range(8):
                        P.pe(lambda e, k=k, o0=o0, wv=wv, c0=c0: e.matmul(pg.t[:, o0:o0 + 128], lhsT=wv[:, k, c0:c0 + 128], rhs=hT.t[:, k, 2:130], start=(k == 0), stop=(k == 7)),
                             [hT.b, wb], [pg.b], acc=True)
                sgt = sg[cb % 4]
                P.act(lambda e: e.activation(out=sgt.t, in_=pg.t[:, 128:256], func=AF.Sigmoid), [pg.b], [sgt.b])
                P.act(lambda e: e.activation(out=cgT.t[:, cb * 128:(cb + 1) * 128], in_=pg.t[:, 256:384], func=AF.Silu), [pg.b], [bcg[cb]])
                yield
                P.dve(lambda e: e.tensor_tensor(out=u_pad.t[:, cb, :, 15:79], in0=pg.t[:, 0:128].rearrange("p (r n) -> p r n", n=64),
                                                in1=sgt.t.rearrange("p (r n) -> p r n", n=64), op=ALU.mult),
                      [pg.b, sgt.b], [bup[cb]])
                yield
                P.act(lambda e: e.copy(out=u_pb.t[:, cb, :, 15:79], in_=u_pad.t[:, cb, :, 15:79]), [bup[cb]], [bupb[cb]])
                yield
                cv = cacc.t[:, cb, :].rearrange("p (r n) -> p r n", n=64)
                pcv = pg.t[:, 384:512].rearrange("p (r n) -> p r n", n=64)
                for k in range(NPE):
                    P.pe(lambda e, k=k: e.matmul(pcv, lhsT=dg[:, k, cb, :], rhs=u_pb.t[:, cb, :, k:k + 64], start=(k == 0), stop=(k == NPE - 1)),
                         [bdg, bupb[cb]], [pg.b], acc=True)
                yield
                P.act(lambda e: e.activation(out=cv, in_=pcv, func=AF.Identity, bias=cwb.t[:, cb:cb + 1]),
                      [pg.b, cwb.b], [bca[cb]])
                yield
                for k in range(NPE, 31):
                    P.dve(lambda e, k=k: e.scalar_tensor_tensor(out=cv, in0=u_pad.t[:, cb, :, k:k + 64], scalar=cw.t[:, k, cb:cb + 1], in1=cv, op0=ALU.mult, op1=ALU.add),
                          [bup[cb], cw.b, bca[cb]], [bca[cb]])
                    yield

            run_interleaved(list(eg) + [(lambda cb=cb: cblk_gen(cb)) for cb in range(8)], max_active=4 + len(eg))
            P.pool(lambda e: e.memset(st.t, 0.0), [], [st.b])
            for hf in range(2):
                pu = PS[hf]
                for i in range(4):
                    cb = hf * 4 + i
                    P.pe(lambda e, cb=cb, i=i, pu=pu: e.transpose(out=pu.t[:, i * 128:(i + 1) * 128], in_=cacc.t[:, cb, :], identity=ident_f.t[:]),
                         [bca[cb], ident_f.b], [pu.b], acc=True)
                P.act(lambda e, hf=hf, pu=pu: e.activation(out=ut.t[:, hf * 512:(hf + 1) * 512], in_=pu.t[:, :], func=AF.Identity, accum_out=st.t[:, hf:hf + 1]),
                      [pu.b], [ut.b, st.b])
            P.act(lambda e: e.activation(out=gs.t, in_=ut.t, func=AF.Square, accum_out=st.t[:, 2:3]), [ut.b], [gs.b, st.b])
            P.dve(lambda e: e.tensor_tensor(out=st.t[:, 3:4], in0=st.t[:, 0:1], in1=st.t[:, 1:2], op=ALU.add), [st.b], [st.b])
            P.dve(lambda e: e.tensor_scalar(out=st.t[:, 3:4], in0=st.t[:, 3:4], scalar1=1.0 / D, scalar2=None, op0=ALU.mult), [st.b], [st.b])
            P.dve(lambda e: e.tensor_tensor(out=st.t[:, 4:5], in0=st.t[:, 3:4], in1=st.t[:, 3:4], op=ALU.mult), [st.b], [st.b])
            P.dve(lambda e: e.scalar_tensor_tensor(out=st.t[:, 5:6], in0=st.t[:, 2:3], scalar=1.0 / D, in1=st.t[:, 4:5], op0=ALU.mult, op1=ALU.subtract), [st.b], [st.b])
            P.dve(lambda e: e.tensor_scalar(out=st.t[:, 5:6], in0=st.t[:, 5:6], scalar1=EPS, scalar2=None, op0=ALU.add), [st.b], [st.b])
            P.act(lambda e: e.activation(out=st.t[:, 5:6], in_=st.t[:, 5:6], func=AF.Ln), [st.b], [st.b])
            P.act(lambda e: e.activation(out=st.t[:, 5:6], in_=st.t[:, 5:6], func=AF.Exp, scale=-0.5), [st.b], [st.b])
            P.dve(lambda e: e.scalar_tensor_tensor(out=st.t[:, 6:7], in0=st.t[:, 3:4], scalar=-1.0, in1=st.t[:, 5:6], op0=ALU.mult, op1=ALU.mult), [st.b], [st.b])
            P.dve(lambda e: e.tensor_scalar(out=ut.t, in0=ut.t, scalar1=st.t[:, 5:6], scalar2=st.t[:, 6:7], op0=ALU.mult, op1=ALU.add), [ut.b, st.b], [ut.b])
            for hf in range(2):
                pb = PS[5 + hf]
                for i in range(4):
                    cb = hf * 4 + i
                    P.pe(lambda e, cb=cb, i=i, pb=pb: e.transpose(out=pb.t[:, i * 128:(i + 1) * 128], in_=ut.t[:, cb * 128:(cb + 1) * 128], identity=ident_f.t[:]),
                         [ut.b, ident_f.b], [pb.b], acc=True)
                for i in range(4):
                    cb = hf * 4 + i
                    P.act(lambda e, cb=cb, i=i, pb=pb: e.activation(out=suT.t[:, cb * 128:(cb + 1) * 128], in_=pb.t[:, i * 128:(i + 1) * 128], func=AF.Silu,
                                                                    scale=lnw_fm.t[:, cb:cb + 1], bias=lnb_fm.t[:, cb:cb + 1]),
                          [pb.b, lnw_fm.b, lnb_fm.b], [suT.b])
            P.dve(lambda e: e.tensor_tensor(out=vT.t.rearrange("p c n -> p (c n)"), in0=suT.t, in1=cgT.t, op=ALU.mult), [suT.b] + bcg, [vT.b])
            for hf in range(2):
                pc = PS[hf]
                for kb in range(8):
                    P.pe(lambda e, kb=kb, hf=hf, pc=pc: e.matmul(pc.t[:, :], lhsT=vT.t[:, kb, :], rhs=woc[:, kb, hf * 512:(hf + 1) * 512], start=(kb == 0), stop=(kb == 7)),
                         [vT.b, bWhi], [pc.b], acc=True)
            dma(bs_in.t, bs_d[c], [bBS[c]], [bs_in.b])
            for gi in range(2):
                for hf in range(2):
                    pq = PS[5 + hf]
                    for k in range(8):
                        P.pe(lambda e, k=k, gi=gi, hf=hf, pq=pq: e.matmul(pq.t[:, :], lhsT=hT.t[:, k, 2:130], rhs=wgt[:, k, gi * D + hf * 512:gi * D + (hf + 1) * 512],
                                                                         start=(k == 0), stop=(k == 7)),
                             [hT.b, bWlo], [pq.b], acc=True)
                    P.act(lambda e, hf=hf, pq=pq: e.activation(out=gs.t[:, hf * 512:(hf + 1) * 512], in_=pq.t[:, :], func=AF.Sigmoid), [pq.b], [gs.b])
                if gi == 0:
                    P.dve(lambda e: e.tensor_tensor(out=bs_in.t, in0=bs_in.t, in1=gs.t, op=ALU.mult), [bs_in.b, gs.b], [bs_in.b])
                else:
                    for hf in range(2):
                        P.dve(lambda e, hf=hf: e.tensor_tensor(out=gs.t[:, hf * 512:(hf + 1) * 512], in0=PS[hf].t[:, :], in1=gs.t[:, hf * 512:(hf + 1) * 512], op=ALU.mult),
                              [PS[hf].b, gs.b], [gs.b])
                    P.dve(lambda e: e.tensor_tensor(out=mrg.t, in0=bs_in.t, in1=gs.t, op=ALU.add), [bs_in.b, gs.b], [mrg.b])
            for i in range(8):
                P.pe(lambda e, i=i: e.transpose(out=psT.t[:, i * 128:(i + 1) * 128], in_=mrg.t[:, i * 128:(i + 1) * 128], identity=ident_b.t[:]),
                     [mrg.b, ident_b.b], [psT.b], acc=True)
            P.dve(lambda e: e.tensor_copy(out=mT.t.rearrange("p c n -> p (c n)"), in_=psT.t[:, :]), [psT.b], [mT.b])
            dma(suT.t, xl[c * 128:(c + 1) * 128, :], [], [suT.b])
            for hf in range(2):
                po = PS[3 + hf]
                for kb in range(8):
                    P.pe(lambda e, kb=kb, hf=hf, po=po: e.matmul(po.t[:, :], lhsT=mT.t[:, kb, :], rhs=wo[:, kb, hf * 512:(hf + 1) * 512], start=(kb == 0), stop=(kb == 7)),
                         [mT.b, bWhi], [po.b], acc=True)
                P.dve(lambda e, hf=hf, po=po: e.tensor_tensor(out=ut.t[:, hf * 512:(hf + 1) * 512], in0=po.t[:, :], in1=gate_bc.t[:, hf * 512:(hf + 1) * 512], op=ALU.mult),
                      [po.b, gate_bc.b], [ut.b])
            P.dve(lambda e: e.tensor_tensor(out=ut.t, in0=ut.t, in1=suT.t, op=ALU.add), [ut.b, suT.b], [ut.b])
            P.pool(lambda e: e.memset(st.t[:, 7:8], 0.0), [], [st.b])
            P.act(lambda e: e.activation(out=gs.t, in_=ut.t, func=AF.Square, accum_out=st.t[:, 7:8]), [ut.b], [gs.b, st.b])
            P.dve(lambda e: e.tensor_scalar(out=st.t[:, 7:8], in0=st.t[:, 7:8], scalar1=1.0 / D, scalar2=EPS, op0=ALU.mult, op1=ALU.add), [st.b], [st.b])
            P.act(lambda e: e.activation(out=st.t[:, 7:8], in_=st.t[:, 7:8], func=AF.Ln), [st.b], [st.b])
            P.act(lambda e: e.activation(out=st.t[:, 7:8], in_=st.t[:, 7:8], func=AF.Exp, scale=-0.5), [st.b], [st.b])
            P.dve(lambda e: e.scalar_tensor_tensor(out=ut.t, in0=ut.t, scalar=st.t[:, 7:8], in1=fnw_bc.t, op0=ALU.mult, op1=ALU.mult), [ut.b, st.b, fnw_bc.b], [ut.b])
            ob = Buf()
            dma(out[c * 128:(c + 1) * 128, :], ut.t, [ut.b], [ob])
            used.append(ob)

        make_hT(hTe[0], xl, 0, False, False, 0, 0, halo=False)
        for c in range(n_own):
            eg = []
            if c + 1 < n_own:
                eg = [lambda c=c: make_hT_gen(hTe[(c + 1) % 2], xl, (c + 1) * 128, False, False, 0, 0, halo=False)]
            t2_chunk(hTe[c % 2], c, eg)

    nc = P.finalize(used)
    return P, nc


def _consts():
    c = np.zeros((7, 128, 128), np.float32)
    i = np.arange(128)
    c[0] = np.eye(128)
    c[1] = (i[:, None] <= i[None, :]).astype(np.float32)
    c[2] = np.where(i[None, :] >= i[:, None], 0.0, -30000.0)
    c[3] = 1.0
    c[4] = np.eye(128)[::-1]
    c[5, 0, :] = 1.0
    c[6, 1, :] = 1.0
    return c


def make_in_maps(inp):
    f = lambda a: np.ascontiguousarray(np.asarray(a, dtype=np.float32))
    x = np.asarray(inp["x"], np.float32)
    ctx = np.asarray(inp["ctx"], np.float32)
    c = np.asarray(inp["c"], np.float32)
    w_in = f(inp["w_in"][0])
    cw = np.asarray(inp["ssm_conv_w"][0], np.float32)
    z = np.zeros((1, C_END), np.float32)
    taps_nat = np.concatenate([cw, z], 0)
    taps_rev = np.concatenate([z, cw[::-1]], 0)
    consts = _consts()
    maps = []
    for core in range(8):
        b, half = core // 2, core % 2
        xb = x[b]
        cb = ctx[b]
        if half == 0:
            xl, xr, cl, cr = xb, xb[::-1], cb, cb[::-1]
            tP, tS = taps_nat, taps_rev
            dP, dS = 0, 1
            cc = np.asarray(inp["conf_conv_w"][0], np.float32)
        else:
            xl, xr, cl, cr = xb[::-1], xb, cb[::-1], cb
            tP, tS = taps_rev, taps_nat
            dP, dS = 1, 0
            cc = np.asarray(inp["conf_conv_w"][0], np.float32)[::-1]
        wdt = np.stack([w_in[:, C_END + 32 * dP:C_END + 32 * dP + 32], w_in[:, C_END + 32 * dS:C_END + 32 * dS + 32]], 0)
        m = {
            "xl": f(xl), "xr": f(xr), "cl": f(cl), "cr": f(cr),
            "cvec": f(np.stack([c[b], np.asarray(inp["c_ctx"], np.float32)], 0)),
            "w_mod": f(inp["w_mod"][0]),
            "b_mod2": f(np.stack([inp["b_mod"][0]] * 2, 0)),
            "norm_w2": f(np.stack([inp["norm_w"][0]] * 2, 0)),
            "w_in": w_in,
            "w_dt": f(wdt),
            "taps": f(np.stack([tP, tS], 0)),
            "conv_b": f(inp["ssm_conv_b"][0]),
            "dtb": f(np.stack([inp["dt_bias"][0][dP], inp["dt_bias"][0][dS]], 0)),
            "alog": f(np.stack([inp["a_log"][0][dP], inp["a_log"][0][dS]], 0)),
            "dskip": f(inp["d_skip"][0]),
            "ssm_nw": f(inp["ssm_norm_w"][0]),
            "w_oss": f(inp["w_out_ssm"][0]),
            "cconv": f(cc),
            "cconv_b": f(inp["conf_conv_b"][0]),
            "ln_w": f(inp["conf_ln_w"][0]),
            "ln_b": f(inp["conf_ln_b"][0]),
            "w_oc": f(inp["w_out_conf"][0]),
            "w_o": f(inp["w_out"][0]),
            "fnw": f(inp["final_norm_w"]),
            "consts": consts,
        }
        maps.append(m)
    return maps


def kernel(**inp):
    P, nc = build_program()
    maps = make_in_maps(inp)
    res = run_bass_kernel_spmd(nc, maps, core_ids=list(range(8)))
    outp = np.empty((4, SEQ, D), np.float32)
    for core in range(8):
        b, half = core // 2, core % 2
        o = res.results[core]["out"]
        if half == 0:
            outp[b, :OWN] = o
        else:
            outp[b, OWN:] = o[::-1]
    return outp
```
